# Optimizing a Trainium2 kernel written in Bass

```python
import math
import jax, jax.numpy as jnp
from jax import lax
import numpy as np

D_MODEL = 1024
BATCH = 32
SEQ = 256
DEPTH = 2
DEC_BATCH = 4
DEC_SEQ = 1024
PAST_LEN = 256

GRID_W = 64
W_MIX = D_MODEL
W_RET = 3 * D_MODEL // 8
H_RET = 6
DK_RET = W_RET // H_RET
DV_RET = W_RET // H_RET
W_SSD = 3 * D_MODEL // 8
P_SSD = 64
H_SSD = W_SSD // P_SSD
N_SSD = 128
G_SSD = 2
SSD_CONV = 4
CONV_CH = W_SSD + 2 * G_SSD * N_SSD
W_LRU = W_MIX - W_RET - W_SSD
LRU_BLOCKS = 4
LRU_BW = W_LRU // LRU_BLOCKS
LRU_CONV = 4
LRU_C = 8.0
D_FF = ((8 * D_MODEL // 3 + 127) // 128) * 128
FFN_CONV = 3
CHUNK = 64
ROPE_BASE = 10000.0
EPS = 1e-6
IN_DIM = 4 * W_RET + W_SSD + CONV_CH + H_SSD + 2 * W_LRU

kernel_name = "hybrid_ret_ssd_rglru_diffusion_step"

F32 = jnp.float32


def rmsnorm(x, g):
    xf = x.astype(F32)
    y = xf * lax.rsqrt(jnp.mean(xf * xf, axis=-1, keepdims=True) + EPS)
    return (y * g.astype(F32)).astype(x.dtype)


def dwconv(x, w, b):
    K, C = w.shape
    left = (K - 1) // 2
    right = K - 1 - left
    y = lax.conv_general_dilated(x, w.astype(x.dtype)[:, None, :], window_strides=(1,),
                                 padding=[(left, right)], dimension_numbers=('NWC', 'WIO', 'NWC'),
                                 feature_group_count=C)
    return y + b.astype(x.dtype)


def rope_1d(x, pos):
    half = x.shape[-1] // 2
    freqs = ROPE_BASE ** (-jnp.arange(half, dtype=F32) / half)
    ang = pos.astype(F32)[:, None] * freqs
    cos, sin = jnp.cos(ang), jnp.sin(ang)
    x1, x2 = x[..., :half].astype(F32), x[..., half:].astype(F32)
    return jnp.concatenate([x1 * cos - x2 * sin, x1 * sin + x2 * cos], axis=-1).astype(x.dtype)


def rope_2d(x, rows, cols):
    h = x.shape[-1] // 2
    return jnp.concatenate([rope_1d(x[..., :h], rows), rope_1d(x[..., h:], cols)], axis=-1)


def chunked_scan(q, k, v, log_a, s0):
    Bsz, H, T, K = q.shape
    V = v.shape[-1]
    n = T // CHUNK
    q = q.reshape(Bsz, H, n, CHUNK, K)
    k = k.reshape(Bsz, H, n, CHUNK, K)
    v = v.reshape(Bsz, H, n, CHUNK, V)
    cum = jnp.cumsum(log_a.astype(F32).reshape(Bsz, H, n, CHUNK), axis=-1)
    idx = jnp.arange(CHUNK)
    lower = idx[:, None] >= idx[None, :]
    decay = jnp.exp(jnp.where(lower, cum[..., :, None] - cum[..., None, :], -jnp.inf))
    scores = jnp.einsum('bhnik,bhnmk->bhnim', q, k) * decay
    o_intra = jnp.einsum('bhnim,bhnmv->bhniv', scores, v)
    to_end = jnp.exp(cum[..., -1:] - cum)
    chunk_states = jnp.einsum('bhnmk,bhnm,bhnmv->bhnkv', k, to_end, v)
    chunk_decay = jnp.exp(cum[..., -1])

    def step(s, inp):
        cs, cd = inp
        return cd[..., None, None] * s + cs, s

    s_final, s_prev = lax.scan(step, s0.astype(F32),
                               (jnp.moveaxis(chunk_states, 2, 0), jnp.moveaxis(chunk_decay, 2, 0)))
    s_prev = jnp.moveaxis(s_prev, 0, 2)
    o_inter = jnp.einsum('bhnik,bhni,bhnkv->bhniv', q, jnp.exp(cum), s_prev)
    return (o_intra + o_inter).reshape(Bsz, H, T, V), s_final


def bidir_scan(q, k, v_f, v_b, la_f, la_b, s0_f, s0_b):
    o_f, s_f = chunked_scan(q, k, v_f, la_f, s0_f)
    fl = lambda t: jnp.flip(t, axis=2)
    o_b, s_b = chunked_scan(fl(q), fl(k), fl(v_b), fl(la_b), s0_b)
    return o_f + fl(o_b), s_f, s_b


def rglru_scan(log_a, u, h0):
    a = jnp.exp(log_a)
    b = jnp.sqrt(-jnp.expm1(2.0 * log_a)) * u.astype(F32)

    def combine(l, r):
        al, bl = l
        ar, br = r
        return al * ar, ar * bl + br

    A, hs = lax.associative_scan(combine, (a, b), axis=1)
    hs = hs + A * h0.astype(F32)[:, None, :]
    return hs, hs[:, -1]


def mixer(h, states, p, pos):
    Bsz, T, _ = h.shape
    s_ret, s_ssd, s_lru = states
    proj = h @ p['w_in']
    sizes = (W_RET, W_RET, W_RET, W_RET, W_SSD, CONV_CH, H_SSD, W_LRU, W_LRU)
    offs = tuple(int(o) for o in np.cumsum(sizes)[:-1])
    q, k, v, g, z, xbc, dt_raw, xl, gl = jnp.split(proj, offs, axis=-1)
    heads = lambda t, nh: t.reshape(Bsz, T, nh, -1).transpose(0, 2, 1, 3)

    q, k, v = heads(q, H_RET), heads(k, H_RET), heads(v, H_RET)
    if pos is not None:
        q = rope_2d(q, pos[0], pos[1])
        k = rope_2d(k, pos[0], pos[1])
    q = q * (DK_RET ** -0.5)
    log_g = jax.nn.log_sigmoid(p['ret_decay'].astype(F32))
    la_f = jnp.broadcast_to(log_g[0][None, :, None], (Bsz, H_RET, T))
    la_b = jnp.broadcast_to(log_g[1][None, :, None], (Bsz, H_RET, T))
    o, rf, rb = bidir_scan(q, k, v, v, la_f, la_b, s_ret[:, 0], s_ret[:, 1])
    o = o.transpose(0, 2, 1, 3)
    mu = jnp.mean(o, axis=-1, keepdims=True)
    var = jnp.mean(jnp.square(o - mu), axis=-1, keepdims=True)
    o = ((o - mu) * lax.rsqrt(var + EPS)).reshape(Bsz, T, W_RET)
    y_ret = (o * p['ret_norm_g'].astype(F32) * jax.nn.silu(g.astype(F32))).astype(h.dtype)

    xbc = jax.nn.silu(dwconv(xbc, p['ssd_conv_w'], p['ssd_conv_b']))
    xs, bm, cm = jnp.split(xbc, (W_SSD, W_SSD + G_SSD * N_SSD), axis=-1)
    xs_h = heads(xs, H_SSD)
    rep = H_SSD // G_SSD
    bm = jnp.repeat(heads(bm, G_SSD), rep, axis=1)
    cm = jnp.repeat(heads(cm, G_SSD), rep, axis=1)
    dt_raw = dt_raw.astype(F32)
    dt_f = jax.nn.softplus(dt_raw + p['ssd_dt_bias'][0].astype(F32)).transpose(0, 2, 1)
    dt_b = jax.nn.softplus(dt_raw + p['ssd_dt_bias'][1].astype(F32)).transpose(0, 2, 1)
    A = -jnp.exp(p['ssd_a_log'].astype(F32))
    o, sf, sb = bidir_scan(cm, bm, xs_h * dt_f[..., None], xs_h * dt_b[..., None],
                           dt_f * A[0][None, :, None], dt_b * A[1][None, :, None],
                           s_ssd[:, 0], s_ssd[:, 1])
    y = o + p['ssd_d'].astype(F32)[None, :, None, None] * xs_h
    y = y.transpose(0, 2, 1, 3).reshape(Bsz, T, W_SSD)
    y_ssd = rmsnorm(y * jax.nn.silu(z.astype(F32)), p['ssd_norm_g']).astype(h.dtype)

    xl = dwconv(xl, p['lru_conv_w'], p['lru_conv_b'])

    def lru_dir(xd, d, h0):
        xb = xd.reshape(Bsz, T, LRU_BLOCKS, LRU_BW)
        r = jax.nn.sigmoid(jnp.einsum('btnc,ncd->btnd', xb, p['lru_w_a'][d]).reshape(Bsz, T, W_LRU)
                           + p['lru_b_a'][d]).astype(F32)
        i = jax.nn.sigmoid(jnp.einsum('btnc,ncd->btnd', xb, p['lru_w_x'][d]).reshape(Bsz, T, W_LRU)
                           + p['lru_b_x'][d]).astype(F32)
        log_a = -LRU_C * r * jax.nn.softplus(-p['lru_lambda'][d].astype(F32))
        return rglru_scan(log_a, i * xd.astype(F32), h0)

    hf, lf = lru_dir(xl, 0, s_lru[:, 0])
    hb, lb = lru_dir(jnp.flip(xl, axis=1), 1, s_lru[:, 1])
    y_lru = ((hf + jnp.flip(hb, axis=1)) * jax.nn.gelu(gl.astype(F32))).astype(h.dtype)

    out = jnp.concatenate([y_ret, y_ssd, y_lru], axis=-1) @ p['w_out']
    new_states = (jnp.stack([rf, rb], axis=1), jnp.stack([sf, sb], axis=1), jnp.stack([lf, lb], axis=1))
    return out, new_states


def conv_ffn(h, p):
    u = dwconv(h @ p['ffn_w_up'], p['ffn_conv_w'], p['ffn_conv_b'])
    val, gate = jnp.split(u, 2, axis=-1)
    return (jax.nn.silu(gate) * val) @ p['ffn_w_down']


def trunk_layer(x, mod, states, p, pos):
    sh1, sc1, g1, sh2, sc2, g2 = jnp.split(mod[:, None, :].astype(x.dtype), 6, axis=-1)
    h = rmsnorm(x, p['norm1_g']) * (1.0 + sc1) + sh1
    mix, new_states = mixer(h, states, p, pos)
    x = x + g1 * mix
    h = rmsnorm(x, p['norm2_g']) * (1.0 + sc2) + sh2
    x = x + g2 * conv_ffn(h, p)
    return x, new_states


def setup_inputs(seed: int = 0) -> dict:
    key = jax.random.key(seed)
    ks = jax.random.split(key, 40)
    nrm = lambda k, shape, s: jax.random.normal(k, shape, F32) * s
    gain = lambda k, shape: 1.0 + nrm(k, shape, 0.02)
    g0 = 1.0 - 2.0 ** (-5.0 - jnp.arange(H_RET, dtype=F32))
    ret_base = jnp.log(g0) - jnp.log1p(-g0)
    dt0 = jnp.exp(jax.random.uniform(ks[16], (DEPTH, 2, H_SSD), F32, math.log(1e-3), math.log(1e-1)))
    u0 = jax.random.uniform(ks[26], (DEPTH, 2, W_LRU), F32, 0.9, 0.999)
    a0 = u0 ** (1.0 / LRU_C)
    return {
        "x_prompt": nrm(ks[0], (BATCH, SEQ, D_MODEL), 1.0),
        "x_sample": nrm(ks[1], (DEC_BATCH, DEC_SEQ, D_MODEL), 1.0),
        "c": nrm(ks[2], (DEC_BATCH, D_MODEL), 1.0),
        "c_ctx": nrm(ks[3], (D_MODEL,), 1.0),
        "state_ret": nrm(ks[4], (DEC_BATCH, DEPTH, 2, H_RET, DK_RET, DV_RET), 1.0),
        "state_ssd": nrm(ks[5], (DEC_BATCH, DEPTH, 2, H_SSD, N_SSD, P_SSD), 0.1),
        "state_lru": nrm(ks[6], (DEC_BATCH, DEPTH, 2, W_LRU), 0.5),
        "w_ada": nrm(ks[7], (DEPTH, D_MODEL, 6 * D_MODEL), 0.5 * D_MODEL ** -0.5),
        "b_ada": nrm(ks[8], (DEPTH, 6 * D_MODEL), 0.01),
        "norm1_g": gain(ks[9], (DEPTH, D_MODEL)),
        "norm2_g": gain(ks[10], (DEPTH, D_MODEL)),
        "w_in": nrm(ks[11], (DEPTH, D_MODEL, IN_DIM), D_MODEL ** -0.5),
        "ret_decay": ret_base + nrm(ks[12], (DEPTH, 2, H_RET), 0.1),
        "ret_norm_g": gain(ks[13], (DEPTH, W_RET)),
        "ssd_conv_w": nrm(ks[14], (DEPTH, SSD_CONV, CONV_CH), SSD_CONV ** -0.5),
        "ssd_conv_b": nrm(ks[15], (DEPTH, CONV_CH), 0.01),
        "ssd_dt_bias": dt0 + jnp.log(-jnp.expm1(-dt0)),
        "ssd_a_log": jnp.log(jax.random.uniform(ks[17], (DEPTH, 2, H_SSD), F32, 1.0, 16.0)),
        "ssd_d": gain(ks[18], (DEPTH, H_SSD)),
        "ssd_norm_g": gain(ks[19], (DEPTH, W_SSD)),
        "lru_conv_w": nrm(ks[20], (DEPTH, LRU_CONV, W_LRU), LRU_CONV ** -0.5),
        "lru_conv_b": nrm(ks[21], (DEPTH, W_LRU), 0.01),
        "lru_w_a": nrm(ks[22], (DEPTH, 2, LRU_BLOCKS, LRU_BW, LRU_BW), LRU_BW ** -0.5),
        "lru_b_a": nrm(ks[23], (DEPTH, 2, W_LRU), 0.01),
        "lru_w_x": nrm(ks[24], (DEPTH, 2, LRU_BLOCKS, LRU_BW, LRU_BW), LRU_BW ** -0.5),
        "lru_b_x": nrm(ks[25], (DEPTH, 2, W_LRU), 0.01),
        "lru_lambda": jnp.log(a0) - jnp.log1p(-a0),
        "w_out": nrm(ks[27], (DEPTH, W_MIX, D_MODEL), W_MIX ** -0.5),
        "ffn_w_up": nrm(ks[28], (DEPTH, D_MODEL, 2 * D_FF), D_MODEL ** -0.5),
        "ffn_conv_w": nrm(ks[29], (DEPTH, FFN_CONV, 2 * D_FF), FFN_CONV ** -0.5),
        "ffn_conv_b": nrm(ks[30], (DEPTH, 2 * D_FF), 0.01),
        "ffn_w_down": nrm(ks[31], (DEPTH, D_FF, D_MODEL), D_FF ** -0.5),
        "final_norm_g": gain(ks[32], (D_MODEL,)),
    }


def reference(x_prompt, x_sample, c, c_ctx, state_ret, state_ssd, state_lru, w_ada, b_ada,
              norm1_g, norm2_g, w_in, ret_decay, ret_norm_g, ssd_conv_w, ssd_conv_b, ssd_dt_bias,
              ssd_a_log, ssd_d, ssd_norm_g, lru_conv_w, lru_conv_b, lru_w_a, lru_b_a, lru_w_x,
              lru_b_x, lru_lambda, w_out, ffn_w_up, ffn_conv_w, ffn_conv_b, ffn_w_down, final_norm_g):
    Bp = x_prompt.shape[0]
    T_lat = x_sample.shape[1]
    ROWS = T_lat // GRID_W
    rows = jnp.repeat(jnp.arange(ROWS), GRID_W)
    cols = jnp.tile(jnp.arange(GRID_W), ROWS)
    zero_states = (jnp.zeros((Bp, 2, H_RET, DK_RET, DV_RET), F32),
                   jnp.zeros((Bp, 2, H_SSD, N_SSD, P_SSD), F32),
                   jnp.zeros((Bp, 2, W_LRU), F32))
    xp, xs = x_prompt, x_sample
    new_ret, new_ssd, new_lru = [], [], []
    for l in range(DEPTH):
        p = {
            'norm1_g': norm1_g[l], 'norm2_g': norm2_g[l], 'w_in': w_in[l],
            'ret_decay': ret_decay[l], 'ret_norm_g': ret_norm_g[l],
            'ssd_conv_w': ssd_conv_w[l], 'ssd_conv_b': ssd_conv_b[l], 'ssd_dt_bias': ssd_dt_bias[l],
            'ssd_a_log': ssd_a_log[l], 'ssd_d': ssd_d[l], 'ssd_norm_g': ssd_norm_g[l],
            'lru_conv_w': lru_conv_w[l], 'lru_conv_b': lru_conv_b[l], 'lru_w_a': lru_w_a[l],
            'lru_b_a': lru_b_a[l], 'lru_w_x': lru_w_x[l], 'lru_b_x': lru_b_x[l],
            'lru_lambda': lru_lambda[l], 'w_out': w_out[l], 'ffn_w_up': ffn_w_up[l],
            'ffn_conv_w': ffn_conv_w[l], 'ffn_conv_b': ffn_conv_b[l], 'ffn_w_down': ffn_w_down[l],
        }
        mod_ctx = (jax.nn.silu(c_ctx) @ w_ada[l] + b_ada[l])[None, :]
        mod_lat = jax.nn.silu(c) @ w_ada[l] + b_ada[l]
        xp, (sr, ss, sl) = trunk_layer(xp, mod_ctx, zero_states, p, None)
        new_ret.append(sr)
        new_ssd.append(ss)
        new_lru.append(sl)
        xs, _ = trunk_layer(xs, mod_lat, (state_ret[:, l], state_ssd[:, l], state_lru[:, l]), p, (rows, cols))
    y_prompt = rmsnorm(xp, final_norm_g)
    y_sample = rmsnorm(xs, final_norm_g)
    new_state_ret = jnp.stack(new_ret, axis=1).astype(x_prompt.dtype)
    new_state_ssd = jnp.stack(new_ssd, axis=1).astype(x_prompt.dtype)
    new_state_lru = jnp.stack(new_lru, axis=1).astype(x_prompt.dtype)
    return (y_prompt, y_sample, new_state_ret, new_state_ssd, new_state_lru)
```

```python
import numpy as np
from contextlib import ExitStack
import concourse.bass as bass
import concourse.mybir as mybir
from concourse.bass_utils import run_bass_kernel_spmd

F32 = mybir.dt.float32
BF16 = mybir.dt.bfloat16
AF = mybir.ActivationFunctionType
ALU = mybir.AluOpType
AX = mybir.AxisListType

D = 1024
L = 2
NSEG = 6
SEGT = 256
T = NSEG * SEGT
NT = T // 128
W_RET = 384
W_SSD = 384
N_SSD = 128
CONV_CH = 896
W_LRU = 256
D_FF = 2816
IN_DIM = 3334
EPS = 1e-6
GROUPS = ((0, 4), (4, 2))

PK_N1, PK_N2, PK_BADA, PK_RNG, PK_SNG = 0, 8, 16, 64, 67
PK_SCW, PK_SCB, PK_LCW, PK_LCB, PK_LBA, PK_LBX, PK_LLAM = 70, 98, 105, 113, 115, 119, 123
PK_FCW, PK_FCB, PK_RDEC, PK_DTB, PK_ALOG, PK_SSDD = 128, 260, 304, 316, 328, 340
NPK = 724


def piece_list():
    pl = []
    for i in range(12):
        pl.append((("ada", i), "w_ada", i * 512, 512, D))
    for pi, nm in enumerate(("q", "k", "v", "g")):
        pl.append(((nm,), "w_in", pi * 384, 384, D))
    pl.append((("z",), "w_in", 1536, 384, D))
    pl.append((("dt",), "w_in", 2816, 6, D))
    pl.append((("xbc", 0), "w_in", 1920, 512, D))
    pl.append((("xbc", 1), "w_in", 2432, 384, D))
    pl.append((("lru",), "w_in", 2822, 512, D))
    for fc in range(8):
        pl.append((("out", fc), "w_out", fc * 128, 128, D))
    for pi in range(6):
        nfc = 4 if pi < 5 else 2
        pl.append((("upv", pi), "ffn_w_up", pi * 512, nfc * 128, D))
        pl.append((("upg", pi), "ffn_w_up", D_FF + pi * 512, nfc * 128, D))
    for fc in range(8):
        pl.append((("dn", fc), "ffn_w_down", fc * 128, 128, D_FF))
    return pl


def piece_offsets():
    offs, o = {}, 0
    for (name, srcn, col0, ncols, K) in piece_list():
        offs[name] = (o, K // 128, ncols)
        o += (K // 128) * ncols
    return offs, o


POST_DELAY = 5
EPOCH = 3000
N_DMA_SEMS = 28


class Sched:
    ENGS = ("pe", "act", "dve", "pool", "sp")

    def __init__(self, nc, stack):
        self.nc = nc
        self.stack = stack
        self.streams = {e: [] for e in self.ENGS}
        self.count = {e: 0 for e in self.ENGS}
        self.dma_sems = []
        self.dma_cnt = []
        self.dma_group = {}
        self.last_write = {}
        self.readers = {}
        self.seen = {e: {} for e in self.ENGS}
        self.out_events = []
        self.act_dma_events = []
        self.waited = {e: set() for e in self.ENGS}

    def _need(self, eng, ev, force=False):
        if ev[0] == "eng":
            _, src, n = ev
            if src == eng and not force:
                if src in ("pe", "sp"):
                    return None
            if self.seen[eng].get(("eng", src), 0) >= n:
                return None
            self.seen[eng][("eng", src)] = n
            self.waited[src].add(n)
            return ev
        _, idx, val = ev
        val = self.dma_cnt[idx]
        if self.seen[eng].get(("dma", idx), 0) >= val:
            return None
        self.seen[eng][("dma", idx)] = val
        return ("dma", idx, val)

    def _deps(self, eng, reads, writes):
        evs = []
        for r in reads:
            if r in self.last_write:
                evs.append(self.last_write[r])
        for w in writes:
            if w in self.last_write:
                evs.append(self.last_write[w])
            evs.extend(self.readers.get(w, []))
        waits = []
        for ev in evs:
            nd = self._need(eng, ev)
            if nd is not None:
                waits.append(nd)
        return waits

    def _record(self, ev, reads, writes):
        for r in reads:
            self.readers.setdefault(r, []).append(ev)
        for w in writes:
            self.last_write[w] = ev
            self.readers[w] = []

    def op(self, eng, name, reads=(), writes=(), **kw):
        reads, writes = list(reads), list(writes)
        waits = self._deps(eng, reads, writes)
        self.count[eng] += 1
        ev = ("eng", eng, self.count[eng])
        self.streams[eng].append((waits, name, kw, ev))
        self._record(ev, reads, writes)
        return ev

    def dma(self, eng, group, reads=(), writes=(), is_output=False, track=True, **kw):
        reads, writes = list(reads), list(writes)
        waits = self._deps(eng, reads, writes)
        gk = (eng, group)
        if gk not in self.dma_group:
            self.dma_group[gk] = len(self.dma_sems)
            self.dma_sems.append(self.stack.enter_context(self.nc.semaphore(f"dq_{eng}_{group}")))
            self.dma_cnt.append(0)
        idx = self.dma_group[gk]
        self.dma_cnt[idx] += 16
        ev = ("dma", idx, self.dma_cnt[idx])
        self.streams[eng].append((waits, "dma_start", kw, ev))
        self._record(ev, reads, writes)
        if is_output:
            self.out_events.append(ev)
        if track:
            self.act_dma_events.append(ev)
        return ev

    def barrier(self):
        evs = [("eng", e, self.count[e]) for e in ("pe", "act", "dve", "pool") if self.count[e] > 0]
        evs += self.act_dma_events
        self.act_dma_events = []
        for eng in self.ENGS:
            waits = []
            for ev in evs:
                if ev[0] == "eng" and ev[1] == eng:
                    continue
                nd = self._need(eng, ev, force=True)
                if nd is not None:
                    waits.append(nd)
            if waits:
                self.streams[eng].append((waits, None, None, None))

    def finish(self, eng="sp"):
        waits = []
        for ev in self.out_events:
            nd = self._need(eng, ev, force=True)
            if nd is not None:
                waits.append(nd)
        self.streams[eng].append((waits, None, None, None))

    def emit(self):
        nc = self.nc
        streams = self.streams
        rank = {}
        sems = {}
        for e in self.ENGS:
            for i, n in enumerate(sorted(self.waited[e])):
                rank[(e, n)] = i + 1
        nep = {e: (len(self.waited[e]) + EPOCH - 1) // EPOCH for e in self.ENGS}
        for e in self.ENGS:
            for ep in range(nep[e]):
                sems[(e, ep)] = self.stack.enter_context(nc.semaphore(f"s_{e}_{ep}"))

        def sem_of(ev):
            if ev[0] == "dma":
                return self.dma_sems[ev[1]], ev[2]
            r = rank[(ev[1], ev[2])]
            ep, val = divmod(r - 1, EPOCH)
            return sems[(ev[1], ep)], val + 1

        def run(e, name):
            for (waits, iname, kw, ev) in streams[name]:
                for w in waits:
                    sem, val = sem_of(w)
                    e.wait_ge(sem, val)
                if iname is not None:
                    ins = getattr(e, iname)(**kw)
                    if ev[0] == "dma":
                        ins.then_inc(self.dma_sems[ev[1]], 16)
                    elif (ev[1], ev[2]) in rank:
                        sem, _ = sem_of(ev)
                        ins.then_inc(sem, 1)

        with nc.Block() as block:
            @block.tensor
            def _(e):
                run(e, "pe")

            @block.scalar
            def _(e):
                run(e, "act")

            @block.vector
            def _(e):
                run(e, "dve")

            @block.gpsimd
            def _(e):
                run(e, "pool")

            @block.sync
            def _(e):
                run(e, "sp")


class Arena:
    def __init__(self, nc, base=16512, limit=229344):
        self.nc = nc
        self.base = base
        self.top = base
        self.limit = limit
        self.n = 0
        self.peak = base

    def alloc(self, name, shape, dt):
        nbytes = int(np.prod(shape[1:])) * (4 if dt == F32 else 2)
        off = (self.top + 31) // 32 * 32
        assert off + nbytes <= self.limit, f"SBUF overflow allocating {name}: {off + nbytes}"
        self.top = off + nbytes
        self.peak = max(self.peak, self.top)
        self.n += 1
        return self.nc.alloc_sbuf_tensor_at(f"{name}_{self.n}", list(shape), dt, offset=off)

    def mark(self):
        return self.top

    def release(self, m):
        self.top = m


def bc(ap, shape):
    return ap.to_broadcast(list(shape))


class StopBuild(Exception):
    pass


def build_program(dbg=None, stop=None):
    dbg = dbg or []

    import os
    FEAT = os.environ.get("FEAT", "").split(",")

    def stage(name):
        if stop is not None and name == stop:
            raise StopBuild()

    nc = bass.Bass("TRN2", target_bir_lowering=False)
    din = lambda name, shape: nc.dram_tensor(name, list(shape), F32, kind="ExternalInput").ap()
    dout = lambda name, shape: nc.dram_tensor(name, list(shape), F32, kind="ExternalOutput").ap()
    xT_d = din("xT", [D, T])
    cT_d = din("cT", [D, NSEG])
    flags_d = din("flags", [128, 32])
    s0ret_d = din("s0_ret", [L, 2, 128, 192])
    s0ssd_d = din("s0_ssd", [L, 2, 128, 384])
    s0lru_d = din("s0_lru", [L, 128, 4])
    rope_d = din("rope", [4, T, 64])
    pk_d = din("pk", [L, 128, NPK])
    pkf_d = din("pkf", [128, 8])
    lruw_d = din("lruw", [L, 8, 128, 128])
    cmat_d = din("cmat", [6, 128, 128])
    POFF, WTOT = piece_offsets()
    wpk_d = din("wpk", [L, 128, WTOT])
    yT_d = dout("yT", [D, T])
    oret_d = dout("o_ret", [NSEG, L, 2, 128, 192])
    ossd_d = dout("o_ssd", [NSEG, L, 2, 128, 384])
    olru_d = dout("o_lru", [NSEG, L, 2, 2, 128])
    dbg_d = {name: dout("dbg_" + name, shape) for (name, shape) in dbg}

    with ExitStack() as st:
        S = Sched(nc, st)
        AR = Arena(nc)
        ps = [st.enter_context(nc.psum_tensor(f"psb{i}", [128, 512], F32)) for i in range(7)]
        pst = st.enter_context(nc.psum_tensor("pstb", [128, 1024], BF16))
        P = lambda b: ("ps", b)

        def V(name, r, w, **kw):
            return S.op("dve", name, r, w, **kw)

        def A(name, r, w, **kw):
            return S.op("act", name, r, w, **kw)

        def G(name, r, w, **kw):
            return S.op("pool", name, r, w, **kw)

        def PE(r, w, **kw):
            return S.op("pe", "matmul", r, w, **kw)

        def act(out, in_, func, r, w, **kw):
            return S.op("act", "activation", r, w, out=out, in_=in_, func=func, **kw)

        def dump(name, ap, keys):
            if name in dbg_d:
                S.dma("pool", "dbg", reads=keys, is_output=True, out=dbg_d[name], in_=ap)

        xT = AR.alloc("xT", [128, 8, T], F32)
        pk = AR.alloc("pk", [128, L, NPK], F32)
        pkf = AR.alloc("pkf", [128, 8], F32)
        flags = AR.alloc("flags", [128, 32], F32)
        cmat = AR.alloc("cmat", [128, 6, 128], F32)
        identb = AR.alloc("identb", [128, 128], BF16)
        onesb = AR.alloc("onesb", [128, 128], BF16)
        scT = AR.alloc("scT", [128, 8, NSEG], BF16)
        modTs = [AR.alloc(f"modT{i}", [128, 48, NSEG], F32) for i in range(2)]
        A1s = [AR.alloc(f"A1{i}", [128, 8, NSEG], F32) for i in range(2)]
        A2s = [AR.alloc(f"A2{i}", [128, 8, NSEG], F32) for i in range(2)]
        lruw = AR.alloc("lruw", [128, L * 8, 128], BF16)
        lrucl = AR.alloc("lrucl", [128, L, 4], F32)
        retla = AR.alloc("retla", [128, L, 12], F32)
        ssdA = AR.alloc("ssdA", [128, L, 12], F32)
        s0lru = AR.alloc("s0lru", [128, L, 4], F32)
        epsT = AR.alloc("epsT", [128, 1], F32)
        mhalfT = AR.alloc("mhalfT", [128, 8], F32)
        NSLOT = 3
        wbuf = [AR.alloc(f"wbuf{i}", [128, 4096], BF16) for i in range(NSLOT)]
        TRI = [cmat[:, 0, :], cmat[:, 1, :]]
        STRICT = [cmat[:, 2, :], cmat[:, 3, :]]
        ONESF = cmat[:, 4, :]

        for kc in range(8):
            S.dma("sp", "setup", writes=[("xT", kc)], out=xT[:, kc, :], in_=xT_d[kc * 128:(kc + 1) * 128, :])
        for l in range(L):
            S.dma("sp", "setup", writes=["pk"], out=pk[:, l, :], in_=pk_d[l])
            S.dma("sp", "setup", writes=["s0lru"], out=s0lru[:, l, :], in_=s0lru_d[l])
            for j in range(8):
                S.dma("pool", "setup", writes=["lruw"], out=lruw[:, l * 8 + j, :], in_=lruw_d[l, j])
        S.dma("sp", "setup", writes=["pkf"], out=pkf[:], in_=pkf_d)
        S.dma("sp", "setup", writes=["flags"], out=flags[:], in_=flags_d)
        for j in range(6):
            S.dma("sp", "setup", writes=["cmat"], out=cmat[:, j, :], in_=cmat_d[j])
        S.dma("pool", "setup", writes=["identb"], out=identb[:], in_=cmat_d[5])
        S.dma("pool", "setup", writes=["onesb"], out=onesb[:], in_=cmat_d[4])
        V("memset", [], ["epsT"], ap=epsT[:], constant=EPS)
        V("memset", [], ["mhalfT"], ap=mhalfT[:], constant=-0.5)

        m0 = AR.mark()
        cTf = AR.alloc("cTf", [128, 8, NSEG], F32)
        for kc in range(8):
            S.dma("sp", "setup", writes=["cTf"], out=cTf[:, kc, :], in_=cT_d[kc * 128:(kc + 1) * 128, :])
        act(scT[:], cTf[:], AF.Silu, ["cTf"], ["scT"])
        tmp12 = AR.alloc("tmp12", [128, L, 12], F32)
        tmp4 = AR.alloc("tmp4", [128, L, 4], F32)
        act(tmp12[:], pk[:, :, PK_RDEC:PK_RDEC + 12], AF.Exp, ["pk"], ["tmp12"], scale=-1.0)
        act(tmp12[:], tmp12[:], AF.Ln, ["tmp12"], ["tmp12"], bias=1.0)
        V("tensor_scalar_mul", ["tmp12"], ["retla"], out=retla[:], in0=tmp12[:], scalar1=-1.0)
        act(ssdA[:], pk[:, :, PK_ALOG:PK_ALOG + 12], AF.Exp, ["pk"], ["ssdA"])
        V("tensor_scalar_mul", ["ssdA"], ["ssdA"], out=ssdA[:], in0=ssdA[:], scalar1=-1.0)
        act(tmp4[:], pk[:, :, PK_LLAM:PK_LLAM + 4], AF.Exp, ["pk"], ["tmp4"], scale=-1.0)
        act(tmp4[:], tmp4[:], AF.Ln, ["tmp4"], ["tmp4"], bias=1.0)
        V("tensor_scalar_mul", ["tmp4"], ["lrucl"], out=lrucl[:], in0=tmp4[:], scalar1=-8.0)
        S.barrier()
        AR.release(m0)

        wstate = {"i": 0}

        pref = {}

        def _issue_piece(l, name):
            off, nk, ncols = POFF[name]
            slot = wstate["i"] % NSLOT
            wstate["i"] += 1
            assert nk * ncols <= 4096
            buf = wbuf[slot]
            S.dma("pool", f"w{slot}", writes=[("w", slot)], track=False,
                  out=buf[:, 0:nk * ncols], in_=wpk_d[l][:, off:off + nk * ncols])
            view = bass.AP(buf, 0, [[4096, 128], [ncols, nk], [1, ncols]])
            return view, ("w", slot)

        def prefetch(l, name):
            if l < L and (l, name) not in pref:
                pref[(l, name)] = _issue_piece(l, name)

        def load_piece(l, name):
            if (l, name) in pref:
                return pref.pop((l, name))
            assert not pref, f"piece order violated: {name} requested while {list(pref)} prefetched"
            return _issue_piece(l, name)

        def mod_piece(l, piece):
            wv, wk = load_piece(l, ("ada", piece))
            for j4 in range(4):
                fc = piece * 4 + j4
                for kc in range(8):
                    PE([wk, "scT"], [P(0)], out=ps[0][:, fc * 8:fc * 8 + NSEG],
                       lhsT=wv[:, kc, j4 * 128:(j4 + 1) * 128], rhs=scT[:, kc, :],
                       start=(kc == 0), stop=(kc == 7))

        def mod_finish(l):
            modT, mk = modTs[l % 2], ("modT", l % 2)
            psv = bass.AP(ps[0], 0, [[512, 128], [8, 48], [1, NSEG]])
            V("tensor_tensor", [P(0), "pk"], [mk], out=modT[:], in0=psv,
              in1=bc(pk[:, l, PK_BADA:PK_BADA + 48].unsqueeze(2), [128, 48, NSEG]), op=ALU.add)
            for (Ax, sc0, ng, nm) in ((A1s[l % 2], 8, PK_N1, ("A1", l % 2)), (A2s[l % 2], 32, PK_N2, ("A2", l % 2))):
                V("tensor_scalar", [mk], [nm], out=Ax[:], in0=modT[:, sc0:sc0 + 8, :],
                  scalar1=1.0, scalar2=None, op0=ALU.add)
                V("tensor_tensor", [nm, "pk"], [nm], out=Ax[:], in0=Ax[:],
                  in1=bc(pk[:, l, ng:ng + 8].unsqueeze(2), [128, 8, NSEG]), op=ALU.mult)

        def norm_stats(t0, j, sq, rstd):
            bk = 1 + j
            for kc in range(8):
                act(sq[:, j, kc, :], xT[:, kc, t0:t0 + 512], AF.Square, [("xT", kc)], [("sq", j, kc)])
            for kc in range(8):
                PE([("sq", j, kc), "onesb"], [P(bk)], out=ps[bk][:, :], lhsT=onesb[:], rhs=sq[:, j, kc, :],
                   start=(kc == 0), stop=(kc == 7))
            act(rstd[:, j, :], ps[bk][:, :], AF.Sqrt, [P(bk), "epsT"], [("rstd", j)], scale=1.0 / D, bias=epsT[:])
            V("reciprocal", [("rstd", j)], [("rstd", j)], out=rstd[:, j, :], in_=rstd[:, j, :])

        def norm_mod(hT, seg0, nseg, Ax, Akey, sh0, tmpa, modT, mk):
            sq, rstd, tmpx = tmpa
            ntb_ = nseg // 2
            norm_stats(seg0 * SEGT, 0, sq, rstd)
            for tb in range(ntb_):
                t0 = (seg0 + 2 * tb) * SEGT
                j = tb % 2
                if tb + 1 < ntb_:
                    norm_stats((seg0 + 2 * tb + 2) * SEGT, (tb + 1) % 2, sq, rstd)
                for kc in range(8):
                    V("tensor_tensor", [("xT", kc), ("rstd", j)], [("tmpx", kc % 2)], out=tmpx[:, kc % 2, :],
                      in0=xT[:, kc, t0:t0 + 512], in1=rstd[:, j, :], op=ALU.mult)
                    for s2 in range(2):
                        s = seg0 + 2 * tb + s2
                        lt = (2 * tb + s2) * SEGT
                        act(hT[:, kc, lt:lt + SEGT], tmpx[:, kc % 2, s2 * SEGT:(s2 + 1) * SEGT], AF.Identity,
                            [("tmpx", kc % 2), Akey, mk], [("hT", kc)],
                            scale=Ax[:, kc, s:s + 1], bias=modT[:, sh0 + kc, s:s + 1])

        def bidir_scan(l, seg0, nseg, kind, tl, post):
            nch = nseg * 2
            Sst, Sbf, X, E, Ecum, Sm, Am, Qs, Vs, oacc, ofin, sm6, cdx, S0 = (
                tl["S"], tl["Sbf"], tl["X"], tl["E"], tl["Ecum"], tl["Sm"], tl["A"], tl["Qs"],
                tl["Vs"], tl["oacc"], tl["ofin"], tl["sm6"], tl["cdx"], tl["S0"])
            ret = kind == "ret"
            ncs = 192 if ret else 384
            visited = set()
            step_of = [0]
            Sstg = tl["Sstg"]
            stg_cnt = [0, 0]
            for d in range(2):
                V("memset", [], [("S", d)], ap=Sst[:, d, 0:ncs], constant=0.0)

            def proc(d, c):
                seg = seg0 + c // 2
                tok = slice(c * 128, (c + 1) * 128)
                seg_start = (c % 2 == 0) if d == 0 else (c % 2 == 1)
                seg_end = (c % 2 == 1) if d == 0 else (c % 2 == 0)
                xd = 0 if ret else d
                bA, bB, bC = ps[3 * d], ps[3 * d + 1], ps[3 * d + 2]
                kA, kB, kC = P(3 * d), P(3 * d + 1), P(3 * d + 2)
                if seg_start:
                    li = seg if d == 0 else seg + 1
                    fi = (8 + seg) if d == 0 else (16 + seg)
                    V("tensor_scalar", [("S", d), "flags"], [("S", d)], out=Sst[:, d, 0:ncs], in0=Sst[:, d, 0:ncs],
                      scalar1=flags[:, li:li + 1], scalar2=None, op0=ALU.mult)
                    V("scalar_tensor_tensor", [("S", d), "flags", ("S0", d)], [("S", d)], out=Sst[:, d, 0:ncs],
                      in0=S0[:, d, 0:ncs], scalar=flags[:, fi:fi + 1], in1=Sst[:, d, 0:ncs],
                      op0=ALU.mult, op1=ALU.add)
                act(Sbf[:, d, 0:ncs], Sst[:, d, 0:ncs], AF.Copy, [("S", d)], [("Sbf", d)])
                la_ap, la_key = tl["la"](d, c)
                first = (c == (0 if d == 0 else nch - 1))
                dyn = (not ret) or first
                if dyn:
                    PE([la_key, "cmat"], [kC], out=bC[:, 384:390], lhsT=STRICT[d], rhs=la_ap, start=True, stop=True)
                    PE([la_key, "cmat"], [kC], out=bC[:, 392:398], lhsT=ONESF, rhs=la_ap, start=True, stop=True)
                    act(sm6[:, d, 0:6], bC[:, 384:390], AF.Exp, [kC], [("toend", d)])
                    if ret:
                        for hp in range(2):
                            src = bass.AP(bC, hp * 64 * 512 + 392 + hp, [[512, 64], [2, 3]])
                            act(cdx[hp * 64:(hp + 1) * 64, d, 0:3], src, AF.Exp, [kC], [("cdx", d)])
                    else:
                        act(cdx[:, d, 0:6], bC[:, 392:398], AF.Exp, [kC], [("cdx", d)])
                v_ap, v_key = tl["v"](d, c)
                yield
                G("tensor_tensor", [v_key, ("toend", d)], [("Vs", d)], out=Vs[:, d, :].rearrange("p (a b) -> p a b", a=6),
                  in0=v_ap.rearrange("p (a b) -> p a b", a=6),
                  in1=bc(sm6[:, d, 0:6].unsqueeze(2), [128, 6, 64]), op=ALU.mult)
                nb = 3 if ret else 6
                V("tensor_tensor", [("S", d), ("cdx", d)], [("S", d)],
                  out=Sst[:, d, 0:ncs].rearrange("p (a b) -> p a b", a=nb),
                  in0=Sst[:, d, 0:ncs].rearrange("p (a b) -> p a b", a=nb),
                  in1=bc(cdx[:, d, 0:nb].unsqueeze(2), [128, nb, 64]), op=ALU.mult)
                yield
                if ret:
                    for fc in range(3):
                        PE(["Ktok", ("Vs", d)], [kC], out=bC[:, fc * 128:(fc + 1) * 128],
                           lhsT=tl["Ktok"][:, c, fc * 128:(fc + 1) * 128], rhs=Vs[:, d, fc * 128:(fc + 1) * 128],
                           start=(fc == 0), stop=(fc == 2), skip_group_check=True)
                    yield
                    for hp in range(2):
                        pr = slice(hp * 64, hp * 64 + 64)
                        diag = bass.AP(bC, hp * 64 * 512 + hp * 64, [[512, 64], [128, 3], [1, 64]])
                        V("tensor_tensor", [("S", d), kC], [("S", d)],
                          out=Sst[pr, d, 0:192].rearrange("p (a b) -> p a b", a=3), in0=diag,
                          in1=Sst[pr, d, 0:192].rearrange("p (a b) -> p a b", a=3), op=ALU.add)
                else:
                    for h in range(6):
                        oc = slice(h * 64, (h + 1) * 64)
                        g = h // 3
                        PE(["Ktok", ("Vs", d)], [kC], out=bC[:, oc], lhsT=tl["Ktok"][:, c, g * 128:(g + 1) * 128],
                           rhs=Vs[:, d, oc], start=(h == 0), stop=(h == 5), skip_group_check=True)
                    yield
                    V("tensor_tensor", [("S", d), kC], [("S", d)], out=Sst[:, d, 0:ncs], in0=bC[:, 0:ncs],
                      in1=Sst[:, d, 0:ncs], op=ALU.add)
                if seg_end:
                    dst = oret_d[seg, l, d] if ret else ossd_d[seg, l, d]
                    sp_ = 0
                    stg_cnt[d] += 1
                    act(Sstg[:, d, sp_, 0:ncs], Sst[:, d, 0:ncs], AF.Copy, [("S", d)], [("Sstg", d, sp_)])
                    S.dma("sp", "out", reads=[("Sstg", d, sp_)], is_output=True, out=dst, in_=Sstg[:, d, sp_, 0:ncs])
                yield
                for half in range(2):
                    h0 = 3 * half
                    Ek, Ck = ("E", d, half), ("Ecum", d, half)
                    Ed, Cd = E[:, d, half, :], Ecum[:, d, half, :]
                    Xk, Smk, Ak, Qk = ("X", xd, half), ("Sm", d, half), ("A", d, half), ("Qs", d, half)
                    MASK = TRI[d]
                    if dyn:
                        G("tensor_tensor", ["cmat", la_key], [Xk], out=X[:, xd, half, :, :],
                          in0=bc(TRI[d].unsqueeze(1), [128, 3, 128]),
                          in1=bc(la_ap[:, h0:h0 + 3].unsqueeze(2), [128, 3, 128]), op=ALU.mult)
                        if not ret:
                            yield
                        Xf = X[:, xd, half, :, :].rearrange("p a b -> p (a b)")
                        PE([Xk, "cmat"], [kA], out=bA[:, 0:384], lhsT=STRICT[d], rhs=Xf, start=True, stop=True)
                        PE([Xk, "cmat"], [kB], out=bB[:, 0:384], lhsT=ONESF, rhs=Xf, start=True, stop=True)
                    if ret:
                        for hl in range(3):
                            h = h0 + hl
                            PE(["KT", "QT"], [kC], out=bC[:, hl * 128:(hl + 1) * 128],
                               lhsT=tl["KT"][:, h // 2, tok], rhs=tl["QT"][:, h % 2, h // 2, tok],
                               start=(hl == 0), stop=(hl == 2))
                    else:
                        PE(["KT", "QT"], [kC], out=bC[:, 0:128], lhsT=tl["KT"][:, half, tok],
                           rhs=tl["QT"][:, half, tok], start=True, stop=True)
                    yield
                    if dyn:
                        act(Ed, bA[:, 0:384], AF.Exp, [kA], [Ek])
                        act(Cd, bB[:, 0:384], AF.Exp, [kB], [Ck])
                        if ret:
                            V("tensor_tensor", [Ek, "cmat"], [Ek], out=Ed.rearrange("p (a b) -> p a b", a=3),
                              in0=Ed.rearrange("p (a b) -> p a b", a=3),
                              in1=bc(MASK.unsqueeze(1), [128, 3, 128]), op=ALU.mult)
                        yield
                    if ret:
                        V("tensor_tensor", [kC, Ek], [Ak], out=Am[:, d, half, :], in0=bC[:, 0:384], in1=Ed, op=ALU.mult)
                        for hl in range(3):
                            h = h0 + hl
                            G("tensor_tensor", ["QT", Ck], [Qk],
                              out=Qs[:, d, half, hl * 128:(hl + 1) * 128], in0=tl["QT"][:, h % 2, h // 2, tok],
                              in1=Cd[:, hl * 128:(hl + 1) * 128], op=ALU.mult)
                    else:
                        V("tensor_tensor", [kC, "cmat"], [Smk], out=Sm[:, d, half, :], in0=bC[:, 0:128], in1=MASK,
                          op=ALU.mult)
                        V("tensor_tensor", [Smk, Ek], [Ak],
                          out=Am[:, d, half, :].rearrange("p (a b) -> p a b", a=3),
                          in0=bc(Sm[:, d, half, :].unsqueeze(1), [128, 3, 128]),
                          in1=Ed.rearrange("p (a b) -> p a b", a=3), op=ALU.mult)
                        G("tensor_tensor", ["QT", Ck], [Qk],
                          out=Qs[:, d, half, :].rearrange("p (a b) -> p a b", a=3),
                          in0=bc(tl["QT"][:, half, tok].unsqueeze(1), [128, 3, 128]),
                          in1=Cd.rearrange("p (a b) -> p a b", a=3), op=ALU.mult)
                    yield
                    for hl in range(3):
                        h = h0 + hl
                        oc = slice(h * 64, (h + 1) * 64)
                        PE([Ak, v_key], [P(6)], out=ps[6][:, oc], lhsT=Am[:, d, half, hl * 128:(hl + 1) * 128],
                           rhs=v_ap[:, oc], start=(hl == 0), stop=False, skip_group_check=True)
                    for hl in range(3):
                        h = h0 + hl
                        oc = slice(h * 64, (h + 1) * 64)
                        rhs_ = Sbf[:, d, (h // 2) * 64:(h // 2) * 64 + 64] if ret else Sbf[:, d, oc]
                        PE([Qk, ("Sbf", d)], [P(6)], out=ps[6][:, oc], lhsT=Qs[:, d, half, hl * 128:(hl + 1) * 128],
                           rhs=rhs_, start=False, stop=(hl == 2), skip_group_check=True)
                    hc = slice(half * 192, (half + 1) * 192)
                    if c not in visited:
                        act(oacc[:, c, hc], ps[6][:, hc], AF.Copy, [P(6)], [("oacc", c)])
                    else:
                        V("tensor_tensor", [P(6), ("oacc", c)], [("ofin", d)], out=ofin[:, d, hc], in0=ps[6][:, hc],
                          in1=oacc[:, c, hc], op=ALU.add)
                    yield
                if c in visited:
                    pending.append((c, ofin[:, d, :], ("ofin", d), step_of[0]))
                visited.add(c)

            pending = []
            for step in range(nch):
                step_of[0] = step
                gens = [proc(0, step), proc(1, nch - 1 - step)]
                alive = [True, True]
                rnd = 0
                while any(alive):
                    for i in range(2):
                        if alive[i]:
                            try:
                                next(gens[i])
                            except StopIteration:
                                alive[i] = False
                    rnd += 1
                    if rnd == POST_DELAY and pending:
                        prev = [p for p in pending if p[3] < step]
                        for (c_, o_, k_, _) in prev:
                            post(c_, o_, k_)
                        pending[:] = [p for p in pending if p[3] >= step]
            for (c_, o_, k_, _) in pending:
                post(c_, o_, k_)

        def transpose_batch(srcs, src_keys):
            key = ("ps", 7)
            for i, sap in enumerate(srcs):
                S.op("pe", "transpose", list(src_keys) + ["identb"], [key], out=pst[:, i * 128:(i + 1) * 128],
                     in_=sap, identity=identb[:])
            return pst[:, 0:len(srcs) * 128], key

        m_persist = AR.mark()
        try:
          stage("setup")
          for l in range(L):
            if l == 0:
                for piece_ in range(12):
                    mod_piece(0, piece_)
            mod_finish(l)
            modT, mk = modTs[l % 2], ("modT", l % 2)
            A1, A2 = A1s[l % 2], A2s[l % 2]
            stage("mod")
            for (seg0, nseg) in GROUPS:
                  ntok = nseg * SEGT
                  nch = nseg * 2
                  ntb = nseg // 2
                  gt0 = seg0 * SEGT
                  S.barrier()
                  mg = AR.mark()
                  hT = AR.alloc("hT", [128, 8, ntok], BF16)
                  ycT = AR.alloc("ycT", [128, 8, ntok], BF16)
                  for nm_ in ("q", "k", "v"):
                      prefetch(l, (nm_,))
                  mn = AR.mark()
                  tmpa = (AR.alloc("sq", [128, 2, 8, 512], BF16), AR.alloc("rstd", [128, 2, 512], F32),
                          AR.alloc("tmpx", [128, 2, 512], F32))
                  norm_mod(hT, seg0, nseg, A1, ("A1", l % 2), 0, tmpa, modT, mk)
                  S.barrier()
                  stage("norm1")
                  AR.release(mn)
                  if l == 0 and seg0 == 0:
                      dump("hT", hT[:, 0, 0:512], [("hT", k) for k in range(8)])

                  mret = AR.mark()
                  Ktok = AR.alloc("Ktok", [128, nch, 384], BF16)
                  Vtok = AR.alloc("Vtok", [128, nch, 384], BF16)
                  SGtok = AR.alloc("SGtok", [128, nch, 384], BF16)
                  QT = AR.alloc("QT", [128, 2, 3, ntok], BF16)
                  V("memset", [], ["QT"], ap=QT[:], constant=0.0)
                  KT = AR.alloc("KT", [128, 3, ntok], BF16)
                  oacc = AR.alloc("oacc", [128, nch, 384], F32)
                  S0r = AR.alloc("S0", [128, 2, 384], F32)
                  for d in range(2):
                      S.dma("sp", "s0", writes=[("S0", d)], out=S0r[:, d, 0:192], in_=s0ret_d[l, d])
                  mproj = AR.mark()
                  ropeT = [AR.alloc(f"rope{i}", [128, nch, 64], F32) for i in range(4)]
                  for i in range(4):
                      S.dma("sp", "rope", writes=["rope"], out=ropeT[i][:],
                            in_=rope_d[i, gt0:gt0 + ntok, :].rearrange("(c p) f -> p c f", p=128))
                  qtmp = [AR.alloc(f"qtmp{i}", [128, 384], BF16) for i in range(2)]
                  rt1 = [AR.alloc(f"rt1{i}", [128, 384], F32) for i in range(2)]
                  rt2 = [AR.alloc(f"rt2{i}", [128, 384], F32) for i in range(2)]
                  rsrc = [AR.alloc(f"rsrc{i}", [128, 384], F32) for i in range(2)]
                  hkeys = [("hT", k) for k in range(8)]
                  def rope_tail(pname, c, qt_, dkey, Ktok=Ktok, QT=QT, KT=KT):
                      sl_ = [(qt_[:, fc * 128:(fc + 1) * 128] if pname == "q" else Ktok[:, c, fc * 128:(fc + 1) * 128]) for fc in range(3)]
                      o_, k_ = transpose_batch(sl_, [dkey])
                      o3_ = o_.rearrange("p (a b) -> p a b", a=3)
                      if pname == "q":
                          for hp in range(2):
                              pr = slice(hp * 64, hp * 64 + 64)
                              act(QT[pr, hp, :, c * 128:(c + 1) * 128], o3_[pr], AF.Copy, [k_], ["QT"])
                      else:
                          act(KT[:, :, c * 128:(c + 1) * 128], o3_, AF.Copy, [k_], ["KT"])

                  pend_r = None
                  for pi, pname in enumerate(("q", "k", "v", "g")):
                      wv, wk = load_piece(l, (pname,))
                      if pname == "v" and pend_r is not None:
                          rope_tail(*pend_r)
                          pend_r = None
                      for c in range(nch):
                          bk = (pi * nch + c) % 7
                          for kc in range(8):
                              PE(hkeys + [wk], [P(bk)], out=ps[bk][:, 0:384], lhsT=hT[:, kc, c * 128:(c + 1) * 128],
                                 rhs=wv[:, kc, :], start=(kc == 0), stop=(kc == 7))
                          if False:
                              pass
                          elif pname in ("q", "k"):
                              rb = c % 2
                              qt_, rs_, r1_, r2_ = qtmp[rb], rsrc[rb], rt1[rb], rt2[rb]
                              dst = qt_[:] if pname == "q" else Ktok[:, c, :]
                              dkey = ("qtmp", rb) if pname == "q" else "Ktok"
                              ct, sn = (ropeT[0], ropeT[1]) if pname == "q" else (ropeT[2], ropeT[3])
                              act(rs_[:], ps[bk][:, 0:384], AF.Copy, [P(bk)], [("rsrc", rb)])
                              V("tensor_tensor", [("rsrc", rb), "rope"], [("rt1", rb)],
                                out=r1_[:].rearrange("p (a b) -> p a b", a=6), in0=rs_[:].rearrange("p (a b) -> p a b", a=6),
                                in1=bc(ct[:, c, :].unsqueeze(1), [128, 6, 64]), op=ALU.mult)
                              s4 = rs_[:].rearrange("p (a r f c) -> p a r f c", a=6, r=2, f=2)
                              r24 = r2_[:].rearrange("p (a r f c) -> p a r f c", a=6, r=2, f=2)
                              sn4 = sn[:, c, :].rearrange("p (r f c) -> p r f c", r=2, f=2)
                              for hf in range(2):
                                  V("tensor_tensor", [("rsrc", rb), "rope"], [("rt2", rb)], out=r24[:, :, :, hf, :],
                                    in0=s4[:, :, :, 1 - hf, :],
                                    in1=bc(sn4[:, :, hf, :].unsqueeze(1), [128, 6, 2, 16]), op=ALU.mult)
                              V("tensor_tensor", [("rt1", rb), ("rt2", rb)], [dkey], out=dst, in0=r1_[:], in1=r2_[:], op=ALU.add)
                              if pend_r is not None:
                                  rope_tail(*pend_r)
                              pend_r = (pname, c, qt_, dkey)
                          elif pname == "v":
                              act(Vtok[:, c, :], ps[bk][:, 0:384], AF.Copy, [P(bk)], ["Vtok"])
                          else:
                              act(SGtok[:, c, :], ps[bk][:, 0:384], AF.Silu, [P(bk)], ["SGtok"])
                  if l == 0 and seg0 == 0:
                      dump("retla", retla[:, 0, :], ["retla"])
                      dump("QT", QT[:, 0, 0, 0:512], ["QT"])
                      dump("Ktok", Ktok[:, 0, :], ["Ktok"])
                  stage("retproj")
                  if "nossd" not in FEAT:
                      prefetch(l, ("z",)); prefetch(l, ("dt",)); prefetch(l, ("xbc", 0))
                  S.barrier()
                  AR.release(mproj)
                  tl = dict(
                      S=AR.alloc("Sst", [128, 2, 384], F32), Sbf=AR.alloc("Sbf", [128, 2, 384], BF16),
                      X=AR.alloc("X", [128, 1, 2, 3, 128], F32), E=AR.alloc("E", [128, 2, 2, 384], F32),
                      Ecum=AR.alloc("Ecum", [128, 2, 2, 384], F32), Sm=None,
                      A=AR.alloc("Am", [128, 2, 2, 384], BF16), Qs=AR.alloc("Qs", [128, 2, 2, 384], BF16),
                      V=None, Vs=AR.alloc("Vs", [128, 2, 384], BF16), oacc=oacc,
                      ofin=AR.alloc("ofin", [128, 2, 384], F32), sm6=AR.alloc("sm6", [128, 2, 8], F32),
                      cdx=AR.alloc("cdx", [128, 2, 8], F32), S0=S0r, Sstg=AR.alloc("Sstg", [128, 2, 1, 192], F32),
                      QT=QT, KT=KT, Ktok=Ktok)
                  tl["la"] = lambda d, c, l=l: (retla[:, l, d * 6:(d + 1) * 6], "retla")
                  tl["v"] = lambda d, c: (Vtok[:, c, :], "Vtok")
                  pr_ = dict(msum=AR.alloc("msum", [128, 8], F32), cen=AR.alloc("cen", [128, 384], F32),
                             sq=AR.alloc("sqr", [128, 384], F32), ynb=AR.alloc("ynb", [128, 384], BF16))

                  def post_ret(c, ofin, ofk, l=l, pr_=pr_, SGtok=SGtok, ycT=ycT, seg0=seg0):
                      msum, cen, sqr, ynb = pr_["msum"], pr_["cen"], pr_["sq"], pr_["ynb"]
                      o3 = ofin.rearrange("p (a b) -> p a b", a=6)
                      V("tensor_reduce", [ofk], ["msum"], out=msum[:, 0:6], in_=o3, axis=AX.X, op=ALU.add)
                      V("tensor_scalar_mul", ["msum"], ["msum"], out=msum[:, 0:6], in0=msum[:, 0:6], scalar1=-1.0 / 64)
                      c3 = cen[:].rearrange("p (a b) -> p a b", a=6)
                      V("tensor_tensor", [ofk, "msum"], ["cen"], out=c3, in0=o3,
                        in1=bc(msum[:, 0:6].unsqueeze(2), [128, 6, 64]), op=ALU.add)
                      V("tensor_tensor", ["cen"], ["sqr"], out=sqr[:], in0=cen[:], in1=cen[:], op=ALU.mult)
                      V("tensor_reduce", ["sqr"], ["msum"], out=msum[:, 0:6],
                        in_=sqr[:].rearrange("p (a b) -> p a b", a=6), axis=AX.X, op=ALU.add)
                      V("tensor_scalar", ["msum"], ["msum"], out=msum[:, 0:6], in0=msum[:, 0:6], scalar1=1.0 / 64, scalar2=EPS,
                        op0=ALU.mult, op1=ALU.add)
                      G("tensor_tensor", ["msum", "mhalfT"], ["msum"], out=msum[:, 0:6], in0=msum[:, 0:6], in1=mhalfT[:, 0:6],
                        op=ALU.pow)
                      V("tensor_tensor", ["cen", "msum"], ["cen"], out=c3, in0=c3,
                        in1=bc(msum[:, 0:6].unsqueeze(2), [128, 6, 64]), op=ALU.mult)
                      V("tensor_tensor", ["cen", "SGtok"], ["ynb"], out=ynb[:], in0=cen[:], in1=SGtok[:, c, :], op=ALU.mult)
                      o_, k_ = transpose_batch([ynb[:, fc * 128:(fc + 1) * 128] for fc in range(3)], ["ynb"])
                      V("tensor_tensor", [k_, "pk"], [("ycT", fc) for fc in range(3)], out=ycT[:, 0:3, c * 128:(c + 1) * 128],
                        in0=o_.rearrange("p (a b) -> p a b", a=3),
                        in1=bc(pk[:, l, PK_RNG:PK_RNG + 3].unsqueeze(2), [128, 3, 128]), op=ALU.mult)

                  bidir_scan(l, seg0, nseg, "ret", tl, post_ret)
                  stage("retscan")
                  if l == 0 and seg0 == 0:
                      dump("ycT_ret", ycT[:, 0, 0:512], [("ycT", 0)])
                      dump("ycT_ret1", ycT[:, 1, 0:512], [("ycT", 1)])
                      dump("ycT_ret2", ycT[:, 2, 0:512], [("ycT", 2)])
                  S.barrier()
                  AR.release(mret)

                  if "nossd" in FEAT:
                      V("memset", [], [("ycT", k) for k in range(3, 6)], ap=ycT[:, 3:6, :], constant=0.0)
                  else:
                      mssd = AR.mark()
                      SZtok = AR.alloc("SZtok", [128, nch, 384], BF16)
                      BCT = AR.alloc("BCT", [128, 4, ntok], BF16)
                      XStok = AR.alloc("XStok", [128, nch, 384], BF16)
                      Btok = AR.alloc("Btok", [128, nch, 256], BF16)
                      dtT = AR.alloc("dtT", [128, nch, 12], F32)
                      laT = AR.alloc("laT", [128, nch, 12], F32)
                      oacc = AR.alloc("oacc2", [128, nch, 384], F32)
                      S0s = AR.alloc("S0", [128, 2, 384], F32)
                      for d in range(2):
                          S.dma("sp", "s0", writes=[("S0", d)], out=S0s[:, d, :], in_=s0ssd_d[l, d])
                      mproj = AR.mark()
                      NXB = 4
                      xpad = [AR.alloc(f"xpad{i}", [128, nseg, 259], F32) for i in range(NXB)]
                      cacc = [AR.alloc(f"cacc{i}", [128, ntok], F32) for i in range(NXB)]
                      xsb = [AR.alloc(f"xsb{i}", [128, ntok], BF16) for i in range(2)]
                      for i in range(NXB):
                          V("memset", [], [("xpad", i)], ap=xpad[i][:], constant=0.0)
                      wv, wk = load_piece(l, ("z",))
                      for c in range(nch):
                          bk = c % 7
                          for kc in range(8):
                              PE(hkeys + [wk], [P(bk)], out=ps[bk][:, 0:384], lhsT=hT[:, kc, c * 128:(c + 1) * 128],
                                 rhs=wv[:, kc, :], start=(kc == 0), stop=(kc == 7))
                          act(SZtok[:, c, :], ps[bk][:, 0:384], AF.Silu, [P(bk)], ["SZtok"])
                      wv, wk = load_piece(l, ("dt",))
                      for c in range(nch):
                          for kc in range(8):
                              PE(hkeys + [wk], [P(6)], out=ps[6][:, c * 8:c * 8 + 6], lhsT=hT[:, kc, c * 128:(c + 1) * 128],
                                 rhs=wv[:, kc, :], start=(kc == 0), stop=(kc == 7))
                      V("tensor_tensor", [P(6), "pk"], ["dtT"], out=dtT[:].rearrange("p c (a b) -> p c a b", a=2),
                        in0=bass.AP(ps[6], 0, [[512, 128], [8, nch], [0, 2], [1, 6]]),
                        in1=bass.AP(pk, l * NPK + PK_DTB, [[L * NPK, 128], [0, nch], [6, 2], [1, 6]]), op=ALU.add)
                      act(dtT[:], dtT[:], AF.Exp, ["dtT"], ["dtT"])
                      act(dtT[:], dtT[:], AF.Ln, ["dtT"], ["dtT"], bias=1.0)
                      V("tensor_tensor", ["dtT", "ssdA"], ["laT"], out=laT[:], in0=dtT[:],
                        in1=bass.AP(ssdA, l * 12, [[L * 12, 128], [0, nch], [1, 12]]), op=ALU.mult)
                      def xbc_tail(fc, par, ca, xsb=xsb, XStok=XStok, BCT=BCT, Btok=Btok, nch=nch):
                          if fc < 3:
                              act(xsb[par % 2][:], ca[:], AF.Silu, [("cacc", par)], [("xsb", par % 2)])
                              o_, k_ = transpose_batch([xsb[par % 2][:, c * 128:(c + 1) * 128] for c in range(nch)], [("xsb", par % 2)])
                              act(XStok[:, :, fc * 128:(fc + 1) * 128], o_.rearrange("p (a b) -> p a b", a=nch), AF.Copy,
                                  [k_], ["XStok"])
                          else:
                              bi = fc - 3
                              act(BCT[:, bi, :], ca[:], AF.Silu, [("cacc", par)], [("BCT", bi)])
                              if bi < 2:
                                  o_, k_ = transpose_batch([BCT[:, bi, c * 128:(c + 1) * 128] for c in range(nch)], [("BCT", bi)])
                                  act(Btok[:, :, bi * 128:(bi + 1) * 128], o_.rearrange("p (a b) -> p a b", a=nch), AF.Copy,
                                      [k_], ["Btok"])

                      pend_x = None
                      for pi, (col0, nfc) in enumerate(((1920, 4), (2432, 3))):
                          wv, wk = load_piece(l, ("xbc", pi))
                          for j in range(nfc):
                              fc = pi * 4 + j
                              par = fc % NXB
                              xp, ca = xpad[par], cacc[par]
                              for tb in range(ntb):
                                  bk = (fc * 2 + tb) % 6
                                  for kc in range(8):
                                      PE(hkeys + [wk], [P(bk)], out=ps[bk][:, :], lhsT=wv[:, kc, j * 128:(j + 1) * 128],
                                         rhs=hT[:, kc, tb * 512:(tb + 1) * 512], start=(kc == 0), stop=(kc == 7))
                                  act(xp[:, 2 * tb:2 * tb + 2, 1:257], ps[bk][:, :].rearrange("p (a b) -> p a b", a=2),
                                      AF.Copy, [P(bk)], [("xpad", par)])
                              lk = flags[:, seg0 + 1:seg0 + nseg]
                              V("tensor_tensor", [("xpad", par), "flags"], [("xpad", par)], out=xp[:, 1:nseg, 0:1],
                                in0=xp[:, 0:nseg - 1, 256:257], in1=lk.unsqueeze(2), op=ALU.mult)
                              V("tensor_tensor", [("xpad", par), "flags"], [("xpad", par)], out=xp[:, 0:nseg - 1, 257:259],
                                in0=xp[:, 1:nseg, 1:3], in1=bc(lk.unsqueeze(2), [128, nseg - 1, 2]), op=ALU.mult)
                              ca3 = ca[:].rearrange("p (a b) -> p a b", a=nseg)
                              wc = PK_SCW + fc * 4
                              act(ca3, xp[:, :, 0:256], AF.Identity, [("xpad", par), "pk"], [("cacc", par)],
                                  scale=pk[:, l, wc:wc + 1], bias=pk[:, l, PK_SCB + fc:PK_SCB + fc + 1])
                              for tap in range(1, 4):
                                  V("scalar_tensor_tensor", [("xpad", par), "pk", ("cacc", par)], [("cacc", par)], out=ca3,
                                    in0=xp[:, :, tap:tap + 256], scalar=pk[:, l, wc + tap:wc + tap + 1], in1=ca3,
                                    op0=ALU.mult, op1=ALU.add)
                              if pend_x is not None:
                                  xbc_tail(*pend_x)
                              pend_x = (fc, par, ca)
                      xbc_tail(*pend_x)
                      stage("ssdproj")
                      prefetch(l, ("lru",)); prefetch(l, ("out", 0)); prefetch(l, ("out", 1))
                      S.barrier()
                      AR.release(mproj)
                      tl = dict(
                          S=AR.alloc("Sst", [128, 2, 384], F32), Sbf=AR.alloc("Sbf", [128, 2, 384], BF16),
                          X=AR.alloc("X", [128, 2, 2, 3, 128], F32), E=AR.alloc("E", [128, 2, 2, 384], F32),
                          Ecum=AR.alloc("Ecum", [128, 2, 2, 384], F32), Sm=AR.alloc("Sm", [128, 2, 2, 128], F32),
                          A=AR.alloc("Am", [128, 2, 2, 384], BF16), Qs=AR.alloc("Qs", [128, 2, 2, 384], BF16),
                          V=None, Vs=AR.alloc("Vs", [128, 2, 384], BF16), oacc=oacc,
                          ofin=AR.alloc("ofin", [128, 2, 384], F32), sm6=AR.alloc("sm6", [128, 2, 8], F32),
                          cdx=AR.alloc("cdx", [128, 2, 8], F32), S0=S0s, Sstg=AR.alloc("Sstg", [128, 2, 1, 384], F32),
                          QT=BCT[:, 2:4, :], KT=BCT[:, 0:2, :], Ktok=Btok)
                      tl["la"] = lambda d, c, laT=laT: (laT[:, c, d * 6:(d + 1) * 6], "laT")
                      Vt = [AR.alloc(f"Vt{i}", [128, 384], BF16) for i in range(4)]
                      vcnt = {"i": 0}

                      def v_ssd(d, c, Vt=Vt, vcnt=vcnt, XStok=XStok, dtT=dtT):
                          i = vcnt["i"] % 4
                          vcnt["i"] += 1
                          G("tensor_tensor", ["XStok", "dtT"], [("Vt", i)], out=Vt[i][:].rearrange("p (a b) -> p a b", a=6),
                            in0=XStok[:, c, :].rearrange("p (a b) -> p a b", a=6),
                            in1=bc(dtT[:, c, d * 6:(d + 1) * 6].unsqueeze(2), [128, 6, 64]), op=ALU.mult)
                          return Vt[i][:], ("Vt", i)
                      tl["v"] = v_ssd
                      ps_ = dict(u=AR.alloc("u", [128, 384], F32), junk=AR.alloc("junk", [128, 384], F32),
                                 ss=AR.alloc("ss", [128, 2], F32), unb=AR.alloc("unb", [128, 384], BF16))

                      def post_ssd(c, ofin, ofk, l=l, ps_=ps_, SZtok=SZtok, XStok=XStok, ycT=ycT):
                          u, junk, ss, unb = ps_["u"], ps_["junk"], ps_["ss"], ps_["unb"]
                          V("tensor_tensor", ["XStok", "pk"], ["u"], out=u[:], in0=XStok[:, c, :],
                            in1=pk[:, l, PK_SSDD:PK_SSDD + 384], op=ALU.mult)
                          V("tensor_tensor", ["u", ofk], ["u"], out=u[:], in0=u[:], in1=ofin, op=ALU.add)
                          V("tensor_tensor", ["u", "SZtok"], ["u"], out=u[:], in0=u[:], in1=SZtok[:, c, :], op=ALU.mult)
                          act(junk[:], u[:], AF.Square, ["u"], ["junk", "ss"], accum_out=ss[:, 0:1])
                          V("tensor_scalar", ["ss"], ["ss"], out=ss[:, 0:1], in0=ss[:, 0:1], scalar1=1.0 / 384, scalar2=EPS,
                            op0=ALU.mult, op1=ALU.add)
                          G("tensor_tensor", ["ss", "mhalfT"], ["ss"], out=ss[:, 0:1], in0=ss[:, 0:1], in1=mhalfT[:, 0:1],
                            op=ALU.pow)
                          V("tensor_scalar", ["u", "ss"], ["unb"], out=unb[:], in0=u[:], scalar1=ss[:, 0:1], scalar2=None,
                            op0=ALU.mult)
                          o_, k_ = transpose_batch([unb[:, fc * 128:(fc + 1) * 128] for fc in range(3)], ["unb"])
                          V("tensor_tensor", [k_, "pk"], [("ycT", 3 + fc) for fc in range(3)],
                            out=ycT[:, 3:6, c * 128:(c + 1) * 128], in0=o_.rearrange("p (a b) -> p a b", a=3),
                            in1=bc(pk[:, l, PK_SNG:PK_SNG + 3].unsqueeze(2), [128, 3, 128]), op=ALU.mult)

                      bidir_scan(l, seg0, nseg, "ssd", tl, post_ssd)
                      stage("ssdscan")
                      if l == 0 and seg0 == 0:
                          for fc in range(3):
                              dump(f"ycT_ssd{fc}", ycT[:, 3 + fc, 0:512], [("ycT", 3 + fc)])
                      S.barrier()
                      AR.release(mssd)

                  if "nolru" in FEAT:
                      V("memset", [], [("ycT", k) for k in range(6, 8)], ap=ycT[:, 6:8, :], constant=0.0)
                  else:
                      mlru = AR.mark()
                      xc = AR.alloc("xc", [128, 2, ntok], F32)
                      xcb = AR.alloc("xcb", [128, 2, ntok], BF16)
                      glT = AR.alloc("glT", [128, 2, ntok], F32)
                      hsT = AR.alloc("hsT", [128, 2, 2, ntok], F32)
                      ini = AR.alloc("ini", [128, 4], F32)
                      msub = AR.mark()
                      xpad = [AR.alloc(f"xpadl{i}", [128, nseg, 259], F32) for i in range(2)]
                      ga = AR.alloc("gscr", [128, ntok], F32)
                      gak = "gscr"
                      for i in range(2):
                          V("memset", [], [("xpad", i)], ap=xpad[i][:], constant=0.0)
                      wv, wk = load_piece(l, ("lru",))
                      for j in range(4):
                          par = j % 2
                          xp = xpad[par]
                          for tb in range(ntb):
                              bk = (j * 2 + tb) % 6
                              for kc in range(8):
                                  PE(hkeys + [wk], [P(bk)], out=ps[bk][:, :], lhsT=wv[:, kc, j * 128:(j + 1) * 128],
                                     rhs=hT[:, kc, tb * 512:(tb + 1) * 512], start=(kc == 0), stop=(kc == 7))
                              if j < 2:
                                  act(xp[:, 2 * tb:2 * tb + 2, 1:257], ps[bk][:, :].rearrange("p (a b) -> p a b", a=2),
                                      AF.Copy, [P(bk)], [("xpad", par)])
                              else:
                                  act(glT[:, j - 2, tb * 512:(tb + 1) * 512], ps[bk][:, :], AF.Copy, [P(bk)], [("glT", j - 2)])
                          if j < 2:
                              lk = flags[:, seg0 + 1:seg0 + nseg]
                              V("tensor_tensor", [("xpad", par), "flags"], [("xpad", par)], out=xp[:, 1:nseg, 0:1],
                                in0=xp[:, 0:nseg - 1, 256:257], in1=lk.unsqueeze(2), op=ALU.mult)
                              V("tensor_tensor", [("xpad", par), "flags"], [("xpad", par)], out=xp[:, 0:nseg - 1, 257:259],
                                in0=xp[:, 1:nseg, 1:3], in1=bc(lk.unsqueeze(2), [128, nseg - 1, 2]), op=ALU.mult)
                              ca3 = xc[:, j, :].rearrange("p (a b) -> p a b", a=nseg)
                              wc = PK_LCW + j * 4
                              act(ca3, xp[:, :, 0:256], AF.Identity, [("xpad", par), "pk"], [("xc", j)],
                                  scale=pk[:, l, wc:wc + 1], bias=pk[:, l, PK_LCB + j:PK_LCB + j + 1])
                              for tap in range(1, 4):
                                  V("scalar_tensor_tensor", [("xpad", par), "pk", ("xc", j)], [("xc", j)], out=ca3,
                                    in0=xp[:, :, tap:tap + 256], scalar=pk[:, l, wc + tap:wc + tap + 1], in1=ca3,
                                    op0=ALU.mult, op1=ALU.add)
                              act(xcb[:, j, :], xc[:, j, :], AF.Copy, [("xc", j)], [("xcb", j)])
                          else:
                              g_ = glT[:, j - 2, :]
                              gk = ("glT", j - 2)
                              act(ga[:], g_, AF.Square, [gk], [gak])
                              V("tensor_scalar", [gak], [gak], out=ga[:], in0=ga[:], scalar1=0.044715, scalar2=1.0,
                                op0=ALU.mult, op1=ALU.add)
                              V("tensor_tensor", [gak, gk], [gak], out=ga[:], in0=ga[:], in1=g_, op=ALU.mult)
                              act(ga[:], ga[:], AF.Tanh, [gak], [gak], scale=0.7978845608028654)
                              V("scalar_tensor_tensor", [gak, gk], [gk], out=g_, in0=ga[:], scalar=1.0, in1=g_,
                                op0=ALU.add, op1=ALU.mult)
                              V("tensor_scalar_mul", [gk], [gk], out=g_, in0=g_, scalar1=0.5)
                      S.barrier()
                      AR.release(msub)
                      gaL = [AR.alloc(f"ga{i}", [128, ntok], F32) for i in range(4)]
                      gbL = [AR.alloc(f"gb{i}", [128, ntok], F32) for i in range(4)]
                      giL = [AR.alloc(f"gi{i}", [128, ntok], F32) for i in range(4)]
                      units = [(d, ch) for d in range(2) for ch in range(2)]
                      U = {u: (gaL[i], gbL[i], giL[i], f"ga{i}", f"gb{i}", f"gi{i}") for i, u in enumerate(units)}
                      bkc = 0
                      for (d, ch) in units:
                          ga, gb, gi, gak, gbk, gik = U[(d, ch)]
                          for gi_, (dstt, bcol) in enumerate(((ga, PK_LBA), (gi, PK_LBX))):
                              for tb in range(ntb):
                                  bk = bkc % 7
                                  bkc += 1
                                  PE([("xcb", ch), "lruw"], [P(bk)], out=ps[bk][:, :],
                                     lhsT=lruw[:, l * 8 + d * 4 + gi_ * 2 + ch, :], rhs=xcb[:, ch, tb * 512:(tb + 1) * 512],
                                     start=True, stop=True)
                                  act(dstt[:, tb * 512:(tb + 1) * 512], ps[bk][:, :], AF.Sigmoid, [P(bk), "pk"],
                                      [gak if gi_ == 0 else gik],
                                      bias=pk[:, l, bcol + 2 * d + ch:bcol + 2 * d + ch + 1])
                      for (d, ch) in units:
                          ga, gb, gi, gak, gbk, gik = U[(d, ch)]
                          act(ga[:], ga[:], AF.Exp, [gak, "lrucl"], [gak], scale=lrucl[:, l, 2 * d + ch:2 * d + ch + 1])
                      for (d, ch) in units:
                          ga, gb, gi, gak, gbk, gik = U[(d, ch)]
                          V("tensor_tensor", [gak], [gbk], out=gb[:], in0=ga[:], in1=ga[:], op=ALU.mult)
                          G("tensor_tensor", [gik, ("xc", ch)], [gik], out=gi[:], in0=gi[:], in1=xc[:, ch, :], op=ALU.mult)
                      for (d, ch) in units:
                          ga, gb, gi, gak, gbk, gik = U[(d, ch)]
                          act(gb[:], gb[:], AF.Sqrt, [gbk], [gbk], scale=-1.0, bias=1.0)
                      for (d, ch) in units:
                          ga, gb, gi, gak, gbk, gik = U[(d, ch)]
                          V("tensor_tensor", [gbk, gik], [gbk], out=gb[:], in0=gb[:], in1=gi[:], op=ALU.mult)
                      lru_outs = []
                      for si in range(nseg):
                          for ui, (d, ch) in enumerate(units):
                              ga, gb, gi, gak, gbk, gik = U[(d, ch)]
                              hk = ("hsT", d, ch, si)
                              hkp = ("hsT", d, ch, si - 1)
                              ik = ("ini", ui)
                              s = si if d == 0 else nseg - 1 - si
                              seg = seg0 + s
                              fi = (8 + seg) if d == 0 else (16 + seg)
                              li = seg if d == 0 else seg + 1
                              V("tensor_scalar", ["s0lru", "flags"], [ik], out=ini[:, ui:ui + 1],
                                in0=s0lru[:, l, 2 * d + ch:2 * d + ch + 1], scalar1=flags[:, fi:fi + 1], scalar2=None,
                                op0=ALU.mult)
                              if si > 0:
                                  tp = (s * SEGT - 1) if d == 0 else ((s + 1) * SEGT)
                                  V("scalar_tensor_tensor", [hkp, "flags", ik], [ik], out=ini[:, ui:ui + 1],
                                    in0=hsT[:, d, ch, tp:tp + 1], scalar=flags[:, li:li + 1], in1=ini[:, ui:ui + 1],
                                    op0=ALU.mult, op1=ALU.add)
                              if d == 0:
                                  sl_ = slice(s * SEGT, (s + 1) * SEGT)
                                  o_ap, a_ap, b_ap = hsT[:, d, ch, sl_], ga[:, sl_], gb[:, sl_]
                              else:
                                  last = (s + 1) * SEGT - 1
                                  o_ap = bass.AP(hsT, (d * 2 + ch) * ntok + last, [[4 * ntok, 128], [-1, SEGT]])
                                  a_ap = bass.AP(ga, last, [[ntok, 128], [-1, SEGT]])
                                  b_ap = bass.AP(gb, last, [[ntok, 128], [-1, SEGT]])
                              t0_ = (s * SEGT) if d == 0 else ((s + 1) * SEGT - 1)
                              V("scalar_tensor_tensor", [gak, gbk, ik], [gbk], out=gb[:, t0_:t0_ + 1], in0=ga[:, t0_:t0_ + 1],
                                scalar=ini[:, ui:ui + 1], in1=gb[:, t0_:t0_ + 1], op0=ALU.mult, op1=ALU.add)
                              V("tensor_tensor_scan", [gak, gbk], [hk], out=o_ap, data0=a_ap, data1=b_ap,
                                initial=0.0, op0=ALU.mult, op1=ALU.add)
                              tp = ((s + 1) * SEGT - 1) if d == 0 else (s * SEGT)
                              lru_outs.append((hk, olru_d[seg, l, d, ch].unsqueeze(1), hsT[:, d, ch, tp:tp + 1]))
                      for (hk_, dst_, src_) in lru_outs:
                          S.dma("sp", "out", reads=[hk_], is_output=True, out=dst_, in_=src_)
                      for ch in range(2):
                          V("tensor_tensor", [("hsT", dd, ch, si_) for dd in range(2) for si_ in range(nseg)], [f"gb{ch}"], out=gbL[ch][:], in0=hsT[:, 0, ch, :],
                            in1=hsT[:, 1, ch, :], op=ALU.add)
                          V("tensor_tensor", [f"gb{ch}", ("glT", ch)], [("ycT", 6 + ch)], out=ycT[:, 6 + ch, :], in0=gbL[ch][:],
                            in1=glT[:, ch, :], op=ALU.mult)
                      stage("lru")
                      if l == 0 and seg0 == 0:
                          for ch in range(2):
                              dump(f"ycT_lru{ch}", ycT[:, 6 + ch, 0:512], [("ycT", 6 + ch)])
                      S.barrier()
                      AR.release(mlru)

                  for fc in range(8):
                      wv, wk = load_piece(l, ("out", fc))
                      for tb in range(ntb):
                          bk = 1 + (fc * ntb + tb) % 5
                          for kc in range(8):
                              PE([wk, ("ycT", kc)], [P(bk)], out=ps[bk][:, :], lhsT=wv[:, kc, :],
                                 rhs=ycT[:, kc, tb * 512:(tb + 1) * 512], start=(kc == 0), stop=(kc == 7))
                          for s2 in range(2):
                              s = seg0 + 2 * tb + s2
                              ts_ = slice(s * SEGT, (s + 1) * SEGT)
                              V("scalar_tensor_tensor", [P(bk), mk, ("xT", fc)], [("xT", fc)], out=xT[:, fc, ts_],
                                in0=ps[bk][:, s2 * SEGT:(s2 + 1) * SEGT], scalar=modT[:, 16 + fc, s:s + 1],
                                in1=xT[:, fc, ts_], op0=ALU.mult, op1=ALU.add)
                  stage("outproj")
                  S.barrier()
                  AR.release(mg)

                  if "noffn" not in FEAT:
                      mf_ = AR.mark()
                      hT = AR.alloc("h2T", [128, 8, ntok], BF16)
                      actT = AR.alloc("actT", [128, 22, ntok], BF16)
                      mn = AR.mark()
                      tmpa = (AR.alloc("sq", [128, 2, 8, 512], BF16), AR.alloc("rstd", [128, 2, 512], F32),
                              AR.alloc("tmpx", [128, 2, 512], F32))
                      prefetch(l, ("upv", 0)); prefetch(l, ("upg", 0))
                      norm_mod(hT, seg0, nseg, A2, ("A2", l % 2), 24, tmpa, modT, mk)
                      S.barrier()
                      AR.release(mn)
                      upad = [AR.alloc(f"upad{i}", [128, nseg, 258], F32) for i in range(4)]
                      uacc = [AR.alloc(f"uacc{i}", [128, ntok], F32) for i in range(4)]
                      for i in range(4):
                          V("memset", [], [("upad", i)], ap=upad[i][:], constant=0.0)
                      lk = flags[:, seg0 + 1:seg0 + nseg]
                      upcnt = {"i": 0}
                      hide_mod = (seg0 == GROUPS[-1][0]) and (l + 1 < L)

                      def up_chunk(wv, wk, j, fcg, bi, l=l, hT=hT, upad=upad, uacc=uacc, lk=lk, nseg=nseg, ntb=ntb, upcnt=upcnt, hide_mod=hide_mod):
                          xp, ca = upad[bi], uacc[bi]
                          for tb in range(ntb):
                              bk = (1 + upcnt["i"] % 6) if hide_mod else (upcnt["i"] % 7)
                              upcnt["i"] += 1
                              for kc in range(8):
                                  PE(hkeys + [wk], [P(bk)], out=ps[bk][:, :], lhsT=wv[:, kc, j * 128:(j + 1) * 128],
                                     rhs=hT[:, kc, tb * 512:(tb + 1) * 512], start=(kc == 0), stop=(kc == 7))
                              act(xp[:, 2 * tb:2 * tb + 2, 1:257], ps[bk][:, :].rearrange("p (a b) -> p a b", a=2),
                                  AF.Copy, [P(bk)], [("upad", bi)])
                          V("tensor_tensor", [("upad", bi), "flags"], [("upad", bi)], out=xp[:, 1:nseg, 0:1],
                            in0=xp[:, 0:nseg - 1, 256:257], in1=lk.unsqueeze(2), op=ALU.mult)
                          V("tensor_tensor", [("upad", bi), "flags"], [("upad", bi)], out=xp[:, 0:nseg - 1, 257:258],
                            in0=xp[:, 1:nseg, 1:2], in1=lk.unsqueeze(2), op=ALU.mult)
                          ca3 = ca[:].rearrange("p (a b) -> p a b", a=nseg)
                          wc = PK_FCW + fcg * 3
                          act(ca3, xp[:, :, 0:256], AF.Identity, [("upad", bi), "pk"], [("uacc", bi)],
                              scale=pk[:, l, wc:wc + 1], bias=pk[:, l, PK_FCB + fcg:PK_FCB + fcg + 1])
                          for tap in range(1, 3):
                              V("scalar_tensor_tensor", [("upad", bi), "pk", ("uacc", bi)], [("uacc", bi)], out=ca3,
                                in0=xp[:, :, tap:tap + 256], scalar=pk[:, l, wc + tap:wc + tap + 1], in1=ca3,
                                op0=ALU.mult, op1=ALU.add)

                      def up_tail(fcv, bv, bg, uacc=uacc, actT=actT):
                          act(uacc[bg][:], uacc[bg][:], AF.Silu, [("uacc", bg)], [("uacc", bg)])
                          V("tensor_tensor", [("uacc", bg), ("uacc", bv)], [("actT", fcv)], out=actT[:, fcv, :],
                            in0=uacc[bg][:], in1=uacc[bv][:], op=ALU.mult)

                      pend_u = None
                      for pi in range(6):
                          nfc = 4 if pi < 5 else 2
                          wvv, wkv = load_piece(l, ("upv", pi))
                          wvg, wkg = load_piece(l, ("upg", pi))
                          for j in range(nfc):
                              fcv = pi * 4 + j
                              bv, bg = (fcv % 2) * 2, (fcv % 2) * 2 + 1
                              up_chunk(wvv, wkv, j, fcv, bv)
                              up_chunk(wvg, wkg, j, 22 + fcv, bg)
                              if pend_u is not None:
                                  up_tail(*pend_u)
                              pend_u = (fcv, bv, bg)
                          if hide_mod:
                              mod_piece(l + 1, 2 * pi)
                              mod_piece(l + 1, 2 * pi + 1)
                      up_tail(*pend_u)
                      stage("ffnup")
                      for fc in range(8):
                          wv, wk = load_piece(l, ("dn", fc))

                          for tb in range(ntb):
                              bk = (1 + (fc * ntb + tb) % 6) if hide_mod else ((fc * ntb + tb) % 7)
                              for kc in range(22):
                                  PE([wk, ("actT", kc)], [P(bk)], out=ps[bk][:, :], lhsT=wv[:, kc, :],
                                     rhs=actT[:, kc, tb * 512:(tb + 1) * 512], start=(kc == 0), stop=(kc == 21))
                              for s2 in range(2):
                                  s = seg0 + 2 * tb + s2
                                  ts_ = slice(s * SEGT, (s + 1) * SEGT)
                                  V("scalar_tensor_tensor", [P(bk), mk, ("xT", fc)], [("xT", fc)], out=xT[:, fc, ts_],
                                    in0=ps[bk][:, s2 * SEGT:(s2 + 1) * SEGT], scalar=modT[:, 40 + fc, s:s + 1],
                                    in1=xT[:, fc, ts_], op0=ALU.mult, op1=ALU.add)
                      stage("ffn")
                      S.barrier()
                      AR.release(mf_)

        except StopBuild:
            AR.release(m_persist)

        S.barrier()
        mf = AR.mark()
        sq = AR.alloc("sq", [128, 2, 8, 512], BF16)
        rstd = AR.alloc("rstd", [128, 2, 512], F32)
        yo = AR.alloc("yo", [128, 4, 512], F32)
        norm_stats(0, 0, sq, rstd)
        for tb in range(3):
            t0 = tb * 512
            j = tb % 2
            if tb + 1 < 3:
                norm_stats(t0 + 512, (tb + 1) % 2, sq, rstd)
            for kc in range(8):
                V("scalar_tensor_tensor", [("xT", kc), ("rstd", j), "pkf"], [("yo", kc % 4)], out=yo[:, kc % 4, :],
                  in0=xT[:, kc, t0:t0 + 512], scalar=pkf[:, kc:kc + 1], in1=rstd[:, j, :], op0=ALU.mult, op1=ALU.mult)
                S.dma("sp", "out", reads=[("yo", kc % 4)], is_output=True,
                      out=yT_d[kc * 128:(kc + 1) * 128, t0:t0 + 512], in_=yo[:, kc % 4, :])
        S.finish("sp")
        S.emit()
    return nc


def core_segments(core):
    if core < 4:
        return [("s", core, i) for i in range(4)] + [("p", 2 * core, 0), ("p", 2 * core + 1, 0)]
    b0 = 8 + 6 * (core - 4)
    return [("p", b0 + i, 0) for i in range(6)]


def _rep(v):
    return np.broadcast_to(np.asarray(v, np.float32).reshape(1, -1), (128, np.asarray(v).size))


def _fm(v, nchunk):
    return np.asarray(v, np.float32).reshape(nchunk, 128).T


def make_consts():
    j = np.arange(128)
    tri_f = (j[:, None] <= j[None, :]).astype(np.float32)
    tri_b = (j[:, None] >= j[None, :]).astype(np.float32)
    strict_f = (j[:, None] > j[None, :]).astype(np.float32)
    strict_b = (j[:, None] < j[None, :]).astype(np.float32)
    ones = np.ones((128, 128), np.float32)
    ident = np.eye(128, dtype=np.float32)
    return np.stack([tri_f, tri_b, strict_f, strict_b, ones, ident])


def make_rope_tables():
    half = 16
    freqs = (np.float32(10000.0) ** (-np.arange(half, dtype=np.float32) / np.float32(half))).astype(np.float32)
    rows = np.repeat(np.arange(16), 64).astype(np.float32)
    cols = np.tile(np.arange(64), 16).astype(np.float32)
    cos = np.zeros((1024, 2, 2, 16), np.float32)
    sin = np.zeros((1024, 2, 2, 16), np.float32)
    for rc, pos in enumerate((rows, cols)):
        ang = (pos[:, None] * freqs[None, :]).astype(np.float32)
        c, s = np.cos(ang).astype(np.float32), np.sin(ang).astype(np.float32)
        cos[:, rc, 0], cos[:, rc, 1] = c, c
        sin[:, rc, 0], sin[:, rc, 1] = -s, s
    return cos.reshape(1024, 64), sin.reshape(1024, 64)


def prep_inputs(inp):
    f32 = lambda a: np.ascontiguousarray(np.asarray(a, np.float32))
    shared = {"cmat": make_consts()}
    offs, wtot = piece_offsets()
    wpk = np.empty((L, 128, wtot), np.float32)
    for l in range(L):
        for (name, srcn, col0, ncols, K) in piece_list():
            o, nk, _ = offs[name]
            blk = np.asarray(inp[srcn][l], np.float32)[:, col0:col0 + ncols]
            wpk[l, :, o:o + nk * ncols] = blk.reshape(nk, 128, ncols).transpose(1, 0, 2).reshape(128, nk * ncols)
    shared["wpk"] = wpk
    pk = np.zeros((L, 128, NPK), np.float32)
    lruw = np.zeros((L, 8, 128, 128), np.float32)
    for l in range(L):
        pk[l, :, PK_N1:PK_N1 + 8] = _fm(inp["norm1_g"][l], 8)
        pk[l, :, PK_N2:PK_N2 + 8] = _fm(inp["norm2_g"][l], 8)
        pk[l, :, PK_BADA:PK_BADA + 48] = _fm(inp["b_ada"][l], 48)
        pk[l, :, PK_RNG:PK_RNG + 3] = _fm(inp["ret_norm_g"][l], 3)
        pk[l, :, PK_SNG:PK_SNG + 3] = _fm(inp["ssd_norm_g"][l], 3)
        cw = np.asarray(inp["ssd_conv_w"][l], np.float32)
        for fc in range(7):
            pk[l, :, PK_SCW + fc * 4:PK_SCW + fc * 4 + 4] = cw[:, fc * 128:(fc + 1) * 128].T
        pk[l, :, PK_SCB:PK_SCB + 7] = _fm(inp["ssd_conv_b"][l], 7)
        lw = np.asarray(inp["lru_conv_w"][l], np.float32)
        for fc in range(2):
            pk[l, :, PK_LCW + fc * 4:PK_LCW + fc * 4 + 4] = lw[:, fc * 128:(fc + 1) * 128].T
        pk[l, :, PK_LCB:PK_LCB + 2] = _fm(inp["lru_conv_b"][l], 2)
        for d in range(2):
            pk[l, :, PK_LBA + 2 * d:PK_LBA + 2 * d + 2] = _fm(inp["lru_b_a"][l, d], 2)
            pk[l, :, PK_LBX + 2 * d:PK_LBX + 2 * d + 2] = _fm(inp["lru_b_x"][l, d], 2)
            pk[l, :, PK_LLAM + 2 * d:PK_LLAM + 2 * d + 2] = _fm(inp["lru_lambda"][l, d], 2)
        fw = np.asarray(inp["ffn_conv_w"][l], np.float32)
        for fc in range(44):
            pk[l, :, PK_FCW + fc * 3:PK_FCW + fc * 3 + 3] = fw[:, fc * 128:(fc + 1) * 128].T
        pk[l, :, PK_FCB:PK_FCB + 44] = _fm(inp["ffn_conv_b"][l], 44)
        pk[l, :, PK_RDEC:PK_RDEC + 12] = _rep(np.asarray(inp["ret_decay"][l]).reshape(-1))
        pk[l, :, PK_DTB:PK_DTB + 12] = _rep(np.asarray(inp["ssd_dt_bias"][l]).reshape(-1))
        pk[l, :, PK_ALOG:PK_ALOG + 12] = _rep(np.asarray(inp["ssd_a_log"][l]).reshape(-1))
        pk[l, :, PK_SSDD:PK_SSDD + 384] = _rep(np.repeat(np.asarray(inp["ssd_d"][l], np.float32), 64))
        for d in range(2):
            for gi, nm in enumerate(("lru_w_a", "lru_w_x")):
                wblk = np.asarray(inp[nm][l, d], np.float32)
                for ch in range(2):
                    m = lruw[l, d * 4 + gi * 2 + ch]
                    for b2 in range(2):
                        m[b2 * 64:(b2 + 1) * 64, b2 * 64:(b2 + 1) * 64] = wblk[ch * 2 + b2]
    shared["pk"] = pk
    shared["lruw"] = lruw
    shared["pkf"] = np.ascontiguousarray(_fm(inp["final_norm_g"], 8))
    cos_t, sin_t = make_rope_tables()
    xp, xs = np.asarray(inp["x_prompt"], np.float32), np.asarray(inp["x_sample"], np.float32)
    c, c_ctx = np.asarray(inp["c"], np.float32), np.asarray(inp["c_ctx"], np.float32)
    per_core = []
    for core in range(8):
        segs = core_segments(core)
        x = np.zeros((T, D), np.float32)
        cT = np.zeros((D, NSEG), np.float32)
        rope = np.zeros((4, T, 64), np.float32)
        rope[0] = 0.125
        rope[2] = 1.0
        flags = np.zeros((128, 32), np.float32)
        for si, (kind, b, part) in enumerate(segs):
            sl = slice(si * SEGT, (si + 1) * SEGT)
            if kind == "s":
                x[sl] = xs[b, part * SEGT:(part + 1) * SEGT]
                cT[:, si] = c[b]
                pos = slice(part * SEGT, (part + 1) * SEGT)
                rope[0, sl], rope[1, sl] = cos_t[pos] * np.float32(0.125), sin_t[pos] * np.float32(0.125)
                rope[2, sl], rope[3, sl] = cos_t[pos], sin_t[pos]
                if part > 0:
                    flags[:, si] = 1.0
                if part == 0:
                    flags[:, 8 + si] = 1.0
                if part == 3:
                    flags[:, 16 + si] = 1.0
            else:
                x[sl] = xp[b]
                cT[:, si] = c_ctx
        s0_ret = np.zeros((L, 2, 128, 192), np.float32)
        s0_ssd = np.zeros((L, 2, 128, 384), np.float32)
        s0_lru = np.zeros((L, 128, 4), np.float32)
        if core < 4:
            sr = np.asarray(inp["state_ret"][core], np.float32)
            ss = np.asarray(inp["state_ssd"][core], np.float32)
            slr = np.asarray(inp["state_lru"][core], np.float32)
            s0_ret[:] = sr.reshape(L, 2, 3, 2, 64, 64).transpose(0, 1, 3, 4, 2, 5).reshape(L, 2, 128, 192)
            s0_ssd[:] = ss.transpose(0, 1, 3, 2, 4).reshape(L, 2, 128, 384)
            s0_lru[:] = slr.reshape(L, 2, 2, 128).transpose(0, 3, 1, 2).reshape(L, 128, 4)
        m = dict(shared)
        m.update({"xT": np.ascontiguousarray(x.T), "cT": cT, "flags": flags, "s0_ret": s0_ret,
                  "s0_ssd": s0_ssd, "s0_lru": s0_lru, "rope": rope})
        per_core.append(m)
    return per_core


_PROG = {}


def kernel(**inputs):
    per_core = prep_inputs(inputs)
    if "nc" not in _PROG:
        _PROG["nc"] = build_program()
    res = run_bass_kernel_spmd(_PROG["nc"], per_core, core_ids=list(range(8)))
    B, SQ = 32, 256
    y_prompt = np.zeros((B, SQ, D), np.float32)
    y_sample = np.zeros((4, 1024, D), np.float32)
    n_ret = np.zeros((B, L, 2, 6, 64, 64), np.float32)
    n_ssd = np.zeros((B, L, 2, 6, 128, 64), np.float32)
    n_lru = np.zeros((B, L, 2, 256), np.float32)
    for core in range(8):
        r = res.results[core]
        y = np.asarray(r["yT"]).T
        o_ret, o_ssd, o_lru = np.asarray(r["o_ret"]), np.asarray(r["o_ssd"]), np.asarray(r["o_lru"])
        for si, (kind, b, part) in enumerate(core_segments(core)):
            sl = slice(si * SEGT, (si + 1) * SEGT)
            if kind == "s":
                y_sample[b, part * SEGT:(part + 1) * SEGT] = y[sl]
            else:
                y_prompt[b] = y[sl]
                n_ret[b] = o_ret[si].reshape(L, 2, 2, 64, 3, 64).transpose(0, 1, 4, 2, 3, 5).reshape(L, 2, 6, 64, 64)
                n_ssd[b] = o_ssd[si].reshape(L, 2, 128, 6, 64).transpose(0, 1, 3, 2, 4)
                n_lru[b] = o_lru[si].reshape(L, 2, 256)
    return (y_prompt, y_sample, n_ret, n_ssd, n_lru)
```

```python
import numpy as np
from contextlib import ExitStack
import concourse.bass as bass
import concourse.mybir as mybir
from concourse.bass_utils import run_bass_kernel_spmd

F32 = mybir.dt.float32
BF16 = mybir.dt.bfloat16
AF = mybir.ActivationFunctionType
ALU = mybir.AluOpType
AX = mybir.AxisListType

D = 1024
L = 2
NSEG = 6
SEGT = 256
T = NSEG * SEGT
NT = T // 128
W_RET = 384
W_SSD = 384
N_SSD = 128
CONV_CH = 896
W_LRU = 256
D_FF = 2816
IN_DIM = 3334
EPS = 1e-6
GROUPS = ((0, 4), (4, 2))

PK_N1, PK_N2, PK_BADA, PK_RNG, PK_SNG = 0, 8, 16, 64, 67
PK_SCW, PK_SCB, PK_LCW, PK_LCB, PK_LBA, PK_LBX, PK_LLAM = 70, 98, 105, 113, 115, 119, 123
PK_FCW, PK_FCB, PK_RDEC, PK_DTB, PK_ALOG, PK_SSDD = 128, 260, 304, 316, 328, 340
NPK = 724


def piece_list():
    pl = []
    for i in range(12):
        pl.append((("ada", i), "w_ada", i * 512, 512, D))
    for pi, nm in enumerate(("q", "k", "v", "g")):
        pl.append(((nm,), "w_in", pi * 384, 384, D))
    pl.append((("z",), "w_in", 1536, 384, D))
    pl.append((("dt",), "w_in", 2816, 6, D))
    pl.append((("xbc", 0), "w_in", 1920, 512, D))
    pl.append((("xbc", 1), "w_in", 2432, 384, D))
    pl.append((("lru",), "w_in", 2822, 512, D))
    for fc in range(8):
        pl.append((("out", fc), "w_out", fc * 128, 128, D))
    for pi in range(6):
        nfc = 4 if pi < 5 else 2
        pl.append((("upv", pi), "ffn_w_up", pi * 512, nfc * 128, D))
        pl.append((("upg", pi), "ffn_w_up", D_FF + pi * 512, nfc * 128, D))
    for fc in range(8):
        pl.append((("dn", fc), "ffn_w_down", fc * 128, 128, D_FF))
    return pl


def piece_offsets():
    offs, o = {}, 0
    for (name, srcn, col0, ncols, K) in piece_list():
        offs[name] = (o, K // 128, ncols)
        o += (K // 128) * ncols
    return offs, o


POST_DELAY = 3
EPOCH = 3000
N_DMA_SEMS = 28


class Sched:
    ENGS = ("pe", "act", "dve", "pool", "sp")

    def __init__(self, nc, stack):
        self.nc = nc
        self.stack = stack
        self.streams = {e: [] for e in self.ENGS}
        self.count = {e: 0 for e in self.ENGS}
        self.dma_sems = []
        self.dma_cnt = []
        self.dma_group = {}
        self.last_write = {}
        self.readers = {}
        self.seen = {e: {} for e in self.ENGS}
        self.out_events = []
        self.act_dma_events = []
        self.waited = {e: set() for e in self.ENGS}

    def _need(self, eng, ev, force=False):
        if ev[0] == "eng":
            _, src, n = ev
            if src == eng and not force:
                if src in ("pe", "sp"):
                    return None
            if self.seen[eng].get(("eng", src), 0) >= n:
                return None
            self.seen[eng][("eng", src)] = n
            self.waited[src].add(n)
            return ev
        _, idx, val = ev
        val = self.dma_cnt[idx]
        if self.seen[eng].get(("dma", idx), 0) >= val:
            return None
        self.seen[eng][("dma", idx)] = val
        return ("dma", idx, val)

    def _deps(self, eng, reads, writes):
        evs = []
        for r in reads:
            if r in self.last_write:
                evs.append(self.last_write[r])
        for w in writes:
            if w in self.last_write:
                evs.append(self.last_write[w])
            evs.extend(self.readers.get(w, []))
        waits = []
        for ev in evs:
            nd = self._need(eng, ev)
            if nd is not None:
                waits.append(nd)
        return waits

    def _record(self, ev, reads, writes):
        for r in reads:
            self.readers.setdefault(r, []).append(ev)
        for w in writes:
            self.last_write[w] = ev
            self.readers[w] = []

    def op(self, eng, name, reads=(), writes=(), **kw):
        reads, writes = list(reads), list(writes)
        waits = self._deps(eng, reads, writes)
        self.count[eng] += 1
        ev = ("eng", eng, self.count[eng])
        self.streams[eng].append((waits, name, kw, ev))
        self._record(ev, reads, writes)
        return ev

    def dma(self, eng, group, reads=(), writes=(), is_output=False, track=True, **kw):
        reads, writes = list(reads), list(writes)
        waits = self._deps(eng, reads, writes)
        gk = (eng, group)
        if gk not in self.dma_group:
            self.dma_group[gk] = len(self.dma_sems)
            self.dma_sems.append(self.stack.enter_context(self.nc.semaphore(f"dq_{eng}_{group}")))
            self.dma_cnt.append(0)
        idx = self.dma_group[gk]
        self.dma_cnt[idx] += 16
        ev = ("dma", idx, self.dma_cnt[idx])
        self.streams[eng].append((waits, "dma_start", kw, ev))
        self._record(ev, reads, writes)
        if is_output:
            self.out_events.append(ev)
        if track:
            self.act_dma_events.append(ev)
        return ev

    def barrier(self):
        evs = [("eng", e, self.count[e]) for e in ("pe", "act", "dve", "pool") if self.count[e] > 0]
        evs += self.act_dma_events
        self.act_dma_events = []
        for eng in self.ENGS:
            waits = []
            for ev in evs:
                if ev[0] == "eng" and ev[1] == eng:
                    continue
                nd = self._need(eng, ev, force=True)
                if nd is not None:
                    waits.append(nd)
            if waits:
                self.streams[eng].append((waits, None, None, None))

    def finish(self, eng="sp"):
        waits = []
        for ev in self.out_events:
            nd = self._need(eng, ev, force=True)
            if nd is not None:
                waits.append(nd)
        self.streams[eng].append((waits, None, None, None))

    def emit(self):
        nc = self.nc
        streams = self.streams
        rank = {}
        sems = {}
        for e in self.ENGS:
            for i, n in enumerate(sorted(self.waited[e])):
                rank[(e, n)] = i + 1
        nep = {e: (len(self.waited[e]) + EPOCH - 1) // EPOCH for e in self.ENGS}
        for e in self.ENGS:
            for ep in range(nep[e]):
                sems[(e, ep)] = self.stack.enter_context(nc.semaphore(f"s_{e}_{ep}"))

        def sem_of(ev):
            if ev[0] == "dma":
                return self.dma_sems[ev[1]], ev[2]
            r = rank[(ev[1], ev[2])]
            ep, val = divmod(r - 1, EPOCH)
            return sems[(ev[1], ep)], val + 1

        def run(e, name):
            for (waits, iname, kw, ev) in streams[name]:
                for w in waits:
                    sem, val = sem_of(w)
                    e.wait_ge(sem, val)
                if iname is not None:
                    ins = getattr(e, iname)(**kw)
                    if ev[0] == "dma":
                        ins.then_inc(self.dma_sems[ev[1]], 16)
                    elif (ev[1], ev[2]) in rank:
                        sem, _ = sem_of(ev)
                        ins.then_inc(sem, 1)

        with nc.Block() as block:
            @block.tensor
            def _(e):
                run(e, "pe")

            @block.scalar
            def _(e):
                run(e, "act")

            @block.vector
            def _(e):
                run(e, "dve")

            @block.gpsimd
            def _(e):
                run(e, "pool")

            @block.sync
            def _(e):
                run(e, "sp")


class Arena:
    def __init__(self, nc, base=16512, limit=229344):
        self.nc = nc
        self.base = base
        self.top = base
        self.limit = limit
        self.n = 0
        self.peak = base

    def alloc(self, name, shape, dt):
        nbytes = int(np.prod(shape[1:])) * (4 if dt == F32 else 2)
        off = (self.top + 31) // 32 * 32
        assert off + nbytes <= self.limit, f"SBUF overflow allocating {name}: {off + nbytes}"
        self.top = off + nbytes
        self.peak = max(self.peak, self.top)
        self.n += 1
        return self.nc.alloc_sbuf_tensor_at(f"{name}_{self.n}", list(shape), dt, offset=off)

    def mark(self):
        return self.top

    def release(self, m):
        self.top = m


def bc(ap, shape):
    return ap.to_broadcast(list(shape))


class StopBuild(Exception):
    pass


def build_program(dbg=None, stop=None):
    dbg = dbg or []

    FEAT = []

    def stage(name):
        if stop is not None and name == stop:
            raise StopBuild()

    nc = bass.Bass("TRN2", target_bir_lowering=False)
    din = lambda name, shape: nc.dram_tensor(name, list(shape), F32, kind="ExternalInput").ap()
    dout = lambda name, shape: nc.dram_tensor(name, list(shape), F32, kind="ExternalOutput").ap()
    xT_d = din("xT", [D, T])
    cT_d = din("cT", [D, NSEG])
    flags_d = din("flags", [128, 32])
    s0ret_d = din("s0_ret", [L, 2, 128, 192])
    s0ssd_d = din("s0_ssd", [L, 2, 128, 384])
    s0lru_d = din("s0_lru", [L, 128, 4])
    rope_d = din("rope", [4, T, 64])
    pk_d = din("pk", [L, 128, NPK])
    pkf_d = din("pkf", [128, 8])
    lruw_d = din("lruw", [L, 8, 128, 128])
    cmat_d = din("cmat", [6, 128, 128])
    POFF, WTOT = piece_offsets()
    wpk_d = din("wpk", [L, 128, WTOT])
    yT_d = dout("yT", [D, T])
    oret_d = dout("o_ret", [NSEG, L, 2, 128, 192])
    ossd_d = dout("o_ssd", [NSEG, L, 2, 128, 384])
    olru_d = dout("o_lru", [NSEG, L, 2, 2, 128])
    dbg_d = {name: dout("dbg_" + name, shape) for (name, shape) in dbg}

    with ExitStack() as st:
        S = Sched(nc, st)
        AR = Arena(nc)
        ps = [st.enter_context(nc.psum_tensor(f"psb{i}", [128, 512], F32)) for i in range(7)]
        pst = st.enter_context(nc.psum_tensor("pstb", [128, 1024], BF16))
        P = lambda b: ("ps", b)

        def V(name, r, w, **kw):
            return S.op("dve", name, r, w, **kw)

        def A(name, r, w, **kw):
            return S.op("act", name, r, w, **kw)

        def G(name, r, w, **kw):
            return S.op("pool", name, r, w, **kw)

        def PE(r, w, **kw):
            return S.op("pe", "matmul", r, w, **kw)

        def act(out, in_, func, r, w, **kw):
            return S.op("act", "activation", r, w, out=out, in_=in_, func=func, **kw)

        def dump(name, ap, keys):
            if name in dbg_d:
                S.dma("pool", "dbg", reads=keys, is_output=True, out=dbg_d[name], in_=ap)

        xT = AR.alloc("xT", [128, 8, T], F32)
        pk = AR.alloc("pk", [128, L, NPK], F32)
        pkf = AR.alloc("pkf", [128, 8], F32)
        flags = AR.alloc("flags", [128, 32], F32)
        cmat = AR.alloc("cmat", [128, 6, 128], F32)
        identb = AR.alloc("identb", [128, 128], BF16)
        onesb = AR.alloc("onesb", [128, 128], BF16)
        scT = AR.alloc("scT", [128, 8, NSEG], BF16)
        modTs = [AR.alloc(f"modT{i}", [128, 48, NSEG], F32) for i in range(2)]
        A1s = [AR.alloc(f"A1{i}", [128, 8, NSEG], F32) for i in range(2)]
        A2s = [AR.alloc(f"A2{i}", [128, 8, NSEG], F32) for i in range(2)]
        lruw = AR.alloc("lruw", [128, L * 8, 128], BF16)
        lrucl = AR.alloc("lrucl", [128, L, 4], F32)
        retla = AR.alloc("retla", [128, L, 12], F32)
        ssdA = AR.alloc("ssdA", [128, L, 12], F32)
        s0lru = AR.alloc("s0lru", [128, L, 4], F32)
        epsT = AR.alloc("epsT", [128, 1], F32)
        mhalfT = AR.alloc("mhalfT", [128, 8], F32)
        NSLOT = 3
        wbuf = [AR.alloc(f"wbuf{i}", [128, 4096], BF16) for i in range(NSLOT)]
        TRI = [cmat[:, 0, :], cmat[:, 1, :]]
        STRICT = [cmat[:, 2, :], cmat[:, 3, :]]
        ONESF = cmat[:, 4, :]

        for kc in range(8):
            S.dma("sp", "setup", writes=[("xT", kc)], out=xT[:, kc, :], in_=xT_d[kc * 128:(kc + 1) * 128, :])
        for l in range(L):
            S.dma("sp", "setup", writes=["pk"], out=pk[:, l, :], in_=pk_d[l])
            S.dma("sp", "setup", writes=["s0lru"], out=s0lru[:, l, :], in_=s0lru_d[l])
            for j in range(8):
                S.dma("pool", "setup", writes=["lruw"], out=lruw[:, l * 8 + j, :], in_=lruw_d[l, j])
        S.dma("sp", "setup", writes=["pkf"], out=pkf[:], in_=pkf_d)
        S.dma("sp", "setup", writes=["flags"], out=flags[:], in_=flags_d)
        for j in range(6):
            S.dma("sp", "setup", writes=["cmat"], out=cmat[:, j, :], in_=cmat_d[j])
        S.dma("pool", "setup", writes=["identb"], out=identb[:], in_=cmat_d[5])
        S.dma("pool", "setup", writes=["onesb"], out=onesb[:], in_=cmat_d[4])
        V("memset", [], ["epsT"], ap=epsT[:], constant=EPS)
        V("memset", [], ["mhalfT"], ap=mhalfT[:], constant=-0.5)

        m0 = AR.mark()
        cTf = AR.alloc("cTf", [128, 8, NSEG], F32)
        for kc in range(8):
            S.dma("sp", "setup", writes=["cTf"], out=cTf[:, kc, :], in_=cT_d[kc * 128:(kc + 1) * 128, :])
        act(scT[:], cTf[:], AF.Silu, ["cTf"], ["scT"])
        tmp12 = AR.alloc("tmp12", [128, L, 12], F32)
        tmp4 = AR.alloc("tmp4", [128, L, 4], F32)
        act(tmp12[:], pk[:, :, PK_RDEC:PK_RDEC + 12], AF.Exp, ["pk"], ["tmp12"], scale=-1.0)
        act(tmp12[:], tmp12[:], AF.Ln, ["tmp12"], ["tmp12"], bias=1.0)
        V("tensor_scalar_mul", ["tmp12"], ["retla"], out=retla[:], in0=tmp12[:], scalar1=-1.0)
        act(ssdA[:], pk[:, :, PK_ALOG:PK_ALOG + 12], AF.Exp, ["pk"], ["ssdA"])
        V("tensor_scalar_mul", ["ssdA"], ["ssdA"], out=ssdA[:], in0=ssdA[:], scalar1=-1.0)
        act(tmp4[:], pk[:, :, PK_LLAM:PK_LLAM + 4], AF.Exp, ["pk"], ["tmp4"], scale=-1.0)
        act(tmp4[:], tmp4[:], AF.Ln, ["tmp4"], ["tmp4"], bias=1.0)
        V("tensor_scalar_mul", ["tmp4"], ["lrucl"], out=lrucl[:], in0=tmp4[:], scalar1=-8.0)
        S.barrier()
        AR.release(m0)

        wstate = {"i": 0}

        pref = {}

        def _issue_piece(l, name):
            off, nk, ncols = POFF[name]
            slot = wstate["i"] % NSLOT
            wstate["i"] += 1
            assert nk * ncols <= 4096
            buf = wbuf[slot]
            S.dma("pool", f"w{slot}", writes=[("w", slot)], track=False,
                  out=buf[:, 0:nk * ncols], in_=wpk_d[l][:, off:off + nk * ncols])
            view = bass.AP(buf, 0, [[4096, 128], [ncols, nk], [1, ncols]])
            return view, ("w", slot)

        def prefetch(l, name):
            if l < L and (l, name) not in pref:
                pref[(l, name)] = _issue_piece(l, name)

        def load_piece(l, name):
            if (l, name) in pref:
                return pref.pop((l, name))
            assert not pref, f"piece order violated: {name} requested while {list(pref)} prefetched"
            return _issue_piece(l, name)

        def mod_piece(l, piece):
            wv, wk = load_piece(l, ("ada", piece))
            for j4 in range(4):
                fc = piece * 4 + j4
                for kc in range(8):
                    PE([wk, "scT"], [P(0)], out=ps[0][:, fc * 8:fc * 8 + NSEG],
                       lhsT=wv[:, kc, j4 * 128:(j4 + 1) * 128], rhs=scT[:, kc, :],
                       start=(kc == 0), stop=(kc == 7))

        def mod_finish(l):
            modT, mk = modTs[l % 2], ("modT", l % 2)
            psv = bass.AP(ps[0], 0, [[512, 128], [8, 48], [1, NSEG]])
            V("tensor_tensor", [P(0), "pk"], [mk], out=modT[:], in0=psv,
              in1=bc(pk[:, l, PK_BADA:PK_BADA + 48].unsqueeze(2), [128, 48, NSEG]), op=ALU.add)
            for (Ax, sc0, ng, nm) in ((A1s[l % 2], 8, PK_N1, ("A1", l % 2)), (A2s[l % 2], 32, PK_N2, ("A2", l % 2))):
                V("tensor_scalar", [mk], [nm], out=Ax[:], in0=modT[:, sc0:sc0 + 8, :],
                  scalar1=1.0, scalar2=None, op0=ALU.add)
                V("tensor_tensor", [nm, "pk"], [nm], out=Ax[:], in0=Ax[:],
                  in1=bc(pk[:, l, ng:ng + 8].unsqueeze(2), [128, 8, NSEG]), op=ALU.mult)

        def norm_stats(t0, j, sq, rstd):
            bk = 1 + j
            for kc in range(8):
                act(sq[:, j, kc, :], xT[:, kc, t0:t0 + 512], AF.Square, [("xT", kc)], [("sq", j, kc)])
            for kc in range(8):
                PE([("sq", j, kc), "onesb"], [P(bk)], out=ps[bk][:, :], lhsT=onesb[:], rhs=sq[:, j, kc, :],
                   start=(kc == 0), stop=(kc == 7))
            act(rstd[:, j, :], ps[bk][:, :], AF.Sqrt, [P(bk), "epsT"], [("rstd", j)], scale=1.0 / D, bias=epsT[:])
            V("reciprocal", [("rstd", j)], [("rstd", j)], out=rstd[:, j, :], in_=rstd[:, j, :])

        def norm_mod(hT, seg0, nseg, Ax, Akey, sh0, tmpa, modT, mk):
            sq, rstd, tmpx = tmpa
            ntb_ = nseg // 2
            norm_stats(seg0 * SEGT, 0, sq, rstd)
            for tb in range(ntb_):
                t0 = (seg0 + 2 * tb) * SEGT
                j = tb % 2
                if tb + 1 < ntb_:
                    norm_stats((seg0 + 2 * tb + 2) * SEGT, (tb + 1) % 2, sq, rstd)
                for kc in range(8):
                    V("tensor_tensor", [("xT", kc), ("rstd", j)], [("tmpx", kc % 2)], out=tmpx[:, kc % 2, :],
                      in0=xT[:, kc, t0:t0 + 512], in1=rstd[:, j, :], op=ALU.mult)
                    for s2 in range(2):
                        s = seg0 + 2 * tb + s2
                        lt = (2 * tb + s2) * SEGT
                        act(hT[:, kc, lt:lt + SEGT], tmpx[:, kc % 2, s2 * SEGT:(s2 + 1) * SEGT], AF.Identity,
                            [("tmpx", kc % 2), Akey, mk], [("hT", kc)],
                            scale=Ax[:, kc, s:s + 1], bias=modT[:, sh0 + kc, s:s + 1])

        def bidir_scan(l, seg0, nseg, kind, tl, post):
            nch = nseg * 2
            Sst, Sbf, X, E, Ecum, Sm, Am, Qs, Vs, oacc, ofin, sm6, cdx, S0 = (
                tl["S"], tl["Sbf"], tl["X"], tl["E"], tl["Ecum"], tl["Sm"], tl["A"], tl["Qs"],
                tl["Vs"], tl["oacc"], tl["ofin"], tl["sm6"], tl["cdx"], tl["S0"])
            ret = kind == "ret"
            ncs = 192 if ret else 384
            visited = set()
            step_of = [0]
            Sstg = tl["Sstg"]
            stg_cnt = [0, 0]
            for d in range(2):
                V("memset", [], [("S", d)], ap=Sst[:, d, 0:ncs], constant=0.0)

            def proc(d, c):
                seg = seg0 + c // 2
                tok = slice(c * 128, (c + 1) * 128)
                seg_start = (c % 2 == 0) if d == 0 else (c % 2 == 1)
                seg_end = (c % 2 == 1) if d == 0 else (c % 2 == 0)
                xd = 0 if ret else d
                bA, bB, bC = ps[3 * d], ps[3 * d + 1], ps[3 * d + 2]
                kA, kB, kC = P(3 * d), P(3 * d + 1), P(3 * d + 2)
                if seg_start:
                    li = seg if d == 0 else seg + 1
                    fi = (8 + seg) if d == 0 else (16 + seg)
                    V("tensor_scalar", [("S", d), "flags"], [("S", d)], out=Sst[:, d, 0:ncs], in0=Sst[:, d, 0:ncs],
                      scalar1=flags[:, li:li + 1], scalar2=None, op0=ALU.mult)
                    V("scalar_tensor_tensor", [("S", d), "flags", ("S0", d)], [("S", d)], out=Sst[:, d, 0:ncs],
                      in0=S0[:, d, 0:ncs], scalar=flags[:, fi:fi + 1], in1=Sst[:, d, 0:ncs],
                      op0=ALU.mult, op1=ALU.add)
                act(Sbf[:, d, 0:ncs], Sst[:, d, 0:ncs], AF.Copy, [("S", d)], [("Sbf", d)])
                la_ap, la_key = tl["la"](d, c)
                first = (c == (0 if d == 0 else nch - 1))
                dyn = (not ret) or first
                if dyn:
                    PE([la_key, "cmat"], [kC], out=bC[:, 384:390], lhsT=STRICT[d], rhs=la_ap, start=True, stop=True)
                    PE([la_key, "cmat"], [kC], out=bC[:, 392:398], lhsT=ONESF, rhs=la_ap, start=True, stop=True)
                    act(sm6[:, d, 0:6], bC[:, 384:390], AF.Exp, [kC], [("toend", d)])
                    if ret:
                        for hp in range(2):
                            src = bass.AP(bC, hp * 64 * 512 + 392 + hp, [[512, 64], [2, 3]])
                            act(cdx[hp * 64:(hp + 1) * 64, d, 0:3], src, AF.Exp, [kC], [("cdx", d)])
                    else:
                        act(cdx[:, d, 0:6], bC[:, 392:398], AF.Exp, [kC], [("cdx", d)])
                v_ap, v_key = tl["v"](d, c)
                yield
                for half in range(2):
                    h0 = 3 * half
                    Ek, Ck = ("E", d, half), ("Ecum", d, half)
                    Ed, Cd = E[:, d, half, :], Ecum[:, d, half, :]
                    Xk, Smk, Ak, Qk = ("X", xd, half), ("Sm", d, half), ("A", d, half), ("Qs", d, half)
                    MASK = TRI[d]
                    if dyn:
                        G("tensor_tensor", ["cmat", la_key], [Xk], out=X[:, xd, half, :, :],
                          in0=bc(TRI[d].unsqueeze(1), [128, 3, 128]),
                          in1=bc(la_ap[:, h0:h0 + 3].unsqueeze(2), [128, 3, 128]), op=ALU.mult)
                        if not ret:
                            yield
                        Xf = X[:, xd, half, :, :].rearrange("p a b -> p (a b)")
                        PE([Xk, "cmat"], [kA], out=bA[:, 0:384], lhsT=STRICT[d], rhs=Xf, start=True, stop=True)
                        PE([Xk, "cmat"], [kB], out=bB[:, 0:384], lhsT=ONESF, rhs=Xf, start=True, stop=True)
                    if ret:
                        for hl in range(3):
                            h = h0 + hl
                            PE(["KT", "QT"], [kC], out=bC[:, hl * 128:(hl + 1) * 128],
                               lhsT=tl["KT"][:, h // 2, tok], rhs=tl["QT"][:, h % 2, h // 2, tok],
                               start=(hl == 0), stop=(hl == 2))
                    else:
                        PE(["KT", "QT"], [kC], out=bC[:, 0:128], lhsT=tl["KT"][:, half, tok],
                           rhs=tl["QT"][:, half, tok], start=True, stop=True)
                    yield
                    if dyn:
                        act(Ed, bA[:, 0:384], AF.Exp, [kA], [Ek])
                        act(Cd, bB[:, 0:384], AF.Exp, [kB], [Ck])
                        if ret:
                            V("tensor_tensor", [Ek, "cmat"], [Ek], out=Ed.rearrange("p (a b) -> p a b", a=3),
                              in0=Ed.rearrange("p (a b) -> p a b", a=3),
                              in1=bc(MASK.unsqueeze(1), [128, 3, 128]), op=ALU.mult)
                        yield
                    if ret:
                        V("tensor_tensor", [kC, Ek], [Ak], out=Am[:, d, half, :], in0=bC[:, 0:384], in1=Ed, op=ALU.mult)
                        for hl in range(3):
                            h = h0 + hl
                            G("tensor_tensor", ["QT", Ck], [Qk],
                              out=Qs[:, d, half, hl * 128:(hl + 1) * 128], in0=tl["QT"][:, h % 2, h // 2, tok],
                              in1=Cd[:, hl * 128:(hl + 1) * 128], op=ALU.mult)
                    else:
                        V("tensor_tensor", [kC, "cmat"], [Smk], out=Sm[:, d, half, :], in0=bC[:, 0:128], in1=MASK,
                          op=ALU.mult)
                        V("tensor_tensor", [Smk, Ek], [Ak],
                          out=Am[:, d, half, :].rearrange("p (a b) -> p a b", a=3),
                          in0=bc(Sm[:, d, half, :].unsqueeze(1), [128, 3, 128]),
                          in1=Ed.rearrange("p (a b) -> p a b", a=3), op=ALU.mult)
                        G("tensor_tensor", ["QT", Ck], [Qk],
                          out=Qs[:, d, half, :].rearrange("p (a b) -> p a b", a=3),
                          in0=bc(tl["QT"][:, half, tok].unsqueeze(1), [128, 3, 128]),
                          in1=Cd.rearrange("p (a b) -> p a b", a=3), op=ALU.mult)
                    yield
                    for hl in range(3):
                        h = h0 + hl
                        oc = slice(h * 64, (h + 1) * 64)
                        PE([Ak, v_key], [P(6)], out=ps[6][:, oc], lhsT=Am[:, d, half, hl * 128:(hl + 1) * 128],
                           rhs=v_ap[:, oc], start=(hl == 0), stop=False, skip_group_check=True)
                    for hl in range(3):
                        h = h0 + hl
                        oc = slice(h * 64, (h + 1) * 64)
                        rhs_ = Sbf[:, d, (h // 2) * 64:(h // 2) * 64 + 64] if ret else Sbf[:, d, oc]
                        PE([Qk, ("Sbf", d)], [P(6)], out=ps[6][:, oc], lhsT=Qs[:, d, half, hl * 128:(hl + 1) * 128],
                           rhs=rhs_, start=False, stop=(hl == 2), skip_group_check=True)
                    hc = slice(half * 192, (half + 1) * 192)
                    if c not in visited:
                        act(oacc[:, c, hc], ps[6][:, hc], AF.Copy, [P(6)], [("oacc", c)])
                    else:
                        V("tensor_tensor", [P(6), ("oacc", c)], [("ofin", d)], out=ofin[:, d, hc], in0=ps[6][:, hc],
                          in1=oacc[:, c, hc], op=ALU.add)
                    yield
                G("tensor_tensor", [v_key, ("toend", d)], [("Vs", d)], out=Vs[:, d, :].rearrange("p (a b) -> p a b", a=6),
                  in0=v_ap.rearrange("p (a b) -> p a b", a=6),
                  in1=bc(sm6[:, d, 0:6].unsqueeze(2), [128, 6, 64]), op=ALU.mult)
                nb = 3 if ret else 6
                V("tensor_tensor", [("S", d), ("cdx", d)], [("S", d)],
                  out=Sst[:, d, 0:ncs].rearrange("p (a b) -> p a b", a=nb),
                  in0=Sst[:, d, 0:ncs].rearrange("p (a b) -> p a b", a=nb),
                  in1=bc(cdx[:, d, 0:nb].unsqueeze(2), [128, nb, 64]), op=ALU.mult)
                yield
                if ret:
                    for fc in range(3):
                        PE(["Ktok", ("Vs", d)], [kC], out=bC[:, fc * 128:(fc + 1) * 128],
                           lhsT=tl["Ktok"][:, c, fc * 128:(fc + 1) * 128], rhs=Vs[:, d, fc * 128:(fc + 1) * 128],
                           start=(fc == 0), stop=(fc == 2), skip_group_check=True)
                    yield
                    for hp in range(2):
                        pr = slice(hp * 64, hp * 64 + 64)
                        diag = bass.AP(bC, hp * 64 * 512 + hp * 64, [[512, 64], [128, 3], [1, 64]])
                        V("tensor_tensor", [("S", d), kC], [("S", d)],
                          out=Sst[pr, d, 0:192].rearrange("p (a b) -> p a b", a=3), in0=diag,
                          in1=Sst[pr, d, 0:192].rearrange("p (a b) -> p a b", a=3), op=ALU.add)
                else:
                    for h in range(6):
                        oc = slice(h * 64, (h + 1) * 64)
                        g = h // 3
                        PE(["Ktok", ("Vs", d)], [kC], out=bC[:, oc], lhsT=tl["Ktok"][:, c, g * 128:(g + 1) * 128],
                           rhs=Vs[:, d, oc], start=(h == 0), stop=(h == 5), skip_group_check=True)
                    yield
                    V("tensor_tensor", [("S", d), kC], [("S", d)], out=Sst[:, d, 0:ncs], in0=bC[:, 0:ncs],
                      in1=Sst[:, d, 0:ncs], op=ALU.add)
                if seg_end:
                    dst = oret_d[seg, l, d] if ret else ossd_d[seg, l, d]
                    sp_ = 0
                    stg_cnt[d] += 1
                    act(Sstg[:, d, sp_, 0:ncs], Sst[:, d, 0:ncs], AF.Copy, [("S", d)], [("Sstg", d, sp_)])
                    S.dma("sp", "out", reads=[("Sstg", d, sp_)], is_output=True, out=dst, in_=Sstg[:, d, sp_, 0:ncs])
                if c in visited:
                    pending.append((c, ofin[:, d, :], ("ofin", d), step_of[0]))
                visited.add(c)

            pending = []
            for step in range(nch):
                step_of[0] = step
                gens = [proc(0, step), proc(1, nch - 1 - step)]
                alive = [True, True]
                rnd = 0
                while any(alive):
                    for i in range(2):
                        if alive[i]:
                            try:
                                next(gens[i])
                            except StopIteration:
                                alive[i] = False
                    rnd += 1
                    if rnd == POST_DELAY and pending:
                        prev = [p for p in pending if p[3] < step]
                        for (c_, o_, k_, _) in prev:
                            post(c_, o_, k_)
                        pending[:] = [p for p in pending if p[3] >= step]
            for (c_, o_, k_, _) in pending:
                post(c_, o_, k_)

        def transpose_batch(srcs, src_keys):
            key = ("ps", 7)
            for i, sap in enumerate(srcs):
                S.op("pe", "transpose", list(src_keys) + ["identb"], [key], out=pst[:, i * 128:(i + 1) * 128],
                     in_=sap, identity=identb[:])
            return pst[:, 0:len(srcs) * 128], key

        m_persist = AR.mark()
        try:
          stage("setup")
          for l in range(L):
            if l == 0:
                for piece_ in range(12):
                    mod_piece(0, piece_)
            mod_finish(l)
            modT, mk = modTs[l % 2], ("modT", l % 2)
            A1, A2 = A1s[l % 2], A2s[l % 2]
            stage("mod")
            for (seg0, nseg) in GROUPS:
                  ntok = nseg * SEGT
                  nch = nseg * 2
                  ntb = nseg // 2
                  gt0 = seg0 * SEGT
                  S.barrier()
                  mg = AR.mark()
                  hT = AR.alloc("hT", [128, 8, ntok], BF16)
                  ycT = AR.alloc("ycT", [128, 8, ntok], BF16)
                  for nm_ in ("q", "k", "v"):
                      prefetch(l, (nm_,))
                  mn = AR.mark()
                  tmpa = (AR.alloc("sq", [128, 2, 8, 512], BF16), AR.alloc("rstd", [128, 2, 512], F32),
                          AR.alloc("tmpx", [128, 2, 512], F32))
                  norm_mod(hT, seg0, nseg, A1, ("A1", l % 2), 0, tmpa, modT, mk)
                  S.barrier()
                  stage("norm1")
                  AR.release(mn)
                  if l == 0 and seg0 == 0:
                      dump("hT", hT[:, 0, 0:512], [("hT", k) for k in range(8)])

                  mret = AR.mark()
                  Ktok = AR.alloc("Ktok", [128, nch, 384], BF16)
                  Vtok = AR.alloc("Vtok", [128, nch, 384], BF16)
                  SGtok = AR.alloc("SGtok", [128, nch, 384], BF16)
                  QT = AR.alloc("QT", [128, 2, 3, ntok], BF16)
                  V("memset", [], ["QT"], ap=QT[:], constant=0.0)
                  KT = AR.alloc("KT", [128, 3, ntok], BF16)
                  oacc = AR.alloc("oacc", [128, nch, 384], F32)
                  S0r = AR.alloc("S0", [128, 2, 384], F32)
                  for d in range(2):
                      S.dma("sp", "s0", writes=[("S0", d)], out=S0r[:, d, 0:192], in_=s0ret_d[l, d])
                  mproj = AR.mark()
                  ropeT = [AR.alloc(f"rope{i}", [128, nch, 64], F32) for i in range(4)]
                  for i in range(4):
                      S.dma("sp", "rope", writes=["rope"], out=ropeT[i][:],
                            in_=rope_d[i, gt0:gt0 + ntok, :].rearrange("(c p) f -> p c f", p=128))
                  qtmp = [AR.alloc(f"qtmp{i}", [128, 384], BF16) for i in range(2)]
                  rt1 = [AR.alloc(f"rt1{i}", [128, 384], F32) for i in range(2)]
                  rt2 = [AR.alloc(f"rt2{i}", [128, 384], F32) for i in range(2)]
                  rsrc = [AR.alloc(f"rsrc{i}", [128, 384], F32) for i in range(2)]
                  hkeys = [("hT", k) for k in range(8)]
                  def rope_tail(pname, c, qt_, dkey, Ktok=Ktok, QT=QT, KT=KT):
                      sl_ = [(qt_[:, fc * 128:(fc + 1) * 128] if pname == "q" else Ktok[:, c, fc * 128:(fc + 1) * 128]) for fc in range(3)]
                      o_, k_ = transpose_batch(sl_, [dkey])
                      o3_ = o_.rearrange("p (a b) -> p a b", a=3)
                      if pname == "q":
                          for hp in range(2):
                              pr = slice(hp * 64, hp * 64 + 64)
                              act(QT[pr, hp, :, c * 128:(c + 1) * 128], o3_[pr], AF.Copy, [k_], ["QT"])
                      else:
                          act(KT[:, :, c * 128:(c + 1) * 128], o3_, AF.Copy, [k_], ["KT"])

                  pend_r = None
                  for pi, pname in enumerate(("q", "k", "v", "g")):
                      wv, wk = load_piece(l, (pname,))
                      if pname == "v" and pend_r is not None:
                          rope_tail(*pend_r)
                          pend_r = None
                      for c in range(nch):
                          bk = (pi * nch + c) % 7
                          for kc in range(8):
                              PE(hkeys + [wk], [P(bk)], out=ps[bk][:, 0:384], lhsT=hT[:, kc, c * 128:(c + 1) * 128],
                                 rhs=wv[:, kc, :], start=(kc == 0), stop=(kc == 7))
                          if False:
                              pass
                          elif pname in ("q", "k"):
                              rb = c % 2
                              qt_, rs_, r1_, r2_ = qtmp[rb], rsrc[rb], rt1[rb], rt2[rb]
                              dst = qt_[:] if pname == "q" else Ktok[:, c, :]
                              dkey = ("qtmp", rb) if pname == "q" else "Ktok"
                              ct, sn = (ropeT[0], ropeT[1]) if pname == "q" else (ropeT[2], ropeT[3])
                              act(rs_[:], ps[bk][:, 0:384], AF.Copy, [P(bk)], [("rsrc", rb)])
                              V("tensor_tensor", [("rsrc", rb), "rope"], [("rt1", rb)],
                                out=r1_[:].rearrange("p (a b) -> p a b", a=6), in0=rs_[:].rearrange("p (a b) -> p a b", a=6),
                                in1=bc(ct[:, c, :].unsqueeze(1), [128, 6, 64]), op=ALU.mult)
                              s4 = rs_[:].rearrange("p (a r f c) -> p a r f c", a=6, r=2, f=2)
                              r24 = r2_[:].rearrange("p (a r f c) -> p a r f c", a=6, r=2, f=2)
                              sn4 = sn[:, c, :].rearrange("p (r f c) -> p r f c", r=2, f=2)
                              for hf in range(2):
                                  V("tensor_tensor", [("rsrc", rb), "rope"], [("rt2", rb)], out=r24[:, :, :, hf, :],
                                    in0=s4[:, :, :, 1 - hf, :],
                                    in1=bc(sn4[:, :, hf, :].unsqueeze(1), [128, 6, 2, 16]), op=ALU.mult)
                              V("tensor_tensor", [("rt1", rb), ("rt2", rb)], [dkey], out=dst, in0=r1_[:], in1=r2_[:], op=ALU.add)
                              if pend_r is not None:
                                  rope_tail(*pend_r)
                              pend_r = (pname, c, qt_, dkey)
                          elif pname == "v":
                              act(Vtok[:, c, :], ps[bk][:, 0:384], AF.Copy, [P(bk)], ["Vtok"])
                          else:
                              act(SGtok[:, c, :], ps[bk][:, 0:384], AF.Silu, [P(bk)], ["SGtok"])
                  if l == 0 and seg0 == 0:
                      dump("retla", retla[:, 0, :], ["retla"])
                      dump("QT", QT[:, 0, 0, 0:512], ["QT"])
                      dump("Ktok", Ktok[:, 0, :], ["Ktok"])
                  stage("retproj")
                  if "nossd" not in FEAT:
                      prefetch(l, ("z",)); prefetch(l, ("dt",)); prefetch(l, ("xbc", 0))
                  S.barrier()
                  AR.release(mproj)
                  tl = dict(
                      S=AR.alloc("Sst", [128, 2, 384], F32), Sbf=AR.alloc("Sbf", [128, 2, 384], BF16),
                      X=AR.alloc("X", [128, 1, 2, 3, 128], F32), E=AR.alloc("E", [128, 2, 2, 384], F32),
                      Ecum=AR.alloc("Ecum", [128, 2, 2, 384], F32), Sm=None,
                      A=AR.alloc("Am", [128, 2, 2, 384], BF16), Qs=AR.alloc("Qs", [128, 2, 2, 384], BF16),
                      V=None, Vs=AR.alloc("Vs", [128, 2, 384], BF16), oacc=oacc,
                      ofin=AR.alloc("ofin", [128, 2, 384], F32), sm6=AR.alloc("sm6", [128, 2, 8], F32),
                      cdx=AR.alloc("cdx", [128, 2, 8], F32), S0=S0r, Sstg=AR.alloc("Sstg", [128, 2, 1, 192], F32),
                      QT=QT, KT=KT, Ktok=Ktok)
                  tl["la"] = lambda d, c, l=l: (retla[:, l, d * 6:(d + 1) * 6], "retla")
                  tl["v"] = lambda d, c: (Vtok[:, c, :], "Vtok")
                  pr_ = dict(msum=AR.alloc("msum", [128, 8], F32), cen=AR.alloc("cen", [128, 384], F32),
                             sq=AR.alloc("sqr", [128, 384], F32), ynb=AR.alloc("ynb", [128, 384], BF16))

                  def post_ret(c, ofin, ofk, l=l, pr_=pr_, SGtok=SGtok, ycT=ycT, seg0=seg0):
                      msum, cen, sqr, ynb = pr_["msum"], pr_["cen"], pr_["sq"], pr_["ynb"]
                      o3 = ofin.rearrange("p (a b) -> p a b", a=6)
                      V("tensor_reduce", [ofk], ["msum"], out=msum[:, 0:6], in_=o3, axis=AX.X, op=ALU.add)
                      V("tensor_scalar_mul", ["msum"], ["msum"], out=msum[:, 0:6], in0=msum[:, 0:6], scalar1=-1.0 / 64)
                      c3 = cen[:].rearrange("p (a b) -> p a b", a=6)
                      V("tensor_tensor", [ofk, "msum"], ["cen"], out=c3, in0=o3,
                        in1=bc(msum[:, 0:6].unsqueeze(2), [128, 6, 64]), op=ALU.add)
                      V("tensor_tensor", ["cen"], ["sqr"], out=sqr[:], in0=cen[:], in1=cen[:], op=ALU.mult)
                      V("tensor_reduce", ["sqr"], ["msum"], out=msum[:, 0:6],
                        in_=sqr[:].rearrange("p (a b) -> p a b", a=6), axis=AX.X, op=ALU.add)
                      V("tensor_scalar", ["msum"], ["msum"], out=msum[:, 0:6], in0=msum[:, 0:6], scalar1=1.0 / 64, scalar2=EPS,
                        op0=ALU.mult, op1=ALU.add)
                      G("tensor_tensor", ["msum", "mhalfT"], ["msum"], out=msum[:, 0:6], in0=msum[:, 0:6], in1=mhalfT[:, 0:6],
                        op=ALU.pow)
                      V("tensor_tensor", ["cen", "msum"], ["cen"], out=c3, in0=c3,
                        in1=bc(msum[:, 0:6].unsqueeze(2), [128, 6, 64]), op=ALU.mult)
                      V("tensor_tensor", ["cen", "SGtok"], ["ynb"], out=ynb[:], in0=cen[:], in1=SGtok[:, c, :], op=ALU.mult)
                      o_, k_ = transpose_batch([ynb[:, fc * 128:(fc + 1) * 128] for fc in range(3)], ["ynb"])
                      V("tensor_tensor", [k_, "pk"], [("ycT", fc) for fc in range(3)], out=ycT[:, 0:3, c * 128:(c + 1) * 128],
                        in0=o_.rearrange("p (a b) -> p a b", a=3),
                        in1=bc(pk[:, l, PK_RNG:PK_RNG + 3].unsqueeze(2), [128, 3, 128]), op=ALU.mult)

                  bidir_scan(l, seg0, nseg, "ret", tl, post_ret)
                  stage("retscan")
                  if l == 0 and seg0 == 0:
                      dump("ycT_ret", ycT[:, 0, 0:512], [("ycT", 0)])
                      dump("ycT_ret1", ycT[:, 1, 0:512], [("ycT", 1)])
                      dump("ycT_ret2", ycT[:, 2, 0:512], [("ycT", 2)])
                  S.barrier()
                  AR.release(mret)

                  if "nossd" in FEAT:
                      V("memset", [], [("ycT", k) for k in range(3, 6)], ap=ycT[:, 3:6, :], constant=0.0)
                  else:
                      mssd = AR.mark()
                      SZtok = AR.alloc("SZtok", [128, nch, 384], BF16)
                      BCT = AR.alloc("BCT", [128, 4, ntok], BF16)
                      XStok = AR.alloc("XStok", [128, nch, 384], BF16)
                      Btok = AR.alloc("Btok", [128, nch, 256], BF16)
                      dtT = AR.alloc("dtT", [128, nch, 12], F32)
                      laT = AR.alloc("laT", [128, nch, 12], F32)
                      oacc = AR.alloc("oacc2", [128, nch, 384], F32)
                      S0s = AR.alloc("S0", [128, 2, 384], F32)
                      for d in range(2):
                          S.dma("sp", "s0", writes=[("S0", d)], out=S0s[:, d, :], in_=s0ssd_d[l, d])
                      mproj = AR.mark()
                      NXB = 4
                      xpad = [AR.alloc(f"xpad{i}", [128, nseg, 259], F32) for i in range(NXB)]
                      cacc = [AR.alloc(f"cacc{i}", [128, ntok], F32) for i in range(NXB)]
                      xsb = [AR.alloc(f"xsb{i}", [128, ntok], BF16) for i in range(2)]
                      for i in range(NXB):
                          V("memset", [], [("xpad", i)], ap=xpad[i][:], constant=0.0)
                      wv, wk = load_piece(l, ("z",))
                      for c in range(nch):
                          bk = c % 7
                          for kc in range(8):
                              PE(hkeys + [wk], [P(bk)], out=ps[bk][:, 0:384], lhsT=hT[:, kc, c * 128:(c + 1) * 128],
                                 rhs=wv[:, kc, :], start=(kc == 0), stop=(kc == 7))
                          act(SZtok[:, c, :], ps[bk][:, 0:384], AF.Silu, [P(bk)], ["SZtok"])
                      wv, wk = load_piece(l, ("dt",))
                      for c in range(nch):
                          for kc in range(8):
                              PE(hkeys + [wk], [P(6)], out=ps[6][:, c * 8:c * 8 + 6], lhsT=hT[:, kc, c * 128:(c + 1) * 128],
                                 rhs=wv[:, kc, :], start=(kc == 0), stop=(kc == 7))
                      V("tensor_tensor", [P(6), "pk"], ["dtT"], out=dtT[:].rearrange("p c (a b) -> p c a b", a=2),
                        in0=bass.AP(ps[6], 0, [[512, 128], [8, nch], [0, 2], [1, 6]]),
                        in1=bass.AP(pk, l * NPK + PK_DTB, [[L * NPK, 128], [0, nch], [6, 2], [1, 6]]), op=ALU.add)
                      act(dtT[:], dtT[:], AF.Exp, ["dtT"], ["dtT"])
                      act(dtT[:], dtT[:], AF.Ln, ["dtT"], ["dtT"], bias=1.0)
                      V("tensor_tensor", ["dtT", "ssdA"], ["laT"], out=laT[:], in0=dtT[:],
                        in1=bass.AP(ssdA, l * 12, [[L * 12, 128], [0, nch], [1, 12]]), op=ALU.mult)
                      def xbc_tail(fc, par, ca, xsb=xsb, XStok=XStok, BCT=BCT, Btok=Btok, nch=nch):
                          if fc < 3:
                              act(xsb[par % 2][:], ca[:], AF.Silu, [("cacc", par)], [("xsb", par % 2)])
                              o_, k_ = transpose_batch([xsb[par % 2][:, c * 128:(c + 1) * 128] for c in range(nch)], [("xsb", par % 2)])
                              act(XStok[:, :, fc * 128:(fc + 1) * 128], o_.rearrange("p (a b) -> p a b", a=nch), AF.Copy,
                                  [k_], ["XStok"])
                          else:
                              bi = fc - 3
                              act(BCT[:, bi, :], ca[:], AF.Silu, [("cacc", par)], [("BCT", bi)])
                              if bi < 2:
                                  o_, k_ = transpose_batch([BCT[:, bi, c * 128:(c + 1) * 128] for c in range(nch)], [("BCT", bi)])
                                  act(Btok[:, :, bi * 128:(bi + 1) * 128], o_.rearrange("p (a b) -> p a b", a=nch), AF.Copy,
                                      [k_], ["Btok"])

                      pend_x = None
                      for pi, (col0, nfc) in enumerate(((1920, 4), (2432, 3))):
                          wv, wk = load_piece(l, ("xbc", pi))
                          for j in range(nfc):
                              fc = pi * 4 + j
                              par = fc % NXB
                              xp, ca = xpad[par], cacc[par]
                              for tb in range(ntb):
                                  bk = (fc * 2 + tb) % 6
                                  for kc in range(8):
                                      PE(hkeys + [wk], [P(bk)], out=ps[bk][:, :], lhsT=wv[:, kc, j * 128:(j + 1) * 128],
                                         rhs=hT[:, kc, tb * 512:(tb + 1) * 512], start=(kc == 0), stop=(kc == 7))
                                  act(xp[:, 2 * tb:2 * tb + 2, 1:257], ps[bk][:, :].rearrange("p (a b) -> p a b", a=2),
                                      AF.Copy, [P(bk)], [("xpad", par)])
                              lk = flags[:, seg0 + 1:seg0 + nseg]
                              V("tensor_tensor", [("xpad", par), "flags"], [("xpad", par)], out=xp[:, 1:nseg, 0:1],
                                in0=xp[:, 0:nseg - 1, 256:257], in1=lk.unsqueeze(2), op=ALU.mult)
                              V("tensor_tensor", [("xpad", par), "flags"], [("xpad", par)], out=xp[:, 0:nseg - 1, 257:259],
                                in0=xp[:, 1:nseg, 1:3], in1=bc(lk.unsqueeze(2), [128, nseg - 1, 2]), op=ALU.mult)
                              ca3 = ca[:].rearrange("p (a b) -> p a b", a=nseg)
                              wc = PK_SCW + fc * 4
                              act(ca3, xp[:, :, 0:256], AF.Identity, [("xpad", par), "pk"], [("cacc", par)],
                                  scale=pk[:, l, wc:wc + 1], bias=pk[:, l, PK_SCB + fc:PK_SCB + fc + 1])
                              for tap in range(1, 4):
                                  V("scalar_tensor_tensor", [("xpad", par), "pk", ("cacc", par)], [("cacc", par)], out=ca3,
                                    in0=xp[:, :, tap:tap + 256], scalar=pk[:, l, wc + tap:wc + tap + 1], in1=ca3,
                                    op0=ALU.mult, op1=ALU.add)
                              if pend_x is not None:
                                  xbc_tail(*pend_x)
                              pend_x = (fc, par, ca)
                      xbc_tail(*pend_x)
                      stage("ssdproj")
                      prefetch(l, ("lru",)); prefetch(l, ("out", 0)); prefetch(l, ("out", 1))
                      S.barrier()
                      AR.release(mproj)
                      tl = dict(
                          S=AR.alloc("Sst", [128, 2, 384], F32), Sbf=AR.alloc("Sbf", [128, 2, 384], BF16),
                          X=AR.alloc("X", [128, 2, 2, 3, 128], F32), E=AR.alloc("E", [128, 2, 2, 384], F32),
                          Ecum=AR.alloc("Ecum", [128, 2, 2, 384], F32), Sm=AR.alloc("Sm", [128, 2, 2, 128], F32),
                          A=AR.alloc("Am", [128, 2, 2, 384], BF16), Qs=AR.alloc("Qs", [128, 2, 2, 384], BF16),
                          V=None, Vs=AR.alloc("Vs", [128, 2, 384], BF16), oacc=oacc,
                          ofin=AR.alloc("ofin", [128, 2, 384], F32), sm6=AR.alloc("sm6", [128, 2, 8], F32),
                          cdx=AR.alloc("cdx", [128, 2, 8], F32), S0=S0s, Sstg=AR.alloc("Sstg", [128, 2, 1, 384], F32),
                          QT=BCT[:, 2:4, :], KT=BCT[:, 0:2, :], Ktok=Btok)
                      tl["la"] = lambda d, c, laT=laT: (laT[:, c, d * 6:(d + 1) * 6], "laT")
                      Vt = [AR.alloc(f"Vt{i}", [128, 384], BF16) for i in range(4)]
                      vcnt = {"i": 0}

                      def v_ssd(d, c, Vt=Vt, vcnt=vcnt, XStok=XStok, dtT=dtT):
                          i = vcnt["i"] % 4
                          vcnt["i"] += 1
                          G("tensor_tensor", ["XStok", "dtT"], [("Vt", i)], out=Vt[i][:].rearrange("p (a b) -> p a b", a=6),
                            in0=XStok[:, c, :].rearrange("p (a b) -> p a b", a=6),
                            in1=bc(dtT[:, c, d * 6:(d + 1) * 6].unsqueeze(2), [128, 6, 64]), op=ALU.mult)
                          return Vt[i][:], ("Vt", i)
                      tl["v"] = v_ssd
                      ps_ = dict(u=AR.alloc("u", [128, 384], F32), junk=AR.alloc("junk", [128, 384], F32),
                                 ss=AR.alloc("ss", [128, 2], F32), unb=AR.alloc("unb", [128, 384], BF16))

                      def post_ssd(c, ofin, ofk, l=l, ps_=ps_, SZtok=SZtok, XStok=XStok, ycT=ycT):
                          u, junk, ss, unb = ps_["u"], ps_["junk"], ps_["ss"], ps_["unb"]
                          V("tensor_tensor", ["XStok", "pk"], ["u"], out=u[:], in0=XStok[:, c, :],
                            in1=pk[:, l, PK_SSDD:PK_SSDD + 384], op=ALU.mult)
                          V("tensor_tensor", ["u", ofk], ["u"], out=u[:], in0=u[:], in1=ofin, op=ALU.add)
                          V("tensor_tensor", ["u", "SZtok"], ["u"], out=u[:], in0=u[:], in1=SZtok[:, c, :], op=ALU.mult)
                          act(junk[:], u[:], AF.Square, ["u"], ["junk", "ss"], accum_out=ss[:, 0:1])
                          V("tensor_scalar", ["ss"], ["ss"], out=ss[:, 0:1], in0=ss[:, 0:1], scalar1=1.0 / 384, scalar2=EPS,
                            op0=ALU.mult, op1=ALU.add)
                          G("tensor_tensor", ["ss", "mhalfT"], ["ss"], out=ss[:, 0:1], in0=ss[:, 0:1], in1=mhalfT[:, 0:1],
                            op=ALU.pow)
                          V("tensor_scalar", ["u", "ss"], ["unb"], out=unb[:], in0=u[:], scalar1=ss[:, 0:1], scalar2=None,
                            op0=ALU.mult)
                          o_, k_ = transpose_batch([unb[:, fc * 128:(fc + 1) * 128] for fc in range(3)], ["unb"])
                          V("tensor_tensor", [k_, "pk"], [("ycT", 3 + fc) for fc in range(3)],
                            out=ycT[:, 3:6, c * 128:(c + 1) * 128], in0=o_.rearrange("p (a b) -> p a b", a=3),
                            in1=bc(pk[:, l, PK_SNG:PK_SNG + 3].unsqueeze(2), [128, 3, 128]), op=ALU.mult)

                      bidir_scan(l, seg0, nseg, "ssd", tl, post_ssd)
                      stage("ssdscan")
                      if l == 0 and seg0 == 0:
                          for fc in range(3):
                              dump(f"ycT_ssd{fc}", ycT[:, 3 + fc, 0:512], [("ycT", 3 + fc)])
                      S.barrier()
                      AR.release(mssd)

                  if "nolru" in FEAT:
                      V("memset", [], [("ycT", k) for k in range(6, 8)], ap=ycT[:, 6:8, :], constant=0.0)
                  else:
                      mlru = AR.mark()
                      xc = AR.alloc("xc", [128, 2, ntok], F32)
                      xcb = AR.alloc("xcb", [128, 2, ntok], BF16)
                      glT = AR.alloc("glT", [128, 2, ntok], F32)
                      hsT = AR.alloc("hsT", [128, 2, 2, ntok], F32)
                      ini = AR.alloc("ini", [128, 4], F32)
                      msub = AR.mark()
                      xpad = [AR.alloc(f"xpadl{i}", [128, nseg, 259], F32) for i in range(2)]
                      ga = AR.alloc("gscr", [128, ntok], F32)
                      gak = "gscr"
                      for i in range(2):
                          V("memset", [], [("xpad", i)], ap=xpad[i][:], constant=0.0)
                      wv, wk = load_piece(l, ("lru",))
                      for j in range(4):
                          par = j % 2
                          xp = xpad[par]
                          for tb in range(ntb):
                              bk = (j * 2 + tb) % 6
                              for kc in range(8):
                                  PE(hkeys + [wk], [P(bk)], out=ps[bk][:, :], lhsT=wv[:, kc, j * 128:(j + 1) * 128],
                                     rhs=hT[:, kc, tb * 512:(tb + 1) * 512], start=(kc == 0), stop=(kc == 7))
                              if j < 2:
                                  act(xp[:, 2 * tb:2 * tb + 2, 1:257], ps[bk][:, :].rearrange("p (a b) -> p a b", a=2),
                                      AF.Copy, [P(bk)], [("xpad", par)])
                              else:
                                  act(glT[:, j - 2, tb * 512:(tb + 1) * 512], ps[bk][:, :], AF.Copy, [P(bk)], [("glT", j - 2)])
                          if j < 2:
                              lk = flags[:, seg0 + 1:seg0 + nseg]
                              V("tensor_tensor", [("xpad", par), "flags"], [("xpad", par)], out=xp[:, 1:nseg, 0:1],
                                in0=xp[:, 0:nseg - 1, 256:257], in1=lk.unsqueeze(2), op=ALU.mult)
                              V("tensor_tensor", [("xpad", par), "flags"], [("xpad", par)], out=xp[:, 0:nseg - 1, 257:259],
                                in0=xp[:, 1:nseg, 1:3], in1=bc(lk.unsqueeze(2), [128, nseg - 1, 2]), op=ALU.mult)
                              ca3 = xc[:, j, :].rearrange("p (a b) -> p a b", a=nseg)
                              wc = PK_LCW + j * 4
                              act(ca3, xp[:, :, 0:256], AF.Identity, [("xpad", par), "pk"], [("xc", j)],
                                  scale=pk[:, l, wc:wc + 1], bias=pk[:, l, PK_LCB + j:PK_LCB + j + 1])
                              for tap in range(1, 4):
                                  V("scalar_tensor_tensor", [("xpad", par), "pk", ("xc", j)], [("xc", j)], out=ca3,
                                    in0=xp[:, :, tap:tap + 256], scalar=pk[:, l, wc + tap:wc + tap + 1], in1=ca3,
                                    op0=ALU.mult, op1=ALU.add)
                              act(xcb[:, j, :], xc[:, j, :], AF.Copy, [("xc", j)], [("xcb", j)])
                          else:
                              g_ = glT[:, j - 2, :]
                              gk = ("glT", j - 2)
                              act(ga[:], g_, AF.Square, [gk], [gak])
                              V("tensor_scalar", [gak], [gak], out=ga[:], in0=ga[:], scalar1=0.044715, scalar2=1.0,
                                op0=ALU.mult, op1=ALU.add)
                              V("tensor_tensor", [gak, gk], [gak], out=ga[:], in0=ga[:], in1=g_, op=ALU.mult)
                              act(ga[:], ga[:], AF.Tanh, [gak], [gak], scale=0.7978845608028654)
                              V("scalar_tensor_tensor", [gak, gk], [gk], out=g_, in0=ga[:], scalar=1.0, in1=g_,
                                op0=ALU.add, op1=ALU.mult)
                              V("tensor_scalar_mul", [gk], [gk], out=g_, in0=g_, scalar1=0.5)
                      S.barrier()
                      AR.release(msub)
                      gaL = [AR.alloc(f"ga{i}", [128, ntok], F32) for i in range(4)]
                      gbL = [AR.alloc(f"gb{i}", [128, ntok], F32) for i in range(4)]
                      giL = [AR.alloc(f"gi{i}", [128, ntok], F32) for i in range(4)]
                      units = [(d, ch) for d in range(2) for ch in range(2)]
                      U = {u: (gaL[i], gbL[i], giL[i], f"ga{i}", f"gb{i}", f"gi{i}") for i, u in enumerate(units)}
                      bkc = 0
                      for (d, ch) in units:
                          ga, gb, gi, gak, gbk, gik = U[(d, ch)]
                          for gi_, (dstt, bcol) in enumerate(((ga, PK_LBA), (gi, PK_LBX))):
                              for tb in range(ntb):
                                  bk = bkc % 7
                                  bkc += 1
                                  PE([("xcb", ch), "lruw"], [P(bk)], out=ps[bk][:, :],
                                     lhsT=lruw[:, l * 8 + d * 4 + gi_ * 2 + ch, :], rhs=xcb[:, ch, tb * 512:(tb + 1) * 512],
                                     start=True, stop=True)
                                  act(dstt[:, tb * 512:(tb + 1) * 512], ps[bk][:, :], AF.Sigmoid, [P(bk), "pk"],
                                      [gak if gi_ == 0 else gik],
                                      bias=pk[:, l, bcol + 2 * d + ch:bcol + 2 * d + ch + 1])
                      for (d, ch) in units:
                          ga, gb, gi, gak, gbk, gik = U[(d, ch)]
                          act(ga[:], ga[:], AF.Exp, [gak, "lrucl"], [gak], scale=lrucl[:, l, 2 * d + ch:2 * d + ch + 1])
                      for (d, ch) in units:
                          ga, gb, gi, gak, gbk, gik = U[(d, ch)]
                          V("tensor_tensor", [gak], [gbk], out=gb[:], in0=ga[:], in1=ga[:], op=ALU.mult)
                          G("tensor_tensor", [gik, ("xc", ch)], [gik], out=gi[:], in0=gi[:], in1=xc[:, ch, :], op=ALU.mult)
                      for (d, ch) in units:
                          ga, gb, gi, gak, gbk, gik = U[(d, ch)]
                          act(gb[:], gb[:], AF.Sqrt, [gbk], [gbk], scale=-1.0, bias=1.0)
                      for (d, ch) in units:
                          ga, gb, gi, gak, gbk, gik = U[(d, ch)]
                          V("tensor_tensor", [gbk, gik], [gbk], out=gb[:], in0=gb[:], in1=gi[:], op=ALU.mult)
                      lru_outs = []
                      for si in range(nseg):
                          for ui, (d, ch) in enumerate(units):
                              ga, gb, gi, gak, gbk, gik = U[(d, ch)]
                              hk = ("hsT", d, ch, si)
                              hkp = ("hsT", d, ch, si - 1)
                              ik = ("ini", ui)
                              s = si if d == 0 else nseg - 1 - si
                              seg = seg0 + s
                              fi = (8 + seg) if d == 0 else (16 + seg)
                              li = seg if d == 0 else seg + 1
                              V("tensor_scalar", ["s0lru", "flags"], [ik], out=ini[:, ui:ui + 1],
                                in0=s0lru[:, l, 2 * d + ch:2 * d + ch + 1], scalar1=flags[:, fi:fi + 1], scalar2=None,
                                op0=ALU.mult)
                              if si > 0:
                                  tp = (s * SEGT - 1) if d == 0 else ((s + 1) * SEGT)
                                  V("scalar_tensor_tensor", [hkp, "flags", ik], [ik], out=ini[:, ui:ui + 1],
                                    in0=hsT[:, d, ch, tp:tp + 1], scalar=flags[:, li:li + 1], in1=ini[:, ui:ui + 1],
                                    op0=ALU.mult, op1=ALU.add)
                              if d == 0:
                                  sl_ = slice(s * SEGT, (s + 1) * SEGT)
                                  o_ap, a_ap, b_ap = hsT[:, d, ch, sl_], ga[:, sl_], gb[:, sl_]
                              else:
                                  last = (s + 1) * SEGT - 1
                                  o_ap = bass.AP(hsT, (d * 2 + ch) * ntok + last, [[4 * ntok, 128], [-1, SEGT]])
                                  a_ap = bass.AP(ga, last, [[ntok, 128], [-1, SEGT]])
                                  b_ap = bass.AP(gb, last, [[ntok, 128], [-1, SEGT]])
                              t0_ = (s * SEGT) if d == 0 else ((s + 1) * SEGT - 1)
                              V("scalar_tensor_tensor", [gak, gbk, ik], [gbk], out=gb[:, t0_:t0_ + 1], in0=ga[:, t0_:t0_ + 1],
                                scalar=ini[:, ui:ui + 1], in1=gb[:, t0_:t0_ + 1], op0=ALU.mult, op1=ALU.add)
                              V("tensor_tensor_scan", [gak, gbk], [hk], out=o_ap, data0=a_ap, data1=b_ap,
                                initial=0.0, op0=ALU.mult, op1=ALU.add)
                              tp = ((s + 1) * SEGT - 1) if d == 0 else (s * SEGT)
                              lru_outs.append((hk, olru_d[seg, l, d, ch].unsqueeze(1), hsT[:, d, ch, tp:tp + 1]))
                      for (hk_, dst_, src_) in lru_outs:
                          S.dma("sp", "out", reads=[hk_], is_output=True, out=dst_, in_=src_)
                      for ch in range(2):
                          V("tensor_tensor", [("hsT", dd, ch, si_) for dd in range(2) for si_ in range(nseg)], [f"gb{ch}"], out=gbL[ch][:], in0=hsT[:, 0, ch, :],
                            in1=hsT[:, 1, ch, :], op=ALU.add)
                          V("tensor_tensor", [f"gb{ch}", ("glT", ch)], [("ycT", 6 + ch)], out=ycT[:, 6 + ch, :], in0=gbL[ch][:],
                            in1=glT[:, ch, :], op=ALU.mult)
                      stage("lru")
                      if l == 0 and seg0 == 0:
                          for ch in range(2):
                              dump(f"ycT_lru{ch}", ycT[:, 6 + ch, 0:512], [("ycT", 6 + ch)])
                      S.barrier()
                      AR.release(mlru)

                  for fc in range(8):
                      wv, wk = load_piece(l, ("out", fc))
                      for tb in range(ntb):
                          bk = 1 + (fc * ntb + tb) % 5
                          for kc in range(8):
                              PE([wk, ("ycT", kc)], [P(bk)], out=ps[bk][:, :], lhsT=wv[:, kc, :],
                                 rhs=ycT[:, kc, tb * 512:(tb + 1) * 512], start=(kc == 0), stop=(kc == 7))
                          for s2 in range(2):
                              s = seg0 + 2 * tb + s2
                              ts_ = slice(s * SEGT, (s + 1) * SEGT)
                              V("scalar_tensor_tensor", [P(bk), mk, ("xT", fc)], [("xT", fc)], out=xT[:, fc, ts_],
                                in0=ps[bk][:, s2 * SEGT:(s2 + 1) * SEGT], scalar=modT[:, 16 + fc, s:s + 1],
                                in1=xT[:, fc, ts_], op0=ALU.mult, op1=ALU.add)
                  stage("outproj")
                  S.barrier()
                  AR.release(mg)

                  if "noffn" not in FEAT:
                      mf_ = AR.mark()
                      hT = AR.alloc("h2T", [128, 8, ntok], BF16)
                      actT = AR.alloc("actT", [128, 22, ntok], BF16)
                      mn = AR.mark()
                      tmpa = (AR.alloc("sq", [128, 2, 8, 512], BF16), AR.alloc("rstd", [128, 2, 512], F32),
                              AR.alloc("tmpx", [128, 2, 512], F32))
                      prefetch(l, ("upv", 0)); prefetch(l, ("upg", 0))
                      norm_mod(hT, seg0, nseg, A2, ("A2", l % 2), 24, tmpa, modT, mk)
                      S.barrier()
                      AR.release(mn)
                      upad = [AR.alloc(f"upad{i}", [128, nseg, 258], F32) for i in range(4)]
                      uacc = [AR.alloc(f"uacc{i}", [128, ntok], F32) for i in range(4)]
                      for i in range(4):
                          V("memset", [], [("upad", i)], ap=upad[i][:], constant=0.0)
                      lk = flags[:, seg0 + 1:seg0 + nseg]
                      upcnt = {"i": 0}
                      hide_mod = (seg0 == GROUPS[-1][0]) and (l + 1 < L)

                      def up_chunk(wv, wk, j, fcg, bi, l=l, hT=hT, upad=upad, uacc=uacc, lk=lk, nseg=nseg, ntb=ntb, upcnt=upcnt, hide_mod=hide_mod):
                          xp, ca = upad[bi], uacc[bi]
                          for tb in range(ntb):
                              bk = (1 + upcnt["i"] % 6) if hide_mod else (upcnt["i"] % 7)
                              upcnt["i"] += 1
                              for kc in range(8):
                                  PE(hkeys + [wk], [P(bk)], out=ps[bk][:, :], lhsT=wv[:, kc, j * 128:(j + 1) * 128],
                                     rhs=hT[:, kc, tb * 512:(tb + 1) * 512], start=(kc == 0), stop=(kc == 7))
                              act(xp[:, 2 * tb:2 * tb + 2, 1:257], ps[bk][:, :].rearrange("p (a b) -> p a b", a=2),
                                  AF.Copy, [P(bk)], [("upad", bi)])
                          V("tensor_tensor", [("upad", bi), "flags"], [("upad", bi)], out=xp[:, 1:nseg, 0:1],
                            in0=xp[:, 0:nseg - 1, 256:257], in1=lk.unsqueeze(2), op=ALU.mult)
                          V("tensor_tensor", [("upad", bi), "flags"], [("upad", bi)], out=xp[:, 0:nseg - 1, 257:258],
                            in0=xp[:, 1:nseg, 1:2], in1=lk.unsqueeze(2), op=ALU.mult)
                          ca3 = ca[:].rearrange("p (a b) -> p a b", a=nseg)
                          wc = PK_FCW + fcg * 3
                          act(ca3, xp[:, :, 0:256], AF.Identity, [("upad", bi), "pk"], [("uacc", bi)],
                              scale=pk[:, l, wc:wc + 1], bias=pk[:, l, PK_FCB + fcg:PK_FCB + fcg + 1])
                          for tap in range(1, 3):
                              V("scalar_tensor_tensor", [("upad", bi), "pk", ("uacc", bi)], [("uacc", bi)], out=ca3,
                                in0=xp[:, :, tap:tap + 256], scalar=pk[:, l, wc + tap:wc + tap + 1], in1=ca3,
                                op0=ALU.mult, op1=ALU.add)

                      def up_tail(fcv, bv, bg, uacc=uacc, actT=actT):
                          act(uacc[bg][:], uacc[bg][:], AF.Silu, [("uacc", bg)], [("uacc", bg)])
                          V("tensor_tensor", [("uacc", bg), ("uacc", bv)], [("actT", fcv)], out=actT[:, fcv, :],
                            in0=uacc[bg][:], in1=uacc[bv][:], op=ALU.mult)

                      pend_u = None
                      for pi in range(6):
                          nfc = 4 if pi < 5 else 2
                          wvv, wkv = load_piece(l, ("upv", pi))
                          wvg, wkg = load_piece(l, ("upg", pi))
                          for j in range(nfc):
                              fcv = pi * 4 + j
                              bv, bg = (fcv % 2) * 2, (fcv % 2) * 2 + 1
                              up_chunk(wvv, wkv, j, fcv, bv)
                              up_chunk(wvg, wkg, j, 22 + fcv, bg)
                              if pend_u is not None:
                                  up_tail(*pend_u)
                              pend_u = (fcv, bv, bg)
                          if hide_mod:
                              mod_piece(l + 1, 2 * pi)
                              mod_piece(l + 1, 2 * pi + 1)
                      up_tail(*pend_u)
                      stage("ffnup")
                      for fc in range(8):
                          wv, wk = load_piece(l, ("dn", fc))

                          for tb in range(ntb):
                              bk = (1 + (fc * ntb + tb) % 6) if hide_mod else ((fc * ntb + tb) % 7)
                              for kc in range(22):
                                  PE([wk, ("actT", kc)], [P(bk)], out=ps[bk][:, :], lhsT=wv[:, kc, :],
                                     rhs=actT[:, kc, tb * 512:(tb + 1) * 512], start=(kc == 0), stop=(kc == 21))
                              for s2 in range(2):
                                  s = seg0 + 2 * tb + s2
                                  ts_ = slice(s * SEGT, (s + 1) * SEGT)
                                  V("scalar_tensor_tensor", [P(bk), mk, ("xT", fc)], [("xT", fc)], out=xT[:, fc, ts_],
                                    in0=ps[bk][:, s2 * SEGT:(s2 + 1) * SEGT], scalar=modT[:, 40 + fc, s:s + 1],
                                    in1=xT[:, fc, ts_], op0=ALU.mult, op1=ALU.add)
                      stage("ffn")
                      S.barrier()
                      AR.release(mf_)

        except StopBuild:
            AR.release(m_persist)

        S.barrier()
        mf = AR.mark()
        sq = AR.alloc("sq", [128, 2, 8, 512], BF16)
        rstd = AR.alloc("rstd", [128, 2, 512], F32)
        yo = AR.alloc("yo", [128, 4, 512], F32)
        norm_stats(0, 0, sq, rstd)
        for tb in range(3):
            t0 = tb * 512
            j = tb % 2
            if tb + 1 < 3:
                norm_stats(t0 + 512, (tb + 1) % 2, sq, rstd)
            for kc in range(8):
                V("scalar_tensor_tensor", [("xT", kc), ("rstd", j), "pkf"], [("yo", kc % 4)], out=yo[:, kc % 4, :],
                  in0=xT[:, kc, t0:t0 + 512], scalar=pkf[:, kc:kc + 1], in1=rstd[:, j, :], op0=ALU.mult, op1=ALU.mult)
                S.dma("sp", "out", reads=[("yo", kc % 4)], is_output=True,
                      out=yT_d[kc * 128:(kc + 1) * 128, t0:t0 + 512], in_=yo[:, kc % 4, :])
        S.finish("sp")
        S.emit()
    return nc


def core_segments(core):
    if core < 4:
        return [("s", core, i) for i in range(4)] + [("p", 2 * core, 0), ("p", 2 * core + 1, 0)]
    b0 = 8 + 6 * (core - 4)
    return [("p", b0 + i, 0) for i in range(6)]


def _rep(v):
    return np.broadcast_to(np.asarray(v, np.float32).reshape(1, -1), (128, np.asarray(v).size))


def _fm(v, nchunk):
    return np.asarray(v, np.float32).reshape(nchunk, 128).T


def make_consts():
    j = np.arange(128)
    tri_f = (j[:, None] <= j[None, :]).astype(np.float32)
    tri_b = (j[:, None] >= j[None, :]).astype(np.float32)
    strict_f = (j[:, None] > j[None, :]).astype(np.float32)
    strict_b = (j[:, None] < j[None, :]).astype(np.float32)
    ones = np.ones((128, 128), np.float32)
    ident = np.eye(128, dtype=np.float32)
    return np.stack([tri_f, tri_b, strict_f, strict_b, ones, ident])


def make_rope_tables():
    half = 16
    freqs = (np.float32(10000.0) ** (-np.arange(half, dtype=np.float32) / np.float32(half))).astype(np.float32)
    rows = np.repeat(np.arange(16), 64).astype(np.float32)
    cols = np.tile(np.arange(64), 16).astype(np.float32)
    cos = np.zeros((1024, 2, 2, 16), np.float32)
    sin = np.zeros((1024, 2, 2, 16), np.float32)
    for rc, pos in enumerate((rows, cols)):
        ang = (pos[:, None] * freqs[None, :]).astype(np.float32)
        c, s = np.cos(ang).astype(np.float32), np.sin(ang).astype(np.float32)
        cos[:, rc, 0], cos[:, rc, 1] = c, c
        sin[:, rc, 0], sin[:, rc, 1] = -s, s
    return cos.reshape(1024, 64), sin.reshape(1024, 64)


def prep_inputs(inp):
    f32 = lambda a: np.ascontiguousarray(np.asarray(a, np.float32))
    shared = {"cmat": make_consts()}
    offs, wtot = piece_offsets()
    wpk = np.empty((L, 128, wtot), np.float32)
    for l in range(L):
        for (name, srcn, col0, ncols, K) in piece_list():
            o, nk, _ = offs[name]
            blk = np.asarray(inp[srcn][l], np.float32)[:, col0:col0 + ncols]
            wpk[l, :, o:o + nk * ncols] = blk.reshape(nk, 128, ncols).transpose(1, 0, 2).reshape(128, nk * ncols)
    shared["wpk"] = wpk
    pk = np.zeros((L, 128, NPK), np.float32)
    lruw = np.zeros((L, 8, 128, 128), np.float32)
    for l in range(L):
        pk[l, :, PK_N1:PK_N1 + 8] = _fm(inp["norm1_g"][l], 8)
        pk[l, :, PK_N2:PK_N2 + 8] = _fm(inp["norm2_g"][l], 8)
        pk[l, :, PK_BADA:PK_BADA + 48] = _fm(inp["b_ada"][l], 48)
        pk[l, :, PK_RNG:PK_RNG + 3] = _fm(inp["ret_norm_g"][l], 3)
        pk[l, :, PK_SNG:PK_SNG + 3] = _fm(inp["ssd_norm_g"][l], 3)
        cw = np.asarray(inp["ssd_conv_w"][l], np.float32)
        for fc in range(7):
            pk[l, :, PK_SCW + fc * 4:PK_SCW + fc * 4 + 4] = cw[:, fc * 128:(fc + 1) * 128].T
        pk[l, :, PK_SCB:PK_SCB + 7] = _fm(inp["ssd_conv_b"][l], 7)
        lw = np.asarray(inp["lru_conv_w"][l], np.float32)
        for fc in range(2):
            pk[l, :, PK_LCW + fc * 4:PK_LCW + fc * 4 + 4] = lw[:, fc * 128:(fc + 1) * 128].T
        pk[l, :, PK_LCB:PK_LCB + 2] = _fm(inp["lru_conv_b"][l], 2)
        for d in range(2):
            pk[l, :, PK_LBA + 2 * d:PK_LBA + 2 * d + 2] = _fm(inp["lru_b_a"][l, d], 2)
            pk[l, :, PK_LBX + 2 * d:PK_LBX + 2 * d + 2] = _fm(inp["lru_b_x"][l, d], 2)
            pk[l, :, PK_LLAM + 2 * d:PK_LLAM + 2 * d + 2] = _fm(inp["lru_lambda"][l, d], 2)
        fw = np.asarray(inp["ffn_conv_w"][l], np.float32)
        for fc in range(44):
            pk[l, :, PK_FCW + fc * 3:PK_FCW + fc * 3 + 3] = fw[:, fc * 128:(fc + 1) * 128].T
        pk[l, :, PK_FCB:PK_FCB + 44] = _fm(inp["ffn_conv_b"][l], 44)
        pk[l, :, PK_RDEC:PK_RDEC + 12] = _rep(np.asarray(inp["ret_decay"][l]).reshape(-1))
        pk[l, :, PK_DTB:PK_DTB + 12] = _rep(np.asarray(inp["ssd_dt_bias"][l]).reshape(-1))
        pk[l, :, PK_ALOG:PK_ALOG + 12] = _rep(np.asarray(inp["ssd_a_log"][l]).reshape(-1))
        pk[l, :, PK_SSDD:PK_SSDD + 384] = _rep(np.repeat(np.asarray(inp["ssd_d"][l], np.float32), 64))
        for d in range(2):
            for gi, nm in enumerate(("lru_w_a", "lru_w_x")):
                wblk = np.asarray(inp[nm][l, d], np.float32)
                for ch in range(2):
                    m = lruw[l, d * 4 + gi * 2 + ch]
                    for b2 in range(2):
                        m[b2 * 64:(b2 + 1) * 64, b2 * 64:(b2 + 1) * 64] = wblk[ch * 2 + b2]
    shared["pk"] = pk
    shared["lruw"] = lruw
    shared["pkf"] = np.ascontiguousarray(_fm(inp["final_norm_g"], 8))
    cos_t, sin_t = make_rope_tables()
    xp, xs = np.asarray(inp["x_prompt"], np.float32), np.asarray(inp["x_sample"], np.float32)
    c, c_ctx = np.asarray(inp["c"], np.float32), np.asarray(inp["c_ctx"], np.float32)
    per_core = []
    for core in range(8):
        segs = core_segments(core)
        x = np.zeros((T, D), np.float32)
        cT = np.zeros((D, NSEG), np.float32)
        rope = np.zeros((4, T, 64), np.float32)
        rope[0] = 0.125
        rope[2] = 1.0
        flags = np.zeros((128, 32), np.float32)
        for si, (kind, b, part) in enumerate(segs):
            sl = slice(si * SEGT, (si + 1) * SEGT)
            if kind == "s":
                x[sl] = xs[b, part * SEGT:(part + 1) * SEGT]
                cT[:, si] = c[b]
                pos = slice(part * SEGT, (part + 1) * SEGT)
                rope[0, sl], rope[1, sl] = cos_t[pos] * np.float32(0.125), sin_t[pos] * np.float32(0.125)
                rope[2, sl], rope[3, sl] = cos_t[pos], sin_t[pos]
                if part > 0:
                    flags[:, si] = 1.0
                if part == 0:
                    flags[:, 8 + si] = 1.0
                if part == 3:
                    flags[:, 16 + si] = 1.0
            else:
                x[sl] = xp[b]
                cT[:, si] = c_ctx
        s0_ret = np.zeros((L, 2, 128, 192), np.float32)
        s0_ssd = np.zeros((L, 2, 128, 384), np.float32)
        s0_lru = np.zeros((L, 128, 4), np.float32)
        if core < 4:
            sr = np.asarray(inp["state_ret"][core], np.float32)
            ss = np.asarray(inp["state_ssd"][core], np.float32)
            slr = np.asarray(inp["state_lru"][core], np.float32)
            s0_ret[:] = sr.reshape(L, 2, 3, 2, 64, 64).transpose(0, 1, 3, 4, 2, 5).reshape(L, 2, 128, 192)
            s0_ssd[:] = ss.transpose(0, 1, 3, 2, 4).reshape(L, 2, 128, 384)
            s0_lru[:] = slr.reshape(L, 2, 2, 128).transpose(0, 3, 1, 2).reshape(L, 128, 4)
        m = dict(shared)
        m.update({"xT": np.ascontiguousarray(x.T), "cT": cT, "flags": flags, "s0_ret": s0_ret,
                  "s0_ssd": s0_ssd, "s0_lru": s0_lru, "rope": rope})
        per_core.append(m)
    return per_core


_PROG = {}


def kernel(**inputs):
    per_core = prep_inputs(inputs)
    if "nc" not in _PROG:
        _PROG["nc"] = build_program()
    res = run_bass_kernel_spmd(_PROG["nc"], per_core, core_ids=list(range(8)))
    B, SQ = 32, 256
    y_prompt = np.zeros((B, SQ, D), np.float32)
    y_sample = np.zeros((4, 1024, D), np.float32)
    n_ret = np.zeros((B, L, 2, 6, 64, 64), np.float32)
    n_ssd = np.zeros((B, L, 2, 6, 128, 64), np.float32)
    n_lru = np.zeros((B, L, 2, 256), np.float32)
    for core in range(8):
        r = res.results[core]
        y = np.asarray(r["yT"]).T
        o_ret, o_ssd, o_lru = np.asarray(r["o_ret"]), np.asarray(r["o_ssd"]), np.asarray(r["o_lru"])
        for si, (kind, b, part) in enumerate(core_segments(core)):
            sl = slice(si * SEGT, (si + 1) * SEGT)
            if kind == "s":
                y_sample[b, part * SEGT:(part + 1) * SEGT] = y[sl]
            else:
                y_prompt[b] = y[sl]
                n_ret[b] = o_ret[si].reshape(L, 2, 2, 64, 3, 64).transpose(0, 1, 4, 2, 3, 5).reshape(L, 2, 6, 64, 64)
                n_ssd[b] = o_ssd[si].reshape(L, 2, 128, 6, 64).transpose(0, 1, 3, 2, 4)
                n_lru[b] = o_lru[si].reshape(L, 2, 256)
    return (y_prompt, y_sample, n_ret, n_ssd, n_lru)
```

```python
import numpy as np
from contextlib import ExitStack
import concourse.bass as bass
import concourse.mybir as mybir
from concourse.bass_utils import run_bass_kernel_spmd

F32 = mybir.dt.float32
BF16 = mybir.dt.bfloat16
AF = mybir.ActivationFunctionType
ALU = mybir.AluOpType
AX = mybir.AxisListType

D = 1024
L = 2
NSEG = 6
SEGT = 256
T = NSEG * SEGT
NT = T // 128
W_RET = 384
W_SSD = 384
N_SSD = 128
CONV_CH = 896
W_LRU = 256
D_FF = 2816
IN_DIM = 3334
EPS = 1e-6
GROUPS = ((0, 4), (4, 2))

PK_N1, PK_N2, PK_BADA, PK_RNG, PK_SNG = 0, 8, 16, 64, 67
PK_SCW, PK_SCB, PK_LCW, PK_LCB, PK_LBA, PK_LBX, PK_LLAM = 70, 98, 105, 113, 115, 119, 123
PK_FCW, PK_FCB, PK_RDEC, PK_DTB, PK_ALOG, PK_SSDD = 128, 260, 304, 316, 328, 340
NPK = 724


def piece_list():
    pl = []
    for i in range(12):
        pl.append((("ada", i), "w_ada", i * 512, 512, D))
    for pi, nm in enumerate(("q", "k", "v", "g")):
        pl.append(((nm,), "w_in", pi * 384, 384, D))
    pl.append((("z",), "w_in", 1536, 384, D))
    pl.append((("dt",), "w_in", 2816, 6, D))
    pl.append((("xbc", 0), "w_in", 1920, 512, D))
    pl.append((("xbc", 1), "w_in", 2432, 384, D))
    pl.append((("lru",), "w_in", 2822, 512, D))
    for fc in range(8):
        pl.append((("out", fc), "w_out", fc * 128, 128, D))
    for pi in range(6):
        nfc = 4 if pi < 5 else 2
        pl.append((("upv", pi), "ffn_w_up", pi * 512, nfc * 128, D))
        pl.append((("upg", pi), "ffn_w_up", D_FF + pi * 512, nfc * 128, D))
    for fc in range(8):
        pl.append((("dn", fc), "ffn_w_down", fc * 128, 128, D_FF))
    return pl


def piece_offsets():
    offs, o = {}, 0
    for (name, srcn, col0, ncols, K) in piece_list():
        offs[name] = (o, K // 128, ncols)
        o += (K // 128) * ncols
    return offs, o


POST_DELAY = 3
EPOCH = 3000
N_DMA_SEMS = 28


class Sched:
    ENGS = ("pe", "act", "dve", "pool", "sp")

    def __init__(self, nc, stack):
        self.nc = nc
        self.stack = stack
        self.streams = {e: [] for e in self.ENGS}
        self.count = {e: 0 for e in self.ENGS}
        self.dma_sems = []
        self.dma_cnt = []
        self.dma_group = {}
        self.last_write = {}
        self.readers = {}
        self.seen = {e: {} for e in self.ENGS}
        self.out_events = []
        self.act_dma_events = []
        self.waited = {e: set() for e in self.ENGS}

    def _need(self, eng, ev, force=False):
        if ev[0] == "eng":
            _, src, n = ev
            if src == eng and not force:
                if src in ("pe", "sp"):
                    return None
            if self.seen[eng].get(("eng", src), 0) >= n:
                return None
            self.seen[eng][("eng", src)] = n
            self.waited[src].add(n)
            return ev
        _, idx, val = ev
        val = self.dma_cnt[idx]
        if self.seen[eng].get(("dma", idx), 0) >= val:
            return None
        self.seen[eng][("dma", idx)] = val
        return ("dma", idx, val)

    def _deps(self, eng, reads, writes):
        evs = []
        for r in reads:
            if r in self.last_write:
                evs.append(self.last_write[r])
        for w in writes:
            if w in self.last_write:
                evs.append(self.last_write[w])
            evs.extend(self.readers.get(w, []))
        waits = []
        for ev in evs:
            nd = self._need(eng, ev)
            if nd is not None:
                waits.append(nd)
        return waits

    def _record(self, ev, reads, writes):
        for r in reads:
            self.readers.setdefault(r, []).append(ev)
        for w in writes:
            self.last_write[w] = ev
            self.readers[w] = []

    def op(self, eng, name, reads=(), writes=(), **kw):
        reads, writes = list(reads), list(writes)
        waits = self._deps(eng, reads, writes)
        self.count[eng] += 1
        ev = ("eng", eng, self.count[eng])
        self.streams[eng].append((waits, name, kw, ev))
        self._record(ev, reads, writes)
        return ev

    def dma(self, eng, group, reads=(), writes=(), is_output=False, track=True, **kw):
        reads, writes = list(reads), list(writes)
        waits = self._deps(eng, reads, writes)
        gk = (eng, group)
        if gk not in self.dma_group:
            self.dma_group[gk] = len(self.dma_sems)
            self.dma_sems.append(self.stack.enter_context(self.nc.semaphore(f"dq_{eng}_{group}")))
            self.dma_cnt.append(0)
        idx = self.dma_group[gk]
        self.dma_cnt[idx] += 16
        ev = ("dma", idx, self.dma_cnt[idx])
        self.streams[eng].append((waits, "dma_start", kw, ev))
        self._record(ev, reads, writes)
        if is_output:
            self.out_events.append(ev)
        if track:
            self.act_dma_events.append(ev)
        return ev

    def barrier(self):
        evs = [("eng", e, self.count[e]) for e in ("pe", "act", "dve", "pool") if self.count[e] > 0]
        evs += self.act_dma_events
        self.act_dma_events = []
        for eng in self.ENGS:
            waits = []
            for ev in evs:
                if ev[0] == "eng" and ev[1] == eng:
                    continue
                nd = self._need(eng, ev, force=True)
                if nd is not None:
                    waits.append(nd)
            if waits:
                self.streams[eng].append((waits, None, None, None))

    def finish(self, eng="sp"):
        waits = []
        for ev in self.out_events:
            nd = self._need(eng, ev, force=True)
            if nd is not None:
                waits.append(nd)
        self.streams[eng].append((waits, None, None, None))

    def emit(self):
        nc = self.nc
        streams = self.streams
        rank = {}
        sems = {}
        for e in self.ENGS:
            for i, n in enumerate(sorted(self.waited[e])):
                rank[(e, n)] = i + 1
        nep = {e: (len(self.waited[e]) + EPOCH - 1) // EPOCH for e in self.ENGS}
        for e in self.ENGS:
            for ep in range(nep[e]):
                sems[(e, ep)] = self.stack.enter_context(nc.semaphore(f"s_{e}_{ep}"))

        def sem_of(ev):
            if ev[0] == "dma":
                return self.dma_sems[ev[1]], ev[2]
            r = rank[(ev[1], ev[2])]
            ep, val = divmod(r - 1, EPOCH)
            return sems[(ev[1], ep)], val + 1

        def run(e, name):
            for (waits, iname, kw, ev) in streams[name]:
                for w in waits:
                    sem, val = sem_of(w)
                    e.wait_ge(sem, val)
                if iname is not None:
                    ins = getattr(e, iname)(**kw)
                    if ev[0] == "dma":
                        ins.then_inc(self.dma_sems[ev[1]], 16)
                    elif (ev[1], ev[2]) in rank:
                        sem, _ = sem_of(ev)
                        ins.then_inc(sem, 1)

        with nc.Block() as block:
            @block.tensor
            def _(e):
                run(e, "pe")

            @block.scalar
            def _(e):
                run(e, "act")

            @block.vector
            def _(e):
                run(e, "dve")

            @block.gpsimd
            def _(e):
                run(e, "pool")

            @block.sync
            def _(e):
                run(e, "sp")


class Arena:
    def __init__(self, nc, base=16512, limit=229344):
        self.nc = nc
        self.base = base
        self.top = base
        self.limit = limit
        self.n = 0
        self.peak = base

    def alloc(self, name, shape, dt):
        nbytes = int(np.prod(shape[1:])) * (4 if dt == F32 else 2)
        off = (self.top + 31) // 32 * 32
        assert off + nbytes <= self.limit, f"SBUF overflow allocating {name}: {off + nbytes}"
        self.top = off + nbytes
        self.peak = max(self.peak, self.top)
        self.n += 1
        return self.nc.alloc_sbuf_tensor_at(f"{name}_{self.n}", list(shape), dt, offset=off)

    def mark(self):
        return self.top

    def release(self, m):
        self.top = m


def bc(ap, shape):
    return ap.to_broadcast(list(shape))


class StopBuild(Exception):
    pass


def build_program(dbg=None, stop=None):
    dbg = dbg or []

    FEAT = []

    def stage(name):
        if stop is not None and name == stop:
            raise StopBuild()

    nc = bass.Bass("TRN2", target_bir_lowering=False)
    din = lambda name, shape: nc.dram_tensor(name, list(shape), F32, kind="ExternalInput").ap()
    dout = lambda name, shape: nc.dram_tensor(name, list(shape), F32, kind="ExternalOutput").ap()
    xT_d = din("xT", [D, T])
    cT_d = din("cT", [D, NSEG])
    flags_d = din("flags", [128, 32])
    s0ret_d = din("s0_ret", [L, 2, 128, 192])
    s0ssd_d = din("s0_ssd", [L, 2, 128, 384])
    s0lru_d = din("s0_lru", [L, 128, 4])
    rope_d = din("rope", [4, T, 64])
    pk_d = din("pk", [L, 128, NPK])
    pkf_d = din("pkf", [128, 8])
    lruw_d = din("lruw", [L, 8, 128, 128])
    cmat_d = din("cmat", [6, 128, 128])
    POFF, WTOT = piece_offsets()
    wpk_d = din("wpk", [L, 128, WTOT])
    yT_d = dout("yT", [D, T])
    oret_d = dout("o_ret", [NSEG, L, 2, 128, 192])
    ossd_d = dout("o_ssd", [NSEG, L, 2, 128, 384])
    olru_d = dout("o_lru", [NSEG, L, 2, 2, 128])
    dbg_d = {name: dout("dbg_" + name, shape) for (name, shape) in dbg}

    with ExitStack() as st:
        S = Sched(nc, st)
        AR = Arena(nc)
        ps = [st.enter_context(nc.psum_tensor(f"psb{i}", [128, 512], F32)) for i in range(7)]
        pst = st.enter_context(nc.psum_tensor("pstb", [128, 1024], BF16))
        P = lambda b: ("ps", b)

        def V(name, r, w, **kw):
            return S.op("dve", name, r, w, **kw)

        def A(name, r, w, **kw):
            return S.op("act", name, r, w, **kw)

        def G(name, r, w, **kw):
            return S.op("pool", name, r, w, **kw)

        def PE(r, w, **kw):
            return S.op("pe", "matmul", r, w, **kw)

        def act(out, in_, func, r, w, **kw):
            return S.op("act", "activation", r, w, out=out, in_=in_, func=func, **kw)

        def dump(name, ap, keys):
            if name in dbg_d:
                S.dma("pool", "dbg", reads=keys, is_output=True, out=dbg_d[name], in_=ap)

        xT = AR.alloc("xT", [128, 8, T], F32)
        pk = AR.alloc("pk", [128, L, NPK], F32)
        pkf = AR.alloc("pkf", [128, 8], F32)
        flags = AR.alloc("flags", [128, 32], F32)
        cmat = AR.alloc("cmat", [128, 6, 128], F32)
        identb = AR.alloc("identb", [128, 128], BF16)
        onesb = AR.alloc("onesb", [128, 128], BF16)
        scT = AR.alloc("scT", [128, 8, NSEG], BF16)
        modTs = [AR.alloc(f"modT{i}", [128, 48, NSEG], F32) for i in range(2)]
        A1s = [AR.alloc(f"A1{i}", [128, 8, NSEG], F32) for i in range(2)]
        A2s = [AR.alloc(f"A2{i}", [128, 8, NSEG], F32) for i in range(2)]
        lruw = AR.alloc("lruw", [128, L * 8, 128], BF16)
        lrucl = AR.alloc("lrucl", [128, L, 4], F32)
        retla = AR.alloc("retla", [128, L, 12], F32)
        ssdA = AR.alloc("ssdA", [128, L, 12], F32)
        s0lru = AR.alloc("s0lru", [128, L, 4], F32)
        epsT = AR.alloc("epsT", [128, 1], F32)
        mhalfT = AR.alloc("mhalfT", [128, 8], F32)
        NSLOT = 3
        wbuf = [AR.alloc(f"wbuf{i}", [128, 4096], BF16) for i in range(NSLOT)]
        TRI = [cmat[:, 0, :], cmat[:, 1, :]]
        STRICT = [cmat[:, 2, :], cmat[:, 3, :]]
        ONESF = cmat[:, 4, :]

        for kc in range(8):
            S.dma("sp", "setup", writes=[("xT", kc)], out=xT[:, kc, :], in_=xT_d[kc * 128:(kc + 1) * 128, :])
        for l in range(L):
            S.dma("sp", "setup", writes=["pk"], out=pk[:, l, :], in_=pk_d[l])
            S.dma("sp", "setup", writes=["s0lru"], out=s0lru[:, l, :], in_=s0lru_d[l])
            for j in range(8):
                S.dma("pool", "setup", writes=["lruw"], out=lruw[:, l * 8 + j, :], in_=lruw_d[l, j])
        S.dma("sp", "setup", writes=["pkf"], out=pkf[:], in_=pkf_d)
        S.dma("sp", "setup", writes=["flags"], out=flags[:], in_=flags_d)
        for j in range(6):
            S.dma("sp", "setup", writes=["cmat"], out=cmat[:, j, :], in_=cmat_d[j])
        S.dma("pool", "setup", writes=["identb"], out=identb[:], in_=cmat_d[5])
        S.dma("pool", "setup", writes=["onesb"], out=onesb[:], in_=cmat_d[4])
        V("memset", [], ["epsT"], ap=epsT[:], constant=EPS)
        V("memset", [], ["mhalfT"], ap=mhalfT[:], constant=-0.5)

        m0 = AR.mark()
        cTf = AR.alloc("cTf", [128, 8, NSEG], F32)
        for kc in range(8):
            S.dma("sp", "setup", writes=["cTf"], out=cTf[:, kc, :], in_=cT_d[kc * 128:(kc + 1) * 128, :])
        act(scT[:], cTf[:], AF.Silu, ["cTf"], ["scT"])
        tmp12 = AR.alloc("tmp12", [128, L, 12], F32)
        tmp4 = AR.alloc("tmp4", [128, L, 4], F32)
        act(tmp12[:], pk[:, :, PK_RDEC:PK_RDEC + 12], AF.Exp, ["pk"], ["tmp12"], scale=-1.0)
        act(tmp12[:], tmp12[:], AF.Ln, ["tmp12"], ["tmp12"], bias=1.0)
        V("tensor_scalar_mul", ["tmp12"], ["retla"], out=retla[:], in0=tmp12[:], scalar1=-1.0)
        act(ssdA[:], pk[:, :, PK_ALOG:PK_ALOG + 12], AF.Exp, ["pk"], ["ssdA"])
        V("tensor_scalar_mul", ["ssdA"], ["ssdA"], out=ssdA[:], in0=ssdA[:], scalar1=-1.0)
        act(tmp4[:], pk[:, :, PK_LLAM:PK_LLAM + 4], AF.Exp, ["pk"], ["tmp4"], scale=-1.0)
        act(tmp4[:], tmp4[:], AF.Ln, ["tmp4"], ["tmp4"], bias=1.0)
        V("tensor_scalar_mul", ["tmp4"], ["lrucl"], out=lrucl[:], in0=tmp4[:], scalar1=-8.0)
        S.barrier()
        AR.release(m0)

        wstate = {"i": 0}

        pref = {}

        def _issue_piece(l, name):
            off, nk, ncols = POFF[name]
            slot = wstate["i"] % NSLOT
            wstate["i"] += 1
            assert nk * ncols <= 4096
            buf = wbuf[slot]
            S.dma("pool", f"w{slot}", writes=[("w", slot)], track=False,
                  out=buf[:, 0:nk * ncols], in_=wpk_d[l][:, off:off + nk * ncols])
            view = bass.AP(buf, 0, [[4096, 128], [ncols, nk], [1, ncols]])
            return view, ("w", slot)

        def prefetch(l, name):
            if l < L and (l, name) not in pref:
                pref[(l, name)] = _issue_piece(l, name)

        def load_piece(l, name):
            if (l, name) in pref:
                return pref.pop((l, name))
            assert not pref, f"piece order violated: {name} requested while {list(pref)} prefetched"
            return _issue_piece(l, name)

        def mod_piece(l, piece):
            wv, wk = load_piece(l, ("ada", piece))
            for j4 in range(4):
                fc = piece * 4 + j4
                for kc in range(8):
                    PE([wk, "scT"], [P(0)], out=ps[0][:, fc * 8:fc * 8 + NSEG],
                       lhsT=wv[:, kc, j4 * 128:(j4 + 1) * 128], rhs=scT[:, kc, :],
                       start=(kc == 0), stop=(kc == 7))

        def mod_finish(l):
            modT, mk = modTs[l % 2], ("modT", l % 2)
            psv = bass.AP(ps[0], 0, [[512, 128], [8, 48], [1, NSEG]])
            V("tensor_tensor", [P(0), "pk"], [mk], out=modT[:], in0=psv,
              in1=bc(pk[:, l, PK_BADA:PK_BADA + 48].unsqueeze(2), [128, 48, NSEG]), op=ALU.add)
            for (Ax, sc0, ng, nm) in ((A1s[l % 2], 8, PK_N1, ("A1", l % 2)), (A2s[l % 2], 32, PK_N2, ("A2", l % 2))):
                V("tensor_scalar", [mk], [nm], out=Ax[:], in0=modT[:, sc0:sc0 + 8, :],
                  scalar1=1.0, scalar2=None, op0=ALU.add)
                V("tensor_tensor", [nm, "pk"], [nm], out=Ax[:], in0=Ax[:],
                  in1=bc(pk[:, l, ng:ng + 8].unsqueeze(2), [128, 8, NSEG]), op=ALU.mult)

        def norm_stats(t0, j, sq, rstd):
            bk = 1 + j
            for kc in range(8):
                act(sq[:, j, kc, :], xT[:, kc, t0:t0 + 512], AF.Square, [("xT", kc)], [("sq", j, kc)])
            for kc in range(8):
                PE([("sq", j, kc), "onesb"], [P(bk)], out=ps[bk][:, :], lhsT=onesb[:], rhs=sq[:, j, kc, :],
                   start=(kc == 0), stop=(kc == 7))
            act(rstd[:, j, :], ps[bk][:, :], AF.Sqrt, [P(bk), "epsT"], [("rstd", j)], scale=1.0 / D, bias=epsT[:])
            V("reciprocal", [("rstd", j)], [("rstd", j)], out=rstd[:, j, :], in_=rstd[:, j, :])

        def norm_mod(hT, seg0, nseg, Ax, Akey, sh0, tmpa, modT, mk):
            sq, rstd, tmpx = tmpa
            ntb_ = nseg // 2
            norm_stats(seg0 * SEGT, 0, sq, rstd)
            for tb in range(ntb_):
                t0 = (seg0 + 2 * tb) * SEGT
                j = tb % 2
                if tb + 1 < ntb_:
                    norm_stats((seg0 + 2 * tb + 2) * SEGT, (tb + 1) % 2, sq, rstd)
                for kc in range(8):
                    V("tensor_tensor", [("xT", kc), ("rstd", j)], [("tmpx", kc % 2)], out=tmpx[:, kc % 2, :],
                      in0=xT[:, kc, t0:t0 + 512], in1=rstd[:, j, :], op=ALU.mult)
                    for s2 in range(2):
                        s = seg0 + 2 * tb + s2
                        lt = (2 * tb + s2) * SEGT
                        act(hT[:, kc, lt:lt + SEGT], tmpx[:, kc % 2, s2 * SEGT:(s2 + 1) * SEGT], AF.Identity,
                            [("tmpx", kc % 2), Akey, mk], [("hT", kc)],
                            scale=Ax[:, kc, s:s + 1], bias=modT[:, sh0 + kc, s:s + 1])

        def bidir_scan(l, seg0, nseg, kind, tl, post):
            nch = nseg * 2
            Sst, Sbf, X, E, Ecum, Sm, Am, Qs, Vs, oacc, ofin, sm6, cdx, S0 = (
                tl["S"], tl["Sbf"], tl["X"], tl["E"], tl["Ecum"], tl["Sm"], tl["A"], tl["Qs"],
                tl["Vs"], tl["oacc"], tl["ofin"], tl["sm6"], tl["cdx"], tl["S0"])
            ret = kind == "ret"
            ncs = 192 if ret else 384
            visited = set()
            step_of = [0]
            Sstg = tl["Sstg"]
            stg_cnt = [0, 0]
            for d in range(2):
                V("memset", [], [("S", d)], ap=Sst[:, d, 0:ncs], constant=0.0)

            def proc(d, c):
                seg = seg0 + c // 2
                tok = slice(c * 128, (c + 1) * 128)
                seg_start = (c % 2 == 0) if d == 0 else (c % 2 == 1)
                seg_end = (c % 2 == 1) if d == 0 else (c % 2 == 0)
                xd = 0 if ret else d
                bA, bB, bC = ps[3 * d], ps[3 * d + 1], ps[3 * d + 2]
                kA, kB, kC = P(3 * d), P(3 * d + 1), P(3 * d + 2)
                if seg_start:
                    li = seg if d == 0 else seg + 1
                    fi = (8 + seg) if d == 0 else (16 + seg)
                    V("tensor_scalar", [("S", d), "flags"], [("S", d)], out=Sst[:, d, 0:ncs], in0=Sst[:, d, 0:ncs],
                      scalar1=flags[:, li:li + 1], scalar2=None, op0=ALU.mult)
                    V("scalar_tensor_tensor", [("S", d), "flags", ("S0", d)], [("S", d)], out=Sst[:, d, 0:ncs],
                      in0=S0[:, d, 0:ncs], scalar=flags[:, fi:fi + 1], in1=Sst[:, d, 0:ncs],
                      op0=ALU.mult, op1=ALU.add)
                act(Sbf[:, d, 0:ncs], Sst[:, d, 0:ncs], AF.Copy, [("S", d)], [("Sbf", d)])
                la_ap, la_key = tl["la"](d, c)
                first = (c == (0 if d == 0 else nch - 1))
                dyn = (not ret) or first
                if dyn:
                    PE([la_key, "cmat"], [kC], out=bC[:, 384:390], lhsT=STRICT[d], rhs=la_ap, start=True, stop=True)
                    PE([la_key, "cmat"], [kC], out=bC[:, 392:398], lhsT=ONESF, rhs=la_ap, start=True, stop=True)
                    act(sm6[:, d, 0:6], bC[:, 384:390], AF.Exp, [kC], [("toend", d)])
                    if ret:
                        for hp in range(2):
                            src = bass.AP(bC, hp * 64 * 512 + 392 + hp, [[512, 64], [2, 3]])
                            act(cdx[hp * 64:(hp + 1) * 64, d, 0:3], src, AF.Exp, [kC], [("cdx", d)])
                    else:
                        act(cdx[:, d, 0:6], bC[:, 392:398], AF.Exp, [kC], [("cdx", d)])
                v_ap, v_key = tl["v"](d, c)
                yield
                for half in range(2):
                    h0 = 3 * half
                    Ek, Ck = ("E", d, half), ("Ecum", d, half)
                    Ed, Cd = E[:, d, half, :], Ecum[:, d, half, :]
                    Xk, Smk, Ak, Qk = ("X", xd, half), ("Sm", d, half), ("A", d, half), ("Qs", d, half)
                    MASK = TRI[d]
                    if dyn:
                        G("tensor_tensor", ["cmat", la_key], [Xk], out=X[:, xd, half, :, :],
                          in0=bc(TRI[d].unsqueeze(1), [128, 3, 128]),
                          in1=bc(la_ap[:, h0:h0 + 3].unsqueeze(2), [128, 3, 128]), op=ALU.mult)
                        if not ret:
                            yield
                        Xf = X[:, xd, half, :, :].rearrange("p a b -> p (a b)")
                        PE([Xk, "cmat"], [kA], out=bA[:, 0:384], lhsT=STRICT[d], rhs=Xf, start=True, stop=True)
                        PE([Xk, "cmat"], [kB], out=bB[:, 0:384], lhsT=ONESF, rhs=Xf, start=True, stop=True)
                    if ret:
                        for hl in range(3):
                            h = h0 + hl
                            PE(["KT", "QT"], [kC], out=bC[:, hl * 128:(hl + 1) * 128],
                               lhsT=tl["KT"][:, h // 2, tok], rhs=tl["QT"][:, h % 2, h // 2, tok],
                               start=(hl == 0), stop=(hl == 2))
                    else:
                        PE(["KT", "QT"], [kC], out=bC[:, 0:128], lhsT=tl["KT"][:, half, tok],
                           rhs=tl["QT"][:, half, tok], start=True, stop=True)
                    yield
                    if dyn:
                        act(Ed, bA[:, 0:384], AF.Exp, [kA], [Ek])
                        act(Cd, bB[:, 0:384], AF.Exp, [kB], [Ck])
                        if ret:
                            V("tensor_tensor", [Ek, "cmat"], [Ek], out=Ed.rearrange("p (a b) -> p a b", a=3),
                              in0=Ed.rearrange("p (a b) -> p a b", a=3),
                              in1=bc(MASK.unsqueeze(1), [128, 3, 128]), op=ALU.mult)
                        yield
                    if ret:
                        V("tensor_tensor", [kC, Ek], [Ak], out=Am[:, d, half, :], in0=bC[:, 0:384], in1=Ed, op=ALU.mult)
                        for hl in range(3):
                            h = h0 + hl
                            G("tensor_tensor", ["QT", Ck], [Qk],
                              out=Qs[:, d, half, hl * 128:(hl + 1) * 128], in0=tl["QT"][:, h % 2, h // 2, tok],
                              in1=Cd[:, hl * 128:(hl + 1) * 128], op=ALU.mult)
                    else:
                        V("tensor_tensor", [kC, "cmat"], [Smk], out=Sm[:, d, half, :], in0=bC[:, 0:128], in1=MASK,
                          op=ALU.mult)
                        V("tensor_tensor", [Smk, Ek], [Ak],
                          out=Am[:, d, half, :].rearrange("p (a b) -> p a b", a=3),
                          in0=bc(Sm[:, d, half, :].unsqueeze(1), [128, 3, 128]),
                          in1=Ed.rearrange("p (a b) -> p a b", a=3), op=ALU.mult)
                        G("tensor_tensor", ["QT", Ck], [Qk],
                          out=Qs[:, d, half, :].rearrange("p (a b) -> p a b", a=3),
                          in0=bc(tl["QT"][:, half, tok].unsqueeze(1), [128, 3, 128]),
                          in1=Cd.rearrange("p (a b) -> p a b", a=3), op=ALU.mult)
                    yield
                    for hl in range(3):
                        h = h0 + hl
                        oc = slice(h * 64, (h + 1) * 64)
                        PE([Ak, v_key], [P(6)], out=ps[6][:, oc], lhsT=Am[:, d, half, hl * 128:(hl + 1) * 128],
                           rhs=v_ap[:, oc], start=(hl == 0), stop=False, skip_group_check=True)
                    for hl in range(3):
                        h = h0 + hl
                        oc = slice(h * 64, (h + 1) * 64)
                        rhs_ = Sbf[:, d, (h // 2) * 64:(h // 2) * 64 + 64] if ret else Sbf[:, d, oc]
                        PE([Qk, ("Sbf", d)], [P(6)], out=ps[6][:, oc], lhsT=Qs[:, d, half, hl * 128:(hl + 1) * 128],
                           rhs=rhs_, start=False, stop=(hl == 2), skip_group_check=True)
                    hc = slice(half * 192, (half + 1) * 192)
                    if c not in visited:
                        act(oacc[:, c, hc], ps[6][:, hc], AF.Copy, [P(6)], [("oacc", c)])
                    else:
                        V("tensor_tensor", [P(6), ("oacc", c)], [("ofin", d)], out=ofin[:, d, hc], in0=ps[6][:, hc],
                          in1=oacc[:, c, hc], op=ALU.add)
                    yield
                G("tensor_tensor", [v_key, ("toend", d)], [("Vs", d)], out=Vs[:, d, :].rearrange("p (a b) -> p a b", a=6),
                  in0=v_ap.rearrange("p (a b) -> p a b", a=6),
                  in1=bc(sm6[:, d, 0:6].unsqueeze(2), [128, 6, 64]), op=ALU.mult)
                nb = 3 if ret else 6
                V("tensor_tensor", [("S", d), ("cdx", d)], [("S", d)],
                  out=Sst[:, d, 0:ncs].rearrange("p (a b) -> p a b", a=nb),
                  in0=Sst[:, d, 0:ncs].rearrange("p (a b) -> p a b", a=nb),
                  in1=bc(cdx[:, d, 0:nb].unsqueeze(2), [128, nb, 64]), op=ALU.mult)
                yield
                if ret:
                    for fc in range(3):
                        PE(["Ktok", ("Vs", d)], [kC], out=bC[:, fc * 128:(fc + 1) * 128],
                           lhsT=tl["Ktok"][:, c, fc * 128:(fc + 1) * 128], rhs=Vs[:, d, fc * 128:(fc + 1) * 128],
                           start=(fc == 0), stop=(fc == 2), skip_group_check=True)
                    yield
                    for hp in range(2):
                        pr = slice(hp * 64, hp * 64 + 64)
                        diag = bass.AP(bC, hp * 64 * 512 + hp * 64, [[512, 64], [128, 3], [1, 64]])
                        V("tensor_tensor", [("S", d), kC], [("S", d)],
                          out=Sst[pr, d, 0:192].rearrange("p (a b) -> p a b", a=3), in0=diag,
                          in1=Sst[pr, d, 0:192].rearrange("p (a b) -> p a b", a=3), op=ALU.add)
                else:
                    for h in range(6):
                        oc = slice(h * 64, (h + 1) * 64)
                        g = h // 3
                        PE(["Ktok", ("Vs", d)], [kC], out=bC[:, oc], lhsT=tl["Ktok"][:, c, g * 128:(g + 1) * 128],
                           rhs=Vs[:, d, oc], start=(h == 0), stop=(h == 5), skip_group_check=True)
                    yield
                    V("tensor_tensor", [("S", d), kC], [("S", d)], out=Sst[:, d, 0:ncs], in0=bC[:, 0:ncs],
                      in1=Sst[:, d, 0:ncs], op=ALU.add)
                if seg_end:
                    dst = oret_d[seg, l, d] if ret else ossd_d[seg, l, d]
                    sp_ = 0
                    stg_cnt[d] += 1
                    act(Sstg[:, d, sp_, 0:ncs], Sst[:, d, 0:ncs], AF.Copy, [("S", d)], [("Sstg", d, sp_)])
                    S.dma("sp", "out", reads=[("Sstg", d, sp_)], is_output=True, out=dst, in_=Sstg[:, d, sp_, 0:ncs])
                if c in visited:
                    pending.append((c, ofin[:, d, :], ("ofin", d), step_of[0]))
                visited.add(c)

            pending = []
            for step in range(nch):
                step_of[0] = step
                gens = [proc(0, step), proc(1, nch - 1 - step)]
                alive = [True, True]
                rnd = 0
                while any(alive):
                    for i in range(2):
                        if alive[i]:
                            try:
                                next(gens[i])
                            except StopIteration:
                                alive[i] = False
                    rnd += 1
                    if rnd == POST_DELAY and pending:
                        prev = [p for p in pending if p[3] < step]
                        for (c_, o_, k_, _) in prev:
                            post(c_, o_, k_)
                        pending[:] = [p for p in pending if p[3] >= step]
            for (c_, o_, k_, _) in pending:
                post(c_, o_, k_)

        def transpose_batch(srcs, src_keys):
            key = ("ps", 7)
            for i, sap in enumerate(srcs):
                S.op("pe", "transpose", list(src_keys) + ["identb"], [key], out=pst[:, i * 128:(i + 1) * 128],
                     in_=sap, identity=identb[:])
            return pst[:, 0:len(srcs) * 128], key

        m_persist = AR.mark()
        try:
          stage("setup")
          for l in range(L):
            if l == 0:
                for piece_ in range(12):
                    mod_piece(0, piece_)
            mod_finish(l)
            modT, mk = modTs[l % 2], ("modT", l % 2)
            A1, A2 = A1s[l % 2], A2s[l % 2]
            stage("mod")
            for (seg0, nseg) in GROUPS:
                  ntok = nseg * SEGT
                  nch = nseg * 2
                  ntb = nseg // 2
                  gt0 = seg0 * SEGT
                  S.barrier()
                  mg = AR.mark()
                  hT = AR.alloc("hT", [128, 8, ntok], BF16)
                  ycT = AR.alloc("ycT", [128, 8, ntok], BF16)
                  for nm_ in ("q", "k", "v"):
                      prefetch(l, (nm_,))
                  mn = AR.mark()
                  tmpa = (AR.alloc("sq", [128, 2, 8, 512], BF16), AR.alloc("rstd", [128, 2, 512], F32),
                          AR.alloc("tmpx", [128, 2, 512], F32))
                  norm_mod(hT, seg0, nseg, A1, ("A1", l % 2), 0, tmpa, modT, mk)
                  S.barrier()
                  stage("norm1")
                  AR.release(mn)
                  if l == 0 and seg0 == 0:
                      dump("hT", hT[:, 0, 0:512], [("hT", k) for k in range(8)])

                  mret = AR.mark()
                  Ktok = AR.alloc("Ktok", [128, nch, 384], BF16)
                  Vtok = AR.alloc("Vtok", [128, nch, 384], BF16)
                  SGtok = AR.alloc("SGtok", [128, nch, 384], BF16)
                  QT = AR.alloc("QT", [128, 2, 3, ntok], BF16)
                  V("memset", [], ["QT"], ap=QT[:], constant=0.0)
                  KT = AR.alloc("KT", [128, 3, ntok], BF16)
                  oacc = AR.alloc("oacc", [128, nch, 384], F32)
                  S0r = AR.alloc("S0", [128, 2, 384], F32)
                  for d in range(2):
                      S.dma("sp", "s0", writes=[("S0", d)], out=S0r[:, d, 0:192], in_=s0ret_d[l, d])
                  mproj = AR.mark()
                  ropeT = [AR.alloc(f"rope{i}", [128, nch, 64], F32) for i in range(4)]
                  for i in range(4):
                      S.dma("sp", "rope", writes=["rope"], out=ropeT[i][:],
                            in_=rope_d[i, gt0:gt0 + ntok, :].rearrange("(c p) f -> p c f", p=128))
                  qtmp = [AR.alloc(f"qtmp{i}", [128, 384], BF16) for i in range(2)]
                  rt1 = [AR.alloc(f"rt1{i}", [128, 384], F32) for i in range(2)]
                  rt2 = [AR.alloc(f"rt2{i}", [128, 384], F32) for i in range(2)]
                  rsrc = [AR.alloc(f"rsrc{i}", [128, 384], F32) for i in range(2)]
                  hkeys = [("hT", k) for k in range(8)]
                  def rope_tail(pname, c, qt_, dkey, Ktok=Ktok, QT=QT, KT=KT):
                      sl_ = [(qt_[:, fc * 128:(fc + 1) * 128] if pname == "q" else Ktok[:, c, fc * 128:(fc + 1) * 128]) for fc in range(3)]
                      o_, k_ = transpose_batch(sl_, [dkey])
                      o3_ = o_.rearrange("p (a b) -> p a b", a=3)
                      if pname == "q":
                          for hp in range(2):
                              pr = slice(hp * 64, hp * 64 + 64)
                              act(QT[pr, hp, :, c * 128:(c + 1) * 128], o3_[pr], AF.Copy, [k_], ["QT"])
                      else:
                          act(KT[:, :, c * 128:(c + 1) * 128], o3_, AF.Copy, [k_], ["KT"])

                  pend_r = None
                  for pi, pname in enumerate(("q", "k", "v", "g")):
                      wv, wk = load_piece(l, (pname,))
                      if pname == "v" and pend_r is not None:
                          rope_tail(*pend_r)
                          pend_r = None
                      for c in range(nch):
                          bk = (pi * nch + c) % 7
                          for kc in range(8):
                              PE(hkeys + [wk], [P(bk)], out=ps[bk][:, 0:384], lhsT=hT[:, kc, c * 128:(c + 1) * 128],
                                 rhs=wv[:, kc, :], start=(kc == 0), stop=(kc == 7))
                          if False:
                              pass
                          elif pname in ("q", "k"):
                              rb = c % 2
                              qt_, rs_, r1_, r2_ = qtmp[rb], rsrc[rb], rt1[rb], rt2[rb]
                              dst = qt_[:] if pname == "q" else Ktok[:, c, :]
                              dkey = ("qtmp", rb) if pname == "q" else "Ktok"
                              ct, sn = (ropeT[0], ropeT[1]) if pname == "q" else (ropeT[2], ropeT[3])
                              act(rs_[:], ps[bk][:, 0:384], AF.Copy, [P(bk)], [("rsrc", rb)])
                              V("tensor_tensor", [("rsrc", rb), "rope"], [("rt1", rb)],
                                out=r1_[:].rearrange("p (a b) -> p a b", a=6), in0=rs_[:].rearrange("p (a b) -> p a b", a=6),
                                in1=bc(ct[:, c, :].unsqueeze(1), [128, 6, 64]), op=ALU.mult)
                              s4 = rs_[:].rearrange("p (a r f c) -> p a r f c", a=6, r=2, f=2)
                              r24 = r2_[:].rearrange("p (a r f c) -> p a r f c", a=6, r=2, f=2)
                              sn4 = sn[:, c, :].rearrange("p (r f c) -> p r f c", r=2, f=2)
                              for hf in range(2):
                                  V("tensor_tensor", [("rsrc", rb), "rope"], [("rt2", rb)], out=r24[:, :, :, hf, :],
                                    in0=s4[:, :, :, 1 - hf, :],
                                    in1=bc(sn4[:, :, hf, :].unsqueeze(1), [128, 6, 2, 16]), op=ALU.mult)
                              V("tensor_tensor", [("rt1", rb), ("rt2", rb)], [dkey], out=dst, in0=r1_[:], in1=r2_[:], op=ALU.add)
                              if pend_r is not None:
                                  rope_tail(*pend_r)
                              pend_r = (pname, c, qt_, dkey)
                          elif pname == "v":
                              act(Vtok[:, c, :], ps[bk][:, 0:384], AF.Copy, [P(bk)], ["Vtok"])
                          else:
                              act(SGtok[:, c, :], ps[bk][:, 0:384], AF.Silu, [P(bk)], ["SGtok"])
                  if l == 0 and seg0 == 0:
                      dump("retla", retla[:, 0, :], ["retla"])
                      dump("QT", QT[:, 0, 0, 0:512], ["QT"])
                      dump("Ktok", Ktok[:, 0, :], ["Ktok"])
                  stage("retproj")
                  if "nossd" not in FEAT:
                      prefetch(l, ("z",)); prefetch(l, ("dt",)); prefetch(l, ("xbc", 0))
                  S.barrier()
                  AR.release(mproj)
                  tl = dict(
                      S=AR.alloc("Sst", [128, 2, 384], F32), Sbf=AR.alloc("Sbf", [128, 2, 384], BF16),
                      X=AR.alloc("X", [128, 1, 2, 3, 128], F32), E=AR.alloc("E", [128, 2, 2, 384], F32),
                      Ecum=AR.alloc("Ecum", [128, 2, 2, 384], F32), Sm=None,
                      A=AR.alloc("Am", [128, 2, 2, 384], BF16), Qs=AR.alloc("Qs", [128, 2, 2, 384], BF16),
                      V=None, Vs=AR.alloc("Vs", [128, 2, 384], BF16), oacc=oacc,
                      ofin=AR.alloc("ofin", [128, 2, 384], F32), sm6=AR.alloc("sm6", [128, 2, 8], F32),
                      cdx=AR.alloc("cdx", [128, 2, 8], F32), S0=S0r, Sstg=AR.alloc("Sstg", [128, 2, 1, 192], F32),
                      QT=QT, KT=KT, Ktok=Ktok)
                  tl["la"] = lambda d, c, l=l: (retla[:, l, d * 6:(d + 1) * 6], "retla")
                  tl["v"] = lambda d, c: (Vtok[:, c, :], "Vtok")
                  pr_ = dict(msum=AR.alloc("msum", [128, 8], F32), cen=AR.alloc("cen", [128, 384], F32),
                             sq=AR.alloc("sqr", [128, 384], F32), ynb=AR.alloc("ynb", [128, 384], BF16))

                  def post_ret(c, ofin, ofk, l=l, pr_=pr_, SGtok=SGtok, ycT=ycT, seg0=seg0):
                      msum, cen, sqr, ynb = pr_["msum"], pr_["cen"], pr_["sq"], pr_["ynb"]
                      o3 = ofin.rearrange("p (a b) -> p a b", a=6)
                      V("tensor_reduce", [ofk], ["msum"], out=msum[:, 0:6], in_=o3, axis=AX.X, op=ALU.add)
                      V("tensor_scalar_mul", ["msum"], ["msum"], out=msum[:, 0:6], in0=msum[:, 0:6], scalar1=-1.0 / 64)
                      c3 = cen[:].rearrange("p (a b) -> p a b", a=6)
                      V("tensor_tensor", [ofk, "msum"], ["cen"], out=c3, in0=o3,
                        in1=bc(msum[:, 0:6].unsqueeze(2), [128, 6, 64]), op=ALU.add)
                      V("tensor_tensor", ["cen"], ["sqr"], out=sqr[:], in0=cen[:], in1=cen[:], op=ALU.mult)
                      V("tensor_reduce", ["sqr"], ["msum"], out=msum[:, 0:6],
                        in_=sqr[:].rearrange("p (a b) -> p a b", a=6), axis=AX.X, op=ALU.add)
                      V("tensor_scalar", ["msum"], ["msum"], out=msum[:, 0:6], in0=msum[:, 0:6], scalar1=1.0 / 64, scalar2=EPS,
                        op0=ALU.mult, op1=ALU.add)
                      G("tensor_tensor", ["msum", "mhalfT"], ["msum"], out=msum[:, 0:6], in0=msum[:, 0:6], in1=mhalfT[:, 0:6],
                        op=ALU.pow)
                      V("tensor_tensor", ["cen", "msum"], ["cen"], out=c3, in0=c3,
                        in1=bc(msum[:, 0:6].unsqueeze(2), [128, 6, 64]), op=ALU.mult)
                      V("tensor_tensor", ["cen", "SGtok"], ["ynb"], out=ynb[:], in0=cen[:], in1=SGtok[:, c, :], op=ALU.mult)
                      o_, k_ = transpose_batch([ynb[:, fc * 128:(fc + 1) * 128] for fc in range(3)], ["ynb"])
                      V("tensor_tensor", [k_, "pk"], [("ycT", fc) for fc in range(3)], out=ycT[:, 0:3, c * 128:(c + 1) * 128],
                        in0=o_.rearrange("p (a b) -> p a b", a=3),
                        in1=bc(pk[:, l, PK_RNG:PK_RNG + 3].unsqueeze(2), [128, 3, 128]), op=ALU.mult)

                  bidir_scan(l, seg0, nseg, "ret", tl, post_ret)
                  stage("retscan")
                  if l == 0 and seg0 == 0:
                      dump("ycT_ret", ycT[:, 0, 0:512], [("ycT", 0)])
                      dump("ycT_ret1", ycT[:, 1, 0:512], [("ycT", 1)])
                      dump("ycT_ret2", ycT[:, 2, 0:512], [("ycT", 2)])
                  S.barrier()
                  AR.release(mret)

                  if "nossd" in FEAT:
                      V("memset", [], [("ycT", k) for k in range(3, 6)], ap=ycT[:, 3:6, :], constant=0.0)
                  else:
                      mssd = AR.mark()
                      SZtok = AR.alloc("SZtok", [128, nch, 384], BF16)
                      BCT = AR.alloc("BCT", [128, 4, ntok], BF16)
                      XStok = AR.alloc("XStok", [128, nch, 384], BF16)
                      Btok = AR.alloc("Btok", [128, nch, 256], BF16)
                      dtT = AR.alloc("dtT", [128, nch, 12], F32)
                      laT = AR.alloc("laT", [128, nch, 12], F32)
                      oacc = AR.alloc("oacc2", [128, nch, 384], F32)
                      S0s = AR.alloc("S0", [128, 2, 384], F32)
                      for d in range(2):
                          S.dma("sp", "s0", writes=[("S0", d)], out=S0s[:, d, :], in_=s0ssd_d[l, d])
                      mproj = AR.mark()
                      NXB = 4
                      xpad = [AR.alloc(f"xpad{i}", [128, nseg, 259], F32) for i in range(NXB)]
                      cacc = [AR.alloc(f"cacc{i}", [128, ntok], F32) for i in range(NXB)]
                      xsb = [AR.alloc(f"xsb{i}", [128, ntok], BF16) for i in range(2)]
                      for i in range(NXB):
                          V("memset", [], [("xpad", i)], ap=xpad[i][:], constant=0.0)
                      wv, wk = load_piece(l, ("z",))
                      for c in range(nch):
                          bk = c % 7
                          for kc in range(8):
                              PE(hkeys + [wk], [P(bk)], out=ps[bk][:, 0:384], lhsT=hT[:, kc, c * 128:(c + 1) * 128],
                                 rhs=wv[:, kc, :], start=(kc == 0), stop=(kc == 7))
                          act(SZtok[:, c, :], ps[bk][:, 0:384], AF.Silu, [P(bk)], ["SZtok"])
                      wv, wk = load_piece(l, ("dt",))
                      for c in range(nch):
                          for kc in range(8):
                              PE(hkeys + [wk], [P(6)], out=ps[6][:, c * 8:c * 8 + 6], lhsT=hT[:, kc, c * 128:(c + 1) * 128],
                                 rhs=wv[:, kc, :], start=(kc == 0), stop=(kc == 7))
                      V("tensor_tensor", [P(6), "pk"], ["dtT"], out=dtT[:].rearrange("p c (a b) -> p c a b", a=2),
                        in0=bass.AP(ps[6], 0, [[512, 128], [8, nch], [0, 2], [1, 6]]),
                        in1=bass.AP(pk, l * NPK + PK_DTB, [[L * NPK, 128], [0, nch], [6, 2], [1, 6]]), op=ALU.add)
                      act(dtT[:], dtT[:], AF.Exp, ["dtT"], ["dtT"])
                      act(dtT[:], dtT[:], AF.Ln, ["dtT"], ["dtT"], bias=1.0)
                      V("tensor_tensor", ["dtT", "ssdA"], ["laT"], out=laT[:], in0=dtT[:],
                        in1=bass.AP(ssdA, l * 12, [[L * 12, 128], [0, nch], [1, 12]]), op=ALU.mult)
                      def xbc_tail(fc, par, ca, xsb=xsb, XStok=XStok, BCT=BCT, Btok=Btok, nch=nch):
                          if fc < 3:
                              act(xsb[par % 2][:], ca[:], AF.Silu, [("cacc", par)], [("xsb", par % 2)])
                              o_, k_ = transpose_batch([xsb[par % 2][:, c * 128:(c + 1) * 128] for c in range(nch)], [("xsb", par % 2)])
                              act(XStok[:, :, fc * 128:(fc + 1) * 128], o_.rearrange("p (a b) -> p a b", a=nch), AF.Copy,
                                  [k_], ["XStok"])
                          else:
                              bi = fc - 3
                              act(BCT[:, bi, :], ca[:], AF.Silu, [("cacc", par)], [("BCT", bi)])
                              if bi < 2:
                                  o_, k_ = transpose_batch([BCT[:, bi, c * 128:(c + 1) * 128] for c in range(nch)], [("BCT", bi)])
                                  act(Btok[:, :, bi * 128:(bi + 1) * 128], o_.rearrange("p (a b) -> p a b", a=nch), AF.Copy,
                                      [k_], ["Btok"])

                      pend_x = None
                      for pi, (col0, nfc) in enumerate(((1920, 4), (2432, 3))):
                          wv, wk = load_piece(l, ("xbc", pi))
                          for j in range(nfc):
                              fc = pi * 4 + j
                              par = fc % NXB
                              xp, ca = xpad[par], cacc[par]
                              for tb in range(ntb):
                                  bk = (fc * 2 + tb) % 6
                                  for kc in range(8):
                                      PE(hkeys + [wk], [P(bk)], out=ps[bk][:, :], lhsT=wv[:, kc, j * 128:(j + 1) * 128],
                                         rhs=hT[:, kc, tb * 512:(tb + 1) * 512], start=(kc == 0), stop=(kc == 7))
                                  act(xp[:, 2 * tb:2 * tb + 2, 1:257], ps[bk][:, :].rearrange("p (a b) -> p a b", a=2),
                                      AF.Copy, [P(bk)], [("xpad", par)])
                              lk = flags[:, seg0 + 1:seg0 + nseg]
                              V("tensor_tensor", [("xpad", par), "flags"], [("xpad", par)], out=xp[:, 1:nseg, 0:1],
                                in0=xp[:, 0:nseg - 1, 256:257], in1=lk.unsqueeze(2), op=ALU.mult)
                              V("tensor_tensor", [("xpad", par), "flags"], [("xpad", par)], out=xp[:, 0:nseg - 1, 257:259],
                                in0=xp[:, 1:nseg, 1:3], in1=bc(lk.unsqueeze(2), [128, nseg - 1, 2]), op=ALU.mult)
                              ca3 = ca[:].rearrange("p (a b) -> p a b", a=nseg)
                              wc = PK_SCW + fc * 4
                              act(ca3, xp[:, :, 0:256], AF.Identity, [("xpad", par), "pk"], [("cacc", par)],
                                  scale=pk[:, l, wc:wc + 1], bias=pk[:, l, PK_SCB + fc:PK_SCB + fc + 1])
                              for tap in range(1, 4):
                                  V("scalar_tensor_tensor", [("xpad", par), "pk", ("cacc", par)], [("cacc", par)], out=ca3,
                                    in0=xp[:, :, tap:tap + 256], scalar=pk[:, l, wc + tap:wc + tap + 1], in1=ca3,
                                    op0=ALU.mult, op1=ALU.add)
                              if pend_x is not None:
                                  xbc_tail(*pend_x)
                              pend_x = (fc, par, ca)
                      xbc_tail(*pend_x)
                      stage("ssdproj")
                      prefetch(l, ("lru",)); prefetch(l, ("out", 0)); prefetch(l, ("out", 1))
                      S.barrier()
                      AR.release(mproj)
                      tl = dict(
                          S=AR.alloc("Sst", [128, 2, 384], F32), Sbf=AR.alloc("Sbf", [128, 2, 384], BF16),
                          X=AR.alloc("X", [128, 2, 2, 3, 128], F32), E=AR.alloc("E", [128, 2, 2, 384], F32),
                          Ecum=AR.alloc("Ecum", [128, 2, 2, 384], F32), Sm=AR.alloc("Sm", [128, 2, 2, 128], F32),
                          A=AR.alloc("Am", [128, 2, 2, 384], BF16), Qs=AR.alloc("Qs", [128, 2, 2, 384], BF16),
                          V=None, Vs=AR.alloc("Vs", [128, 2, 384], BF16), oacc=oacc,
                          ofin=AR.alloc("ofin", [128, 2, 384], F32), sm6=AR.alloc("sm6", [128, 2, 8], F32),
                          cdx=AR.alloc("cdx", [128, 2, 8], F32), S0=S0s, Sstg=AR.alloc("Sstg", [128, 2, 1, 384], F32),
                          QT=BCT[:, 2:4, :], KT=BCT[:, 0:2, :], Ktok=Btok)
                      tl["la"] = lambda d, c, laT=laT: (laT[:, c, d * 6:(d + 1) * 6], "laT")
                      Vt = [AR.alloc(f"Vt{i}", [128, 384], BF16) for i in range(4)]
                      vcnt = {"i": 0}

                      def v_ssd(d, c, Vt=Vt, vcnt=vcnt, XStok=XStok, dtT=dtT):
                          i = vcnt["i"] % 4
                          vcnt["i"] += 1
                          G("tensor_tensor", ["XStok", "dtT"], [("Vt", i)], out=Vt[i][:].rearrange("p (a b) -> p a b", a=6),
                            in0=XStok[:, c, :].rearrange("p (a b) -> p a b", a=6),
                            in1=bc(dtT[:, c, d * 6:(d + 1) * 6].unsqueeze(2), [128, 6, 64]), op=ALU.mult)
                          return Vt[i][:], ("Vt", i)
                      tl["v"] = v_ssd
                      ps_ = dict(u=AR.alloc("u", [128, 384], F32), junk=AR.alloc("junk", [128, 384], F32),
                                 ss=AR.alloc("ss", [128, 2], F32), unb=AR.alloc("unb", [128, 384], BF16))

                      def post_ssd(c, ofin, ofk, l=l, ps_=ps_, SZtok=SZtok, XStok=XStok, ycT=ycT):
                          u, junk, ss, unb = ps_["u"], ps_["junk"], ps_["ss"], ps_["unb"]
                          V("tensor_tensor", ["XStok", "pk"], ["u"], out=u[:], in0=XStok[:, c, :],
                            in1=pk[:, l, PK_SSDD:PK_SSDD + 384], op=ALU.mult)
                          V("tensor_tensor", ["u", ofk], ["u"], out=u[:], in0=u[:], in1=ofin, op=ALU.add)
                          V("tensor_tensor", ["u", "SZtok"], ["u"], out=u[:], in0=u[:], in1=SZtok[:, c, :], op=ALU.mult)
                          act(junk[:], u[:], AF.Square, ["u"], ["junk", "ss"], accum_out=ss[:, 0:1])
                          V("tensor_scalar", ["ss"], ["ss"], out=ss[:, 0:1], in0=ss[:, 0:1], scalar1=1.0 / 384, scalar2=EPS,
                            op0=ALU.mult, op1=ALU.add)
                          G("tensor_tensor", ["ss", "mhalfT"], ["ss"], out=ss[:, 0:1], in0=ss[:, 0:1], in1=mhalfT[:, 0:1],
                            op=ALU.pow)
                          V("tensor_scalar", ["u", "ss"], ["unb"], out=unb[:], in0=u[:], scalar1=ss[:, 0:1], scalar2=None,
                            op0=ALU.mult)
                          o_, k_ = transpose_batch([unb[:, fc * 128:(fc + 1) * 128] for fc in range(3)], ["unb"])
                          V("tensor_tensor", [k_, "pk"], [("ycT", 3 + fc) for fc in range(3)],
                            out=ycT[:, 3:6, c * 128:(c + 1) * 128], in0=o_.rearrange("p (a b) -> p a b", a=3),
                            in1=bc(pk[:, l, PK_SNG:PK_SNG + 3].unsqueeze(2), [128, 3, 128]), op=ALU.mult)

                      bidir_scan(l, seg0, nseg, "ssd", tl, post_ssd)
                      stage("ssdscan")
                      if l == 0 and seg0 == 0:
                          for fc in range(3):
                              dump(f"ycT_ssd{fc}", ycT[:, 3 + fc, 0:512], [("ycT", 3 + fc)])
                      S.barrier()
                      AR.release(mssd)

                  if "nolru" in FEAT:
                      V("memset", [], [("ycT", k) for k in range(6, 8)], ap=ycT[:, 6:8, :], constant=0.0)
                  else:
                      mlru = AR.mark()
                      xc = AR.alloc("xc", [128, 2, ntok], F32)
                      xcb = AR.alloc("xcb", [128, 2, ntok], BF16)
                      glT = AR.alloc("glT", [128, 2, ntok], F32)
                      hsT = AR.alloc("hsT", [128, 2, 2, ntok], F32)
                      ini = AR.alloc("ini", [128, 4], F32)
                      msub = AR.mark()
                      xpad = [AR.alloc(f"xpadl{i}", [128, nseg, 259], F32) for i in range(2)]
                      ga = AR.alloc("gscr", [128, ntok], F32)
                      gak = "gscr"
                      for i in range(2):
                          V("memset", [], [("xpad", i)], ap=xpad[i][:], constant=0.0)
                      wv, wk = load_piece(l, ("lru",))
                      for j in range(4):
                          par = j % 2
                          xp = xpad[par]
                          for tb in range(ntb):
                              bk = (j * 2 + tb) % 6
                              for kc in range(8):
                                  PE(hkeys + [wk], [P(bk)], out=ps[bk][:, :], lhsT=wv[:, kc, j * 128:(j + 1) * 128],
                                     rhs=hT[:, kc, tb * 512:(tb + 1) * 512], start=(kc == 0), stop=(kc == 7))
                              if j < 2:
                                  act(xp[:, 2 * tb:2 * tb + 2, 1:257], ps[bk][:, :].rearrange("p (a b) -> p a b", a=2),
                                      AF.Copy, [P(bk)], [("xpad", par)])
                              else:
                                  act(glT[:, j - 2, tb * 512:(tb + 1) * 512], ps[bk][:, :], AF.Copy, [P(bk)], [("glT", j - 2)])
                          if j < 2:
                              lk = flags[:, seg0 + 1:seg0 + nseg]
                              V("tensor_tensor", [("xpad", par), "flags"], [("xpad", par)], out=xp[:, 1:nseg, 0:1],
                                in0=xp[:, 0:nseg - 1, 256:257], in1=lk.unsqueeze(2), op=ALU.mult)
                              V("tensor_tensor", [("xpad", par), "flags"], [("xpad", par)], out=xp[:, 0:nseg - 1, 257:259],
                                in0=xp[:, 1:nseg, 1:3], in1=bc(lk.unsqueeze(2), [128, nseg - 1, 2]), op=ALU.mult)
                              ca3 = xc[:, j, :].rearrange("p (a b) -> p a b", a=nseg)
                              wc = PK_LCW + j * 4
                              act(ca3, xp[:, :, 0:256], AF.Identity, [("xpad", par), "pk"], [("xc", j)],
                                  scale=pk[:, l, wc:wc + 1], bias=pk[:, l, PK_LCB + j:PK_LCB + j + 1])
                              for tap in range(1, 4):
                                  V("scalar_tensor_tensor", [("xpad", par), "pk", ("xc", j)], [("xc", j)], out=ca3,
                                    in0=xp[:, :, tap:tap + 256], scalar=pk[:, l, wc + tap:wc + tap + 1], in1=ca3,
                                    op0=ALU.mult, op1=ALU.add)
                              act(xcb[:, j, :], xc[:, j, :], AF.Copy, [("xc", j)], [("xcb", j)])
                          else:
                              g_ = glT[:, j - 2, :]
                              gk = ("glT", j - 2)
                              act(ga[:], g_, AF.Square, [gk], [gak])
                              G("tensor_scalar", [gak], [gak], out=ga[:], in0=ga[:], scalar1=0.044715, scalar2=1.0,
                                op0=ALU.mult, op1=ALU.add)
                              G("tensor_tensor", [gak, gk], [gak], out=ga[:], in0=ga[:], in1=g_, op=ALU.mult)
                              act(ga[:], ga[:], AF.Tanh, [gak], [gak], scale=0.7978845608028654)
                              V("scalar_tensor_tensor", [gak, gk], [gk], out=g_, in0=ga[:], scalar=1.0, in1=g_,
                                op0=ALU.add, op1=ALU.mult)
                              V("tensor_scalar_mul", [gk], [gk], out=g_, in0=g_, scalar1=0.5)
                      S.barrier()
                      AR.release(msub)
                      gaL = [AR.alloc(f"ga{i}", [128, ntok], F32) for i in range(4)]
                      gbL = [AR.alloc(f"gb{i}", [128, ntok], F32) for i in range(4)]
                      giL = [AR.alloc(f"gi{i}", [128, ntok], F32) for i in range(4)]
                      units = [(d, ch) for d in range(2) for ch in range(2)]
                      U = {u: (gaL[i], gbL[i], giL[i], f"ga{i}", f"gb{i}", f"gi{i}") for i, u in enumerate(units)}
                      bkc = 0
                      for (d, ch) in units:
                          ga, gb, gi, gak, gbk, gik = U[(d, ch)]
                          for gi_, (dstt, bcol) in enumerate(((ga, PK_LBA), (gi, PK_LBX))):
                              for tb in range(ntb):
                                  bk = bkc % 7
                                  bkc += 1
                                  PE([("xcb", ch), "lruw"], [P(bk)], out=ps[bk][:, :],
                                     lhsT=lruw[:, l * 8 + d * 4 + gi_ * 2 + ch, :], rhs=xcb[:, ch, tb * 512:(tb + 1) * 512],
                                     start=True, stop=True)
                                  act(dstt[:, tb * 512:(tb + 1) * 512], ps[bk][:, :], AF.Sigmoid, [P(bk), "pk"],
                                      [gak if gi_ == 0 else gik],
                                      bias=pk[:, l, bcol + 2 * d + ch:bcol + 2 * d + ch + 1])
                      for (d, ch) in units:
                          ga, gb, gi, gak, gbk, gik = U[(d, ch)]
                          act(ga[:], ga[:], AF.Exp, [gak, "lrucl"], [gak], scale=lrucl[:, l, 2 * d + ch:2 * d + ch + 1])
                      for (d, ch) in units:
                          ga, gb, gi, gak, gbk, gik = U[(d, ch)]
                          G("tensor_tensor", [gak], [gbk], out=gb[:], in0=ga[:], in1=ga[:], op=ALU.mult)
                          V("tensor_tensor", [gik, ("xc", ch)], [gik], out=gi[:], in0=gi[:], in1=xc[:, ch, :], op=ALU.mult)
                      for (d, ch) in units:
                          ga, gb, gi, gak, gbk, gik = U[(d, ch)]
                          act(gb[:], gb[:], AF.Sqrt, [gbk], [gbk], scale=-1.0, bias=1.0)
                      for (d, ch) in units:
                          ga, gb, gi, gak, gbk, gik = U[(d, ch)]
                          G("tensor_tensor", [gbk, gik], [gbk], out=gb[:], in0=gb[:], in1=gi[:], op=ALU.mult)
                      lru_outs = []
                      for si in range(nseg):
                          for ui, (d, ch) in enumerate(units):
                              ga, gb, gi, gak, gbk, gik = U[(d, ch)]
                              hk = ("hsT", d, ch, si)
                              hkp = ("hsT", d, ch, si - 1)
                              ik = ("ini", ui)
                              s = si if d == 0 else nseg - 1 - si
                              seg = seg0 + s
                              fi = (8 + seg) if d == 0 else (16 + seg)
                              li = seg if d == 0 else seg + 1
                              V("tensor_scalar", ["s0lru", "flags"], [ik], out=ini[:, ui:ui + 1],
                                in0=s0lru[:, l, 2 * d + ch:2 * d + ch + 1], scalar1=flags[:, fi:fi + 1], scalar2=None,
                                op0=ALU.mult)
                              if si > 0:
                                  tp = (s * SEGT - 1) if d == 0 else ((s + 1) * SEGT)
                                  V("scalar_tensor_tensor", [hkp, "flags", ik], [ik], out=ini[:, ui:ui + 1],
                                    in0=hsT[:, d, ch, tp:tp + 1], scalar=flags[:, li:li + 1], in1=ini[:, ui:ui + 1],
                                    op0=ALU.mult, op1=ALU.add)
                              if d == 0:
                                  sl_ = slice(s * SEGT, (s + 1) * SEGT)
                                  o_ap, a_ap, b_ap = hsT[:, d, ch, sl_], ga[:, sl_], gb[:, sl_]
                              else:
                                  last = (s + 1) * SEGT - 1
                                  o_ap = bass.AP(hsT, (d * 2 + ch) * ntok + last, [[4 * ntok, 128], [-1, SEGT]])
                                  a_ap = bass.AP(ga, last, [[ntok, 128], [-1, SEGT]])
                                  b_ap = bass.AP(gb, last, [[ntok, 128], [-1, SEGT]])
                              t0_ = (s * SEGT) if d == 0 else ((s + 1) * SEGT - 1)
                              V("scalar_tensor_tensor", [gak, gbk, ik], [gbk], out=gb[:, t0_:t0_ + 1], in0=ga[:, t0_:t0_ + 1],
                                scalar=ini[:, ui:ui + 1], in1=gb[:, t0_:t0_ + 1], op0=ALU.mult, op1=ALU.add)
                              V("tensor_tensor_scan", [gak, gbk], [hk], out=o_ap, data0=a_ap, data1=b_ap,
                                initial=0.0, op0=ALU.mult, op1=ALU.add)
                              tp = ((s + 1) * SEGT - 1) if d == 0 else (s * SEGT)
                              lru_outs.append((hk, olru_d[seg, l, d, ch].unsqueeze(1), hsT[:, d, ch, tp:tp + 1]))
                      for (hk_, dst_, src_) in lru_outs:
                          S.dma("sp", "out", reads=[hk_], is_output=True, out=dst_, in_=src_)
                      for ch in range(2):
                          V("tensor_tensor", [("hsT", dd, ch, si_) for dd in range(2) for si_ in range(nseg)], [f"gb{ch}"], out=gbL[ch][:], in0=hsT[:, 0, ch, :],
                            in1=hsT[:, 1, ch, :], op=ALU.add)
                          G("tensor_tensor", [f"gb{ch}", ("glT", ch)], [("ycT", 6 + ch)], out=ycT[:, 6 + ch, :], in0=gbL[ch][:],
                            in1=glT[:, ch, :], op=ALU.mult)
                      stage("lru")
                      if l == 0 and seg0 == 0:
                          for ch in range(2):
                              dump(f"ycT_lru{ch}", ycT[:, 6 + ch, 0:512], [("ycT", 6 + ch)])
                      S.barrier()
                      AR.release(mlru)

                  for fc in range(8):
                      wv, wk = load_piece(l, ("out", fc))
                      for tb in range(ntb):
                          bk = 1 + (fc * ntb + tb) % 5
                          for kc in range(8):
                              PE([wk, ("ycT", kc)], [P(bk)], out=ps[bk][:, :], lhsT=wv[:, kc, :],
                                 rhs=ycT[:, kc, tb * 512:(tb + 1) * 512], start=(kc == 0), stop=(kc == 7))
                          for s2 in range(2):
                              s = seg0 + 2 * tb + s2
                              ts_ = slice(s * SEGT, (s + 1) * SEGT)
                              V("scalar_tensor_tensor", [P(bk), mk, ("xT", fc)], [("xT", fc)], out=xT[:, fc, ts_],
                                in0=ps[bk][:, s2 * SEGT:(s2 + 1) * SEGT], scalar=modT[:, 16 + fc, s:s + 1],
                                in1=xT[:, fc, ts_], op0=ALU.mult, op1=ALU.add)
                  stage("outproj")
                  S.barrier()
                  AR.release(mg)

                  if "noffn" not in FEAT:
                      mf_ = AR.mark()
                      hT = AR.alloc("h2T", [128, 8, ntok], BF16)
                      actT = AR.alloc("actT", [128, 22, ntok], BF16)
                      mn = AR.mark()
                      tmpa = (AR.alloc("sq", [128, 2, 8, 512], BF16), AR.alloc("rstd", [128, 2, 512], F32),
                              AR.alloc("tmpx", [128, 2, 512], F32))
                      prefetch(l, ("upv", 0)); prefetch(l, ("upg", 0))
                      norm_mod(hT, seg0, nseg, A2, ("A2", l % 2), 24, tmpa, modT, mk)
                      S.barrier()
                      AR.release(mn)
                      upad = [AR.alloc(f"upad{i}", [128, nseg, 258], F32) for i in range(4)]
                      uacc = [AR.alloc(f"uacc{i}", [128, ntok], F32) for i in range(4)]
                      for i in range(4):
                          V("memset", [], [("upad", i)], ap=upad[i][:], constant=0.0)
                      lk = flags[:, seg0 + 1:seg0 + nseg]
                      upcnt = {"i": 0}
                      hide_mod = (seg0 == GROUPS[-1][0]) and (l + 1 < L)

                      def up_chunk(wv, wk, j, fcg, bi, l=l, hT=hT, upad=upad, uacc=uacc, lk=lk, nseg=nseg, ntb=ntb, upcnt=upcnt, hide_mod=hide_mod):
                          xp, ca = upad[bi], uacc[bi]
                          for tb in range(ntb):
                              bk = (1 + upcnt["i"] % 6) if hide_mod else (upcnt["i"] % 7)
                              upcnt["i"] += 1
                              for kc in range(8):
                                  PE(hkeys + [wk], [P(bk)], out=ps[bk][:, :], lhsT=wv[:, kc, j * 128:(j + 1) * 128],
                                     rhs=hT[:, kc, tb * 512:(tb + 1) * 512], start=(kc == 0), stop=(kc == 7))
                              act(xp[:, 2 * tb:2 * tb + 2, 1:257], ps[bk][:, :].rearrange("p (a b) -> p a b", a=2),
                                  AF.Copy, [P(bk)], [("upad", bi)])
                          V("tensor_tensor", [("upad", bi), "flags"], [("upad", bi)], out=xp[:, 1:nseg, 0:1],
                            in0=xp[:, 0:nseg - 1, 256:257], in1=lk.unsqueeze(2), op=ALU.mult)
                          V("tensor_tensor", [("upad", bi), "flags"], [("upad", bi)], out=xp[:, 0:nseg - 1, 257:258],
                            in0=xp[:, 1:nseg, 1:2], in1=lk.unsqueeze(2), op=ALU.mult)
                          ca3 = ca[:].rearrange("p (a b) -> p a b", a=nseg)
                          wc = PK_FCW + fcg * 3
                          act(ca3, xp[:, :, 0:256], AF.Identity, [("upad", bi), "pk"], [("uacc", bi)],
                              scale=pk[:, l, wc:wc + 1], bias=pk[:, l, PK_FCB + fcg:PK_FCB + fcg + 1])
                          for tap in range(1, 3):
                              V("scalar_tensor_tensor", [("upad", bi), "pk", ("uacc", bi)], [("uacc", bi)], out=ca3,
                                in0=xp[:, :, tap:tap + 256], scalar=pk[:, l, wc + tap:wc + tap + 1], in1=ca3,
                                op0=ALU.mult, op1=ALU.add)

                      def up_tail(fcv, bv, bg, uacc=uacc, actT=actT):
                          act(uacc[bg][:], uacc[bg][:], AF.Silu, [("uacc", bg)], [("uacc", bg)])
                          V("tensor_tensor", [("uacc", bg), ("uacc", bv)], [("actT", fcv)], out=actT[:, fcv, :],
                            in0=uacc[bg][:], in1=uacc[bv][:], op=ALU.mult)

                      pend_u = None
                      for pi in range(6):
                          nfc = 4 if pi < 5 else 2
                          wvv, wkv = load_piece(l, ("upv", pi))
                          wvg, wkg = load_piece(l, ("upg", pi))
                          for j in range(nfc):
                              fcv = pi * 4 + j
                              bv, bg = (fcv % 2) * 2, (fcv % 2) * 2 + 1
                              up_chunk(wvv, wkv, j, fcv, bv)
                              up_chunk(wvg, wkg, j, 22 + fcv, bg)
                              if pend_u is not None:
                                  up_tail(*pend_u)
                              pend_u = (fcv, bv, bg)
                          if hide_mod:
                              mod_piece(l + 1, 2 * pi)
                              mod_piece(l + 1, 2 * pi + 1)
                      up_tail(*pend_u)
                      stage("ffnup")
                      for fc in range(8):
                          wv, wk = load_piece(l, ("dn", fc))

                          for tb in range(ntb):
                              bk = (1 + (fc * ntb + tb) % 6) if hide_mod else ((fc * ntb + tb) % 7)
                              for kc in range(22):
                                  PE([wk, ("actT", kc)], [P(bk)], out=ps[bk][:, :], lhsT=wv[:, kc, :],
                                     rhs=actT[:, kc, tb * 512:(tb + 1) * 512], start=(kc == 0), stop=(kc == 21))
                              for s2 in range(2):
                                  s = seg0 + 2 * tb + s2
                                  ts_ = slice(s * SEGT, (s + 1) * SEGT)
                                  V("scalar_tensor_tensor", [P(bk), mk, ("xT", fc)], [("xT", fc)], out=xT[:, fc, ts_],
                                    in0=ps[bk][:, s2 * SEGT:(s2 + 1) * SEGT], scalar=modT[:, 40 + fc, s:s + 1],
                                    in1=xT[:, fc, ts_], op0=ALU.mult, op1=ALU.add)
                      stage("ffn")
                      S.barrier()
                      AR.release(mf_)

        except StopBuild:
            AR.release(m_persist)

        S.barrier()
        mf = AR.mark()
        sq = AR.alloc("sq", [128, 2, 8, 512], BF16)
        rstd = AR.alloc("rstd", [128, 2, 512], F32)
        yo = AR.alloc("yo", [128, 4, 512], F32)
        norm_stats(0, 0, sq, rstd)
        for tb in range(3):
            t0 = tb * 512
            j = tb % 2
            if tb + 1 < 3:
                norm_stats(t0 + 512, (tb + 1) % 2, sq, rstd)
            for kc in range(8):
                V("scalar_tensor_tensor", [("xT", kc), ("rstd", j), "pkf"], [("yo", kc % 4)], out=yo[:, kc % 4, :],
                  in0=xT[:, kc, t0:t0 + 512], scalar=pkf[:, kc:kc + 1], in1=rstd[:, j, :], op0=ALU.mult, op1=ALU.mult)
                S.dma("sp", "out", reads=[("yo", kc % 4)], is_output=True,
                      out=yT_d[kc * 128:(kc + 1) * 128, t0:t0 + 512], in_=yo[:, kc % 4, :])
        S.finish("sp")
        S.emit()
    return nc


def core_segments(core):
    if core < 4:
        return [("s", core, i) for i in range(4)] + [("p", 2 * core, 0), ("p", 2 * core + 1, 0)]
    b0 = 8 + 6 * (core - 4)
    return [("p", b0 + i, 0) for i in range(6)]


def _rep(v):
    return np.broadcast_to(np.asarray(v, np.float32).reshape(1, -1), (128, np.asarray(v).size))


def _fm(v, nchunk):
    return np.asarray(v, np.float32).reshape(nchunk, 128).T


def make_consts():
    j = np.arange(128)
    tri_f = (j[:, None] <= j[None, :]).astype(np.float32)
    tri_b = (j[:, None] >= j[None, :]).astype(np.float32)
    strict_f = (j[:, None] > j[None, :]).astype(np.float32)
    strict_b = (j[:, None] < j[None, :]).astype(np.float32)
    ones = np.ones((128, 128), np.float32)
    ident = np.eye(128, dtype=np.float32)
    return np.stack([tri_f, tri_b, strict_f, strict_b, ones, ident])


def make_rope_tables():
    half = 16
    freqs = (np.float32(10000.0) ** (-np.arange(half, dtype=np.float32) / np.float32(half))).astype(np.float32)
    rows = np.repeat(np.arange(16), 64).astype(np.float32)
    cols = np.tile(np.arange(64), 16).astype(np.float32)
    cos = np.zeros((1024, 2, 2, 16), np.float32)
    sin = np.zeros((1024, 2, 2, 16), np.float32)
    for rc, pos in enumerate((rows, cols)):
        ang = (pos[:, None] * freqs[None, :]).astype(np.float32)
        c, s = np.cos(ang).astype(np.float32), np.sin(ang).astype(np.float32)
        cos[:, rc, 0], cos[:, rc, 1] = c, c
        sin[:, rc, 0], sin[:, rc, 1] = -s, s
    return cos.reshape(1024, 64), sin.reshape(1024, 64)


def prep_inputs(inp):
    f32 = lambda a: np.ascontiguousarray(np.asarray(a, np.float32))
    shared = {"cmat": make_consts()}
    offs, wtot = piece_offsets()
    wpk = np.empty((L, 128, wtot), np.float32)
    for l in range(L):
        for (name, srcn, col0, ncols, K) in piece_list():
            o, nk, _ = offs[name]
            blk = np.asarray(inp[srcn][l], np.float32)[:, col0:col0 + ncols]
            wpk[l, :, o:o + nk * ncols] = blk.reshape(nk, 128, ncols).transpose(1, 0, 2).reshape(128, nk * ncols)
    shared["wpk"] = wpk
    pk = np.zeros((L, 128, NPK), np.float32)
    lruw = np.zeros((L, 8, 128, 128), np.float32)
    for l in range(L):
        pk[l, :, PK_N1:PK_N1 + 8] = _fm(inp["norm1_g"][l], 8)
        pk[l, :, PK_N2:PK_N2 + 8] = _fm(inp["norm2_g"][l], 8)
        pk[l, :, PK_BADA:PK_BADA + 48] = _fm(inp["b_ada"][l], 48)
        pk[l, :, PK_RNG:PK_RNG + 3] = _fm(inp["ret_norm_g"][l], 3)
        pk[l, :, PK_SNG:PK_SNG + 3] = _fm(inp["ssd_norm_g"][l], 3)
        cw = np.asarray(inp["ssd_conv_w"][l], np.float32)
        for fc in range(7):
            pk[l, :, PK_SCW + fc * 4:PK_SCW + fc * 4 + 4] = cw[:, fc * 128:(fc + 1) * 128].T
        pk[l, :, PK_SCB:PK_SCB + 7] = _fm(inp["ssd_conv_b"][l], 7)
        lw = np.asarray(inp["lru_conv_w"][l], np.float32)
        for fc in range(2):
            pk[l, :, PK_LCW + fc * 4:PK_LCW + fc * 4 + 4] = lw[:, fc * 128:(fc + 1) * 128].T
        pk[l, :, PK_LCB:PK_LCB + 2] = _fm(inp["lru_conv_b"][l], 2)
        for d in range(2):
            pk[l, :, PK_LBA + 2 * d:PK_LBA + 2 * d + 2] = _fm(inp["lru_b_a"][l, d], 2)
            pk[l, :, PK_LBX + 2 * d:PK_LBX + 2 * d + 2] = _fm(inp["lru_b_x"][l, d], 2)
            pk[l, :, PK_LLAM + 2 * d:PK_LLAM + 2 * d + 2] = _fm(inp["lru_lambda"][l, d], 2)
        fw = np.asarray(inp["ffn_conv_w"][l], np.float32)
        for fc in range(44):
            pk[l, :, PK_FCW + fc * 3:PK_FCW + fc * 3 + 3] = fw[:, fc * 128:(fc + 1) * 128].T
        pk[l, :, PK_FCB:PK_FCB + 44] = _fm(inp["ffn_conv_b"][l], 44)
        pk[l, :, PK_RDEC:PK_RDEC + 12] = _rep(np.asarray(inp["ret_decay"][l]).reshape(-1))
        pk[l, :, PK_DTB:PK_DTB + 12] = _rep(np.asarray(inp["ssd_dt_bias"][l]).reshape(-1))
        pk[l, :, PK_ALOG:PK_ALOG + 12] = _rep(np.asarray(inp["ssd_a_log"][l]).reshape(-1))
        pk[l, :, PK_SSDD:PK_SSDD + 384] = _rep(np.repeat(np.asarray(inp["ssd_d"][l], np.float32), 64))
        for d in range(2):
            for gi, nm in enumerate(("lru_w_a", "lru_w_x")):
                wblk = np.asarray(inp[nm][l, d], np.float32)
                for ch in range(2):
                    m = lruw[l, d * 4 + gi * 2 + ch]
                    for b2 in range(2):
                        m[b2 * 64:(b2 + 1) * 64, b2 * 64:(b2 + 1) * 64] = wblk[ch * 2 + b2]
    shared["pk"] = pk
    shared["lruw"] = lruw
    shared["pkf"] = np.ascontiguousarray(_fm(inp["final_norm_g"], 8))
    cos_t, sin_t = make_rope_tables()
    xp, xs = np.asarray(inp["x_prompt"], np.float32), np.asarray(inp["x_sample"], np.float32)
    c, c_ctx = np.asarray(inp["c"], np.float32), np.asarray(inp["c_ctx"], np.float32)
    per_core = []
    for core in range(8):
        segs = core_segments(core)
        x = np.zeros((T, D), np.float32)
        cT = np.zeros((D, NSEG), np.float32)
        rope = np.zeros((4, T, 64), np.float32)
        rope[0] = 0.125
        rope[2] = 1.0
        flags = np.zeros((128, 32), np.float32)
        for si, (kind, b, part) in enumerate(segs):
            sl = slice(si * SEGT, (si + 1) * SEGT)
            if kind == "s":
                x[sl] = xs[b, part * SEGT:(part + 1) * SEGT]
                cT[:, si] = c[b]
                pos = slice(part * SEGT, (part + 1) * SEGT)
                rope[0, sl], rope[1, sl] = cos_t[pos] * np.float32(0.125), sin_t[pos] * np.float32(0.125)
                rope[2, sl], rope[3, sl] = cos_t[pos], sin_t[pos]
                if part > 0:
                    flags[:, si] = 1.0
                if part == 0:
                    flags[:, 8 + si] = 1.0
                if part == 3:
                    flags[:, 16 + si] = 1.0
            else:
                x[sl] = xp[b]
                cT[:, si] = c_ctx
        s0_ret = np.zeros((L, 2, 128, 192), np.float32)
        s0_ssd = np.zeros((L, 2, 128, 384), np.float32)
        s0_lru = np.zeros((L, 128, 4), np.float32)
        if core < 4:
            sr = np.asarray(inp["state_ret"][core], np.float32)
            ss = np.asarray(inp["state_ssd"][core], np.float32)
            slr = np.asarray(inp["state_lru"][core], np.float32)
            s0_ret[:] = sr.reshape(L, 2, 3, 2, 64, 64).transpose(0, 1, 3, 4, 2, 5).reshape(L, 2, 128, 192)
            s0_ssd[:] = ss.transpose(0, 1, 3, 2, 4).reshape(L, 2, 128, 384)
            s0_lru[:] = slr.reshape(L, 2, 2, 128).transpose(0, 3, 1, 2).reshape(L, 128, 4)
        m = dict(shared)
        m.update({"xT": np.ascontiguousarray(x.T), "cT": cT, "flags": flags, "s0_ret": s0_ret,
                  "s0_ssd": s0_ssd, "s0_lru": s0_lru, "rope": rope})
        per_core.append(m)
    return per_core


_PROG = {}


def kernel(**inputs):
    per_core = prep_inputs(inputs)
    if "nc" not in _PROG:
        _PROG["nc"] = build_program()
    res = run_bass_kernel_spmd(_PROG["nc"], per_core, core_ids=list(range(8)))
    B, SQ = 32, 256
    y_prompt = np.zeros((B, SQ, D), np.float32)
    y_sample = np.zeros((4, 1024, D), np.float32)
    n_ret = np.zeros((B, L, 2, 6, 64, 64), np.float32)
    n_ssd = np.zeros((B, L, 2, 6, 128, 64), np.float32)
    n_lru = np.zeros((B, L, 2, 256), np.float32)
    for core in range(8):
        r = res.results[core]
        y = np.asarray(r["yT"]).T
        o_ret, o_ssd, o_lru = np.asarray(r["o_ret"]), np.asarray(r["o_ssd"]), np.asarray(r["o_lru"])
        for si, (kind, b, part) in enumerate(core_segments(core)):
            sl = slice(si * SEGT, (si + 1) * SEGT)
            if kind == "s":
                y_sample[b, part * SEGT:(part + 1) * SEGT] = y[sl]
            else:
                y_prompt[b] = y[sl]
                n_ret[b] = o_ret[si].reshape(L, 2, 2, 64, 3, 64).transpose(0, 1, 4, 2, 3, 5).reshape(L, 2, 6, 64, 64)
                n_ssd[b] = o_ssd[si].reshape(L, 2, 128, 6, 64).transpose(0, 1, 3, 2, 4)
                n_lru[b] = o_lru[si].reshape(L, 2, 256)
    return (y_prompt, y_sample, n_ret, n_ssd, n_lru)
```
